# Optimizing a Trainium2 kernel written in Bass

```python
import math
import jax, jax.numpy as jnp
from jax import lax
import numpy as np

D_MODEL = 1024
BATCH = 8
SEQ = 2048
DEPTH = 4


f32 = jnp.float32

GRID_W = 64
CTX_LEN = 256
D_MIX = D_MODEL
POOL_WIDTH = D_MIX // 4
POOL_WINDOWS = (2, 4, 8, 16)
POOL_GROUP = POOL_WIDTH // len(POOL_WINDOWS)
SSM_WIDTH = D_MIX // 4
SSM_GROUP_CH = 16
SSM_GROUPS = SSM_WIDTH // SSM_GROUP_CH
SSM_STATE = 64
ATTN_WIDTH = D_MIX - POOL_WIDTH - SSM_WIDTH
HEAD_DIM = 64
N_HEADS = ATTN_WIDTH // HEAD_DIM
N_KV_HEADS = 2
GROUP = N_HEADS // N_KV_HEADS
KV_WIDTH = N_KV_HEADS * HEAD_DIM
WINDOW = 128
BLOCK = 128
ROPE_BASE = 10000.0
ROPE_FREQS = HEAD_DIM // 4
IN_SPLITS = (POOL_WIDTH, POOL_WIDTH + SSM_WIDTH, POOL_WIDTH + SSM_WIDTH + ATTN_WIDTH,
             POOL_WIDTH + SSM_WIDTH + ATTN_WIDTH + KV_WIDTH)
IN_WIDTH = POOL_WIDTH + SSM_WIDTH + ATTN_WIDTH + 2 * KV_WIDTH
D_FF = -(-8 * D_MODEL // (3 * 256)) * 256
N_MOD = 6
EPS = 1e-6

kernel_name = "hybrid_pool_s5_swa_prefix_dit"


def _rms(x, g):
    x32 = x.astype(f32)
    y = x32 * lax.rsqrt(jnp.mean(x32 * x32, axis=-1, keepdims=True) + EPS)
    return y * g.astype(f32)


def _rmsnorm(x, g):
    return _rms(x, g).astype(x.dtype)


def _axial_rope_tables(L):
    rows = L // GRID_W
    pos = jnp.arange(L)
    row = jnp.repeat(jnp.arange(rows), GRID_W, total_repeat_length=L).astype(f32)
    col = (pos % GRID_W).astype(f32)
    inv = jnp.power(ROPE_BASE, -jnp.arange(ROPE_FREQS, dtype=f32) / ROPE_FREQS)
    ang = jnp.stack([row[:, None] * inv, col[:, None] * inv], axis=1)
    return jnp.cos(ang), jnp.sin(ang)


def _apply_axial_rope(x, cos, sin):
    xs = x.reshape(x.shape[:-1] + (2, 2, ROPE_FREQS))
    x1, x2 = xs[..., 0, :], xs[..., 1, :]
    c = cos[None, :, None]
    s = sin[None, :, None]
    return jnp.stack([x1 * c - x2 * s, x2 * c + x1 * s], axis=-2).reshape(x.shape)


def _pool_mixer(u, pool_w, pool_scale):
    B, L, _ = u.shape
    u32 = u.astype(f32)
    cs = jnp.pad(jnp.cumsum(u32, axis=1), ((0, 0), (1, 0), (0, 0)))
    us = u32.reshape(B, L, len(POOL_WINDOWS), POOL_GROUP)
    css = cs.reshape(B, L + 1, len(POOL_WINDOWS), POOL_GROUP)
    t = jnp.arange(L)
    diffs = []
    for gi, w in enumerate(POOL_WINDOWS):
        lo = jnp.clip(t - w // 2, 0, L)
        hi = jnp.clip(t + w // 2, 0, L)
        csg = css[:, :, gi]
        mean = (csg[:, hi] - csg[:, lo]) / (hi - lo).astype(f32)[None, :, None]
        diffs.append(mean - us[:, :, gi])
    d = jnp.stack(diffs, axis=2)
    y = jnp.einsum('blgc,gcd->blgd', d, pool_w.astype(f32)).reshape(B, L, POOL_WIDTH)
    return (y * pool_scale.astype(f32)).astype(u.dtype)


def _scan_op(left, right):
    a_l, b_l = left
    a_r, b_r = right
    return a_l * a_r, a_r * b_l + b_r


def _linear_scan(a_bar, bu, s0):
    if s0 is not None:
        bu = bu.at[:, 0].add(a_bar * s0)
    a_full = jnp.broadcast_to(a_bar, bu.shape)
    _, s = lax.associative_scan(_scan_op, (a_full, bu), axis=1)
    return s


def _ssm_discretize(a_re, a_im, log_dt, b_re, b_im):
    lam = lax.complex(jnp.minimum(a_re.astype(f32), -1e-4), a_im.astype(f32))
    dt = jnp.exp(log_dt.astype(f32))[:, None]
    a_bar = jnp.exp(lam * dt)
    b = lax.complex(b_re.astype(f32), b_im.astype(f32))
    b_bar = ((a_bar - 1.0) / lam)[..., None] * b
    return a_bar, b_bar


def _ssm_mixer(u_x, u_c, a_re, a_im, log_dt, b_re, b_im, c_re, c_im, d_skip, glu_w, glu_b, need_ctx):
    B, L, _ = u_x.shape
    Lc = u_c.shape[1]
    ux = u_x.astype(f32).reshape(B, L, SSM_GROUPS, SSM_GROUP_CH)
    uc = u_c.astype(f32).reshape(B, Lc, SSM_GROUPS, SSM_GROUP_CH)
    dg = d_skip.astype(f32).reshape(SSM_GROUPS, SSM_GROUP_CH)
    y_x = dg * ux
    y_c = dg * uc if need_ctx else None
    for direction in (0, 1):
        a_bar, b_bar = _ssm_discretize(a_re[direction], a_im[direction], log_dt[direction],
                                       b_re[direction], b_im[direction])
        cmat = lax.complex(c_re[direction].astype(f32), c_im[direction].astype(f32))
        bu_c = jnp.einsum('gpc,blgc->blgp', b_bar, uc)
        bu_x = jnp.einsum('gpc,blgc->blgp', b_bar, ux)
        if direction == 1:
            bu_c = jnp.flip(bu_c, axis=1)
            bu_x = jnp.flip(bu_x, axis=1)
        s_c = _linear_scan(a_bar, bu_c, None)
        s_x = _linear_scan(a_bar, bu_x, s_c[:, -1])
        if direction == 1:
            s_c = jnp.flip(s_c, axis=1)
            s_x = jnp.flip(s_x, axis=1)
        y_x = y_x + jnp.real(jnp.einsum('gcp,blgp->blgc', cmat, s_x))
        if need_ctx:
            y_c = y_c + jnp.real(jnp.einsum('gcp,blgp->blgc', cmat, s_c))

    def glu(y):
        y = jax.nn.gelu(y.reshape(y.shape[0], y.shape[1], SSM_WIDTH))
        return y * jax.nn.sigmoid(y @ glu_w.astype(f32) + glu_b.astype(f32))

    out_x = glu(y_x).astype(u_x.dtype)
    out_c = glu(y_c).astype(u_c.dtype) if need_ctx else None
    return out_x, out_c


def _attn_mixer(q_x, k_x, v_x, q_c, k_c, v_c, q_norm, k_norm, sink, cos, sin, need_ctx):
    B, L, _ = q_x.shape
    Lc = k_c.shape[1]
    nb = L // BLOCK
    scale = HEAD_DIM ** -0.5
    q = _apply_axial_rope(_rms(q_x.reshape(B, L, N_HEADS, HEAD_DIM), q_norm), cos, sin)
    k = _apply_axial_rope(_rms(k_x.reshape(B, L, N_KV_HEADS, HEAD_DIM), k_norm), cos, sin)
    v = v_x.reshape(B, L, N_KV_HEADS, HEAD_DIM)
    kc = _rms(k_c.reshape(B, Lc, N_KV_HEADS, HEAD_DIM), k_norm)
    vc = v_c.reshape(B, Lc, N_KV_HEADS, HEAD_DIM)
    sink_g = sink.astype(f32).reshape(N_KV_HEADS, GROUP)

    qb = q.reshape(B, nb, BLOCK, N_KV_HEADS, GROUP, HEAD_DIM)

    def band(t):
        tp = jnp.pad(t, ((0, 0), (BLOCK, BLOCK), (0, 0), (0, 0)))
        tp = tp.reshape(B, nb + 2, BLOCK, N_KV_HEADS, HEAD_DIM)
        return jnp.concatenate([tp[:, :-2], tp[:, 1:-1], tp[:, 2:]], axis=2)

    kb, vb = band(k), band(v)
    s_win = jnp.einsum('bnqhgd,bnkhd->bnhgqk', qb, kb) * scale
    blk = jnp.arange(nb)[:, None, None]
    q_pos = blk * BLOCK + jnp.arange(BLOCK)[None, :, None]
    k_pos = (blk - 1) * BLOCK + jnp.arange(3 * BLOCK)[None, None, :]
    valid = (jnp.abs(k_pos - q_pos) <= WINDOW) & (k_pos >= 0) & (k_pos < L)
    s_win = jnp.where(valid[None, :, None, None], s_win, -jnp.inf)
    s_ctx = jnp.einsum('bnqhgd,bchd->bnhgqc', qb, kc) * scale
    s_sink = jnp.broadcast_to(sink_g[None, None, :, :, None, None], s_win.shape[:-1] + (1,))
    p = jax.nn.softmax(jnp.concatenate([s_win, s_ctx, s_sink], axis=-1), axis=-1)
    o = (jnp.einsum('bnhgqk,bnkhd->bnqhgd', p[..., :3 * BLOCK], vb.astype(f32))
         + jnp.einsum('bnhgqc,bchd->bnqhgd', p[..., 3 * BLOCK:3 * BLOCK + Lc], vc.astype(f32)))
    out_x = o.reshape(B, L, ATTN_WIDTH).astype(q_x.dtype)

    out_c = None
    if need_ctx:
        qc = _rms(q_c.reshape(B, Lc, N_HEADS, HEAD_DIM), q_norm).reshape(B, Lc, N_KV_HEADS, GROUP, HEAD_DIM)
        sc = jnp.einsum('bqhgd,bkhd->bhgqk', qc, kc) * scale
        sc_sink = jnp.broadcast_to(sink_g[None, :, :, None, None], sc.shape[:-1] + (1,))
        pc = jax.nn.softmax(jnp.concatenate([sc, sc_sink], axis=-1), axis=-1)
        oc = jnp.einsum('bhgqk,bkhd->bqhgd', pc[..., :Lc], vc.astype(f32))
        out_c = oc.reshape(B, Lc, ATTN_WIDTH).astype(q_c.dtype)
    return out_x, out_c


def _mixing(hx, hc, w_in, w_out, pool_w, pool_scale, a_re, a_im, log_dt, b_re, b_im, c_re, c_im,
            d_skip, glu_w, glu_b, q_norm, k_norm, sink, cos, sin, need_ctx):
    pool_x, ssm_x, q_x, k_x, v_x = jnp.split(hx @ w_in, IN_SPLITS, axis=-1)
    pool_c, ssm_c, q_c, k_c, v_c = jnp.split(hc @ w_in, IN_SPLITS, axis=-1)
    po_x = _pool_mixer(pool_x, pool_w, pool_scale)
    so_x, so_c = _ssm_mixer(ssm_x, ssm_c, a_re, a_im, log_dt, b_re, b_im, c_re, c_im,
                            d_skip, glu_w, glu_b, need_ctx)
    ao_x, ao_c = _attn_mixer(q_x, k_x, v_x, q_c, k_c, v_c, q_norm, k_norm, sink, cos, sin, need_ctx)
    out_x = jnp.concatenate([po_x, so_x, ao_x], axis=-1) @ w_out
    out_c = None
    if need_ctx:
        po_c = _pool_mixer(pool_c, pool_w, pool_scale)
        out_c = jnp.concatenate([po_c, so_c, ao_c], axis=-1) @ w_out
    return out_x, out_c


def _swiglu(h, w_gate, w_up, w_down):
    return (jax.nn.silu(h @ w_gate) * (h @ w_up)) @ w_down


def setup_inputs(seed: int = 0) -> dict:
    key = jax.random.key(seed)
    ks = jax.random.split(key, 32)
    nrm = jax.random.normal
    D = D_MODEL
    a_im0 = math.pi * jnp.arange(SSM_STATE, dtype=f32)
    return {
        "x": nrm(ks[0], (BATCH, SEQ, D), f32),
        "c": nrm(ks[1], (BATCH, D), f32),
        "ctx": nrm(ks[2], (BATCH, CTX_LEN, D), f32),
        "c_ctx": nrm(ks[3], (D,), f32),
        "w_mod": nrm(ks[4], (DEPTH, D, N_MOD * D), f32) * (0.5 * D ** -0.5),
        "b_mod": nrm(ks[5], (DEPTH, N_MOD * D), f32) * 0.02,
        "norm_mix": 1.0 + 0.02 * nrm(ks[6], (DEPTH, D), f32),
        "norm_ffn": 1.0 + 0.02 * nrm(ks[7], (DEPTH, D), f32),
        "w_in": nrm(ks[8], (DEPTH, D, IN_WIDTH), f32) * D ** -0.5,
        "w_out": nrm(ks[9], (DEPTH, D_MIX, D), f32) * D_MIX ** -0.5,
        "pool_w": nrm(ks[10], (DEPTH, len(POOL_WINDOWS), POOL_GROUP, POOL_GROUP), f32) * POOL_GROUP ** -0.5,
        "pool_scale": 1.0 + 0.1 * nrm(ks[11], (DEPTH, POOL_WIDTH), f32),
        "ssm_a_re": -0.5 + 0.01 * nrm(ks[12], (DEPTH, 2, SSM_GROUPS, SSM_STATE), f32),
        "ssm_a_im": a_im0 + 0.01 * nrm(ks[13], (DEPTH, 2, SSM_GROUPS, SSM_STATE), f32),
        "ssm_log_dt": jax.random.uniform(ks[14], (DEPTH, 2, SSM_GROUPS), f32,
                                         math.log(1e-3), math.log(1e-1)),
        "ssm_b_re": nrm(ks[15], (DEPTH, 2, SSM_GROUPS, SSM_STATE, SSM_GROUP_CH), f32) * (0.5 / SSM_GROUP_CH) ** 0.5,
        "ssm_b_im": nrm(ks[16], (DEPTH, 2, SSM_GROUPS, SSM_STATE, SSM_GROUP_CH), f32) * (0.5 / SSM_GROUP_CH) ** 0.5,
        "ssm_c_re": nrm(ks[17], (DEPTH, 2, SSM_GROUPS, SSM_GROUP_CH, SSM_STATE), f32) * (0.5 / SSM_STATE) ** 0.5,
        "ssm_c_im": nrm(ks[18], (DEPTH, 2, SSM_GROUPS, SSM_GROUP_CH, SSM_STATE), f32) * (0.5 / SSM_STATE) ** 0.5,
        "ssm_d": nrm(ks[19], (DEPTH, SSM_WIDTH), f32),
        "ssm_glu_w": nrm(ks[20], (DEPTH, SSM_WIDTH, SSM_WIDTH), f32) * SSM_WIDTH ** -0.5,
        "ssm_glu_b": 0.02 * nrm(ks[21], (DEPTH, SSM_WIDTH), f32),
        "q_norm": 1.0 + 0.02 * nrm(ks[22], (DEPTH, HEAD_DIM), f32),
        "k_norm": 1.0 + 0.02 * nrm(ks[23], (DEPTH, HEAD_DIM), f32),
        "attn_sink": 0.5 * nrm(ks[24], (DEPTH, N_HEADS), f32),
        "ffn_w_gate": nrm(ks[25], (DEPTH, D, D_FF), f32) * D ** -0.5,
        "ffn_w_up": nrm(ks[26], (DEPTH, D, D_FF), f32) * D ** -0.5,
        "ffn_w_down": nrm(ks[27], (DEPTH, D_FF, D), f32) * D_FF ** -0.5,
    }


def reference(x, c, ctx, c_ctx, w_mod, b_mod, norm_mix, norm_ffn, w_in, w_out, pool_w, pool_scale,
              ssm_a_re, ssm_a_im, ssm_log_dt, ssm_b_re, ssm_b_im, ssm_c_re, ssm_c_im, ssm_d,
              ssm_glu_w, ssm_glu_b, q_norm, k_norm, attn_sink, ffn_w_gate, ffn_w_up, ffn_w_down):
    L = x.shape[1]
    cos, sin = _axial_rope_tables(L)
    silu_c = jax.nn.silu(c)
    silu_cc = jax.nn.silu(c_ctx)
    for l in range(DEPTH):
        need_ctx = l < DEPTH - 1
        mx = silu_c @ w_mod[l] + b_mod[l]
        mc = silu_cc @ w_mod[l] + b_mod[l]
        sh1, sc1, g1, sh2, sc2, g2 = jnp.split(mx[:, None, :], N_MOD, axis=-1)
        csh1, csc1, cg1, csh2, csc2, cg2 = jnp.split(mc, N_MOD, axis=-1)

        hx = _rmsnorm(x, norm_mix[l]) * (1.0 + sc1) + sh1
        hc = _rmsnorm(ctx, norm_mix[l]) * (1.0 + csc1) + csh1
        mix_x, mix_c = _mixing(hx, hc, w_in[l], w_out[l], pool_w[l], pool_scale[l],
                               ssm_a_re[l], ssm_a_im[l], ssm_log_dt[l], ssm_b_re[l], ssm_b_im[l],
                               ssm_c_re[l], ssm_c_im[l], ssm_d[l], ssm_glu_w[l], ssm_glu_b[l],
                               q_norm[l], k_norm[l], attn_sink[l], cos, sin, need_ctx)
        x = x + g1 * mix_x
        hx2 = _rmsnorm(x, norm_ffn[l]) * (1.0 + sc2) + sh2
        x = x + g2 * _swiglu(hx2, ffn_w_gate[l], ffn_w_up[l], ffn_w_down[l])

        if need_ctx:
            ctx = ctx + cg1 * mix_c
            hc2 = _rmsnorm(ctx, norm_ffn[l]) * (1.0 + csc2) + csh2
            ctx = ctx + cg2 * _swiglu(hc2, ffn_w_gate[l], ffn_w_up[l], ffn_w_down[l])
    return x
```

```python
import math
import numpy as np
SK = ()
import concourse.bass as bass
import concourse.mybir as mybir
from concourse.bass_utils import run_bass_kernel_spmd

F32 = mybir.dt.float32
BF = mybir.dt.bfloat16
AF = mybir.ActivationFunctionType
ALU = mybir.AluOpType

ENGS = ("pe", "act", "dve", "pool", "sp")


class Foot:
    __slots__ = ("name", "p0", "p1", "iv")

    def __init__(self, ap):
        self.name = ap.tensor.name
        es = mybir.dt.size(ap.dtype)
        pairs = [tuple(x) for x in ap.ap]
        off = ap.offset
        if str(ap.space) == "DRAM":
            self.p0, self.p1 = 0, 1
            dims = pairs
            base = off
        else:
            pstep, pcnt = pairs[0]
            if pstep == 0:
                self.p0 = 0
                base = off
            else:
                self.p0 = off // pstep
                base = off - self.p0 * pstep
            self.p1 = self.p0 + pcnt
            dims = pairs[1:]
        ivs = [(base, base + 1)]
        for step, cnt in reversed(dims):
            if cnt == 1 or step == 0:
                continue
            if len(ivs) == 1 and abs(step) == ivs[0][1] - ivs[0][0]:
                lo, hi = ivs[0]
                if step > 0:
                    ivs = [(lo, lo + step * cnt)]
                else:
                    ivs = [(lo + step * (cnt - 1), hi)]
                continue
            if len(ivs) * cnt > 48:
                lo = min(a for a, _ in ivs)
                hi = max(b for _, b in ivs)
                ext = step * (cnt - 1)
                ivs = [(lo + min(0, ext), hi + max(0, ext))]
                continue
            new = []
            for k in range(cnt):
                for a, b in ivs:
                    new.append((a + k * step, b + k * step))
            new.sort()
            ivs = new
        ivs.sort()
        self.iv = tuple((a * es, b * es) for a, b in ivs)

    def overlaps(self, o):
        if self.p1 <= o.p0 or o.p1 <= self.p0:
            return False
        x, y = self.iv, o.iv
        if len(x) == 1 and len(y) == 1:
            return x[0][0] < y[0][1] and y[0][0] < x[0][1]
        i = j = 0
        while i < len(x) and j < len(y):
            a, b = x[i]
            c, d = y[j]
            if a < d and c < b:
                return True
            if b <= d:
                i += 1
            else:
                j += 1
        return False

    def covers(self, o):
        if not (self.p0 <= o.p0 and o.p1 <= self.p1):
            return False
        if len(self.iv) != 1:
            return False
        a, b = self.iv[0]
        return all(a <= c and d <= b for c, d in o.iv)


class Op:
    __slots__ = ("eng", "fn", "deps", "dma", "sem", "semval", "incd", "count", "seq")
    SEQ = 0

    def __init__(self, eng, fn, dma):
        self.eng, self.fn, self.dma = eng, fn, dma
        Op.SEQ += 1
        self.seq = Op.SEQ
        self.deps = []
        self.sem = None
        self.semval = 0
        self.incd = False
        self.count = 0


class Sched:
    def __init__(self, nc, n_dma_sems=56):
        self.nc = nc
        self.ops = {e: [] for e in ENGS}
        self.recs = {}
        self.n_dma_sems = n_dma_sems

    def op(self, eng, fn, reads=(), writes=(), dma=False):
        o = Op(eng, fn, dma)
        self.ops[eng].append(o)
        rf = [Foot(a) for a in reads if a is not None and not isinstance(a, (int, float))]
        wf = [Foot(a) for a in writes if a is not None]
        deps = {}
        psum_names = set()
        for f in rf:
            if f.name.startswith("psb"):
                psum_names.add(f.name)
                continue
            for rec in self.recs.get(f.name, ()):
                if rec[1] == "w" and rec[0].overlaps(f):
                    deps[rec[2]] = True
        for f in wf:
            if f.name.startswith("psb"):
                psum_names.add(f.name)
                continue
            for rec in self.recs.get(f.name, ()):
                if rec[0].overlaps(f):
                    deps.setdefault(rec[2], False)
        for nm in psum_names:
            for rec in self.recs.get(nm, ()):
                if rec[2].eng != eng:
                    deps[rec[2]] = True
        for d, israw in deps.items():
            if d is o:
                continue
            if d.eng == o.eng and not d.dma and not o.dma:
                if o.eng == "pe":
                    continue
            o.deps.append(d)
        for nm in psum_names:
            self.recs[nm] = [(None, "w", o)]
        for f in wf:
            if f.name in psum_names:
                continue
            lst = self.recs.setdefault(f.name, [])
            lst[:] = [r for r in lst if not f.covers(r[0])]
            lst.append((f, "w", o))
        for f in rf:
            if f.name in psum_names:
                continue
            self.recs.setdefault(f.name, []).append((f, "r", o))
        return o

    def emit(self):
        nc = self.nc
        from contextlib import ExitStack
        for e in ENGS:
            for o in self.ops[e]:
                for d in o.deps:
                    d.incd = True
        for e in ENGS:
            c = 0
            for o in self.ops[e]:
                if not o.dma and o.incd:
                    c += 1
                    o.count = c
        with ExitStack() as st:
            csem = {e: st.enter_context(nc.semaphore("c_" + e)) for e in ENGS}
            dsems = [st.enter_context(nc.semaphore("d%d" % i)) for i in range(self.n_dma_sems)]
            alldma = [o for e in ENGS for o in self.ops[e] if o.dma]
            alldma.sort(key=lambda o: o.seq)
            half = self.n_dma_sems // 2
            use = [0] * self.n_dma_sems
            prev = [None] * self.n_dma_sems
            cnt = {"hw": 0, "sw": 0}
            for o in alldma:
                kind = "sw" if o.eng == "pool" else "hw"
                s = (cnt[kind] % half) + (half if kind == "sw" else 0)
                cnt[kind] += 1
                use[s] += 1
                o.sem = s
                o.semval = 16 * use[s]
                if prev[s] is not None:
                    o.deps.append(prev[s])
                prev[s] = o
            blk = st.enter_context(nc.Block())

            def run(ename, eng):
                waited_c = {e: 0 for e in ENGS}
                waited_d = {}
                for o in self.ops[ename]:
                    for d in o.deps:
                        if d.dma:
                            if waited_d.get(d.sem, 0) < d.semval:
                                eng.wait_ge(dsems[d.sem], d.semval)
                                waited_d[d.sem] = d.semval
                        elif waited_c[d.eng] < d.count:
                            eng.wait_ge(csem[d.eng], d.count)
                            waited_c[d.eng] = d.count
                    ins = o.fn(eng)
                    if o.dma:
                        ins.then_inc(dsems[o.sem], 16)
                    elif o.incd:
                        ins.then_inc(csem[ename], 1)
                if ename == "sp":
                    last = {}
                    for o in alldma:
                        last[o.sem] = max(last.get(o.sem, 0), o.semval)
                    for s, v in last.items():
                        eng.wait_ge(dsems[s], v)

            blk.tensor(lambda e: run("pe", e))
            blk.scalar(lambda e: run("act", e))
            blk.vector(lambda e: run("dve", e))
            blk.gpsimd(lambda e: run("pool", e))
            blk.sync(lambda e: run("sp", e))


D = 1024
NB = 8
SEQ = 2048
LC = 256
T = SEQ + LC
NL = 4
DFF = 2816
NF = DFF // 128
EPS = 1e-6
CH = [(0, 256, 1), (256, 512, 0), (768, 512, 0), (1280, 512, 0), (1792, 512, 0)]
NTT = T // 128
C_POOL, C_V, C_SSM, C_Q, C_K, NCOL = 0, 256, 384, 640, 1152, 1408
TWO_PI_HI = 6.28125
TWO_PI_LO = 2.0 * math.pi - 6.28125
NEG = -30000.0


def _pi_perm(d):
    r = d % 32
    return d + 16 if r < 16 else d - 16


def build_program(n_layers=NL, dbg=False, stop=None):
    nc = bass.Bass("TRN2", target_bir_lowering=False)
    S = Sched(nc)

    def din(name, shape):
        return nc.dram_tensor(name, list(shape), F32, kind="ExternalInput").ap()

    xT = din("xT", [D, T])
    cvec = din("cvec", [128, 8, 2])
    w_mod = din("w_mod", [NL, D, 6 * D])
    b_modp = din("b_modp", [NL, 128, 48])
    normp = din("normp", [NL, 128, 2, 8])
    w_inp = din("w_inp", [NL, D, NCOL])
    w_out = din("w_out", [NL, D, D])
    poolwbd = din("poolwbd", [NL, 128, 2, 128])
    poolsc = din("poolsc", [NL, 128, 2])
    band = din("band", [128, 5, 2, 256])
    ssm_small = din("ssm_small", [NL, 128, 3, 16])
    ssm_B = din("ssm_B", [NL, 16, 128, 2, 128])
    ssm_C = din("ssm_C", [NL, 16, 128, 2, 128])
    ssm_dg = din("ssm_dg", [NL, 128, 2, 2])
    glu_w = din("glu_w", [NL, 256, 256])
    qkn = din("qkn", [NL, 128, 4])
    sinkr = din("sinkr", [NL, 128, 8])
    w_gate = din("w_gate", [NL, D, DFF])
    w_up = din("w_up", [NL, D, DFF])
    w_down = din("w_down", [NL, DFF, D])
    ident = din("ident", [128, 128])
    pswap = din("pswap", [128, 128])
    blk64 = din("blk64", [128, 128])
    rope_c = din("rope_c", [128, SEQ])
    rope_s = din("rope_s", [128, SEQ])
    maskd = din("maskd", [128, 2, 256])

    outT = nc.dram_tensor("outT", [D, SEQ], F32, kind="ExternalOutput").ap()
    Xd = nc.dram_tensor("xscratch", [D, T], F32).ap()
    dbg_out = {}

    AW = 53200
    a32 = nc.alloc_sbuf_tensor("arena", [128, AW], F32)
    a16 = a32.bitcast(BF)
    PSB = [nc.alloc_psum_tensor("psb%d" % i, [128, 512], F32) for i in range(8)]
    PSB16 = [p.bitcast(BF) for p in PSB]

    def _shape(v, shape):
        if len(shape) == 1:
            return v
        if len(shape) == 2:
            return v.rearrange("p (a b) -> p a b", a=shape[0], b=shape[1])
        if len(shape) == 3:
            return v.rearrange("p (a b c) -> p a b c", a=shape[0], b=shape[1], c=shape[2])
        return v.rearrange("p (a b c d) -> p a b c d", a=shape[0], b=shape[1], c=shape[2], d=shape[3])

    class Bump:
        def __init__(self, lo, hi):
            self.lo, self.hi, self.cur = lo, hi, lo

        def reset(self):
            self.cur = self.lo

        def take(self, nbytes):
            nbytes = (nbytes + 31) // 32 * 32
            o = self.cur
            self.cur += nbytes
            assert self.cur <= self.hi, ("arena overflow", self.cur, self.hi)
            return o

        def f32(self, *shape):
            n = int(np.prod(shape))
            o = self.take(4 * n)
            return _shape(a32[:, o // 4:o // 4 + n], shape)

        def bf(self, *shape):
            n = int(np.prod(shape))
            o = self.take(2 * n)
            return _shape(a16[:, o // 2:o // 2 + n], shape)

    CONST_B = 14 * 1024
    PROJ_B = 52 * 1024
    PC = Bump(0, CONST_B)
    PP = Bump(CONST_B, CONST_B + PROJ_B)
    SC = Bump(CONST_B + PROJ_B, AW * 4)

    def isap(x):
        return x is not None and not isinstance(x, (int, float))

    def DMA(q, out, in_):
        S.op(q, lambda e: e.dma_start(out=out, in_=in_), reads=[in_], writes=[out], dma=True)

    def TT(eng, out, a, b, op):
        S.op(eng, lambda e: e.tensor_tensor(out=out, in0=a, in1=b, op=op), reads=[a, b], writes=[out])

    def TS(eng, out, a, s1, op0, s2=None, op1=None):
        if op1 is None:
            S.op(eng, lambda e: e.tensor_scalar(out=out, in0=a, scalar1=s1, scalar2=None, op0=op0),
                 reads=[a, s1], writes=[out])
        else:
            S.op(eng, lambda e: e.tensor_scalar(out=out, in0=a, scalar1=s1, scalar2=s2, op0=op0, op1=op1),
                 reads=[a, s1, s2], writes=[out])

    def STT(eng, out, a, sc, b, op0, op1):
        S.op(eng, lambda e: e.scalar_tensor_tensor(out=out, in0=a, scalar=sc, in1=b, op0=op0, op1=op1),
             reads=[a, sc, b], writes=[out])

    def ACT(out, in_, func, bias=None, scale=None):
        kw = {}
        if bias is not None:
            kw["bias"] = bias
        if scale is not None:
            kw["scale"] = scale
        S.op("act", lambda e: e.activation(out=out, in_=in_, func=func, **kw), reads=[in_, bias, scale], writes=[out])

    def COPY(eng, out, in_):
        if eng == "act":
            S.op("act", lambda e: e.activation(out=out, in_=in_, func=AF.Copy), reads=[in_], writes=[out])
        else:
            S.op(eng, lambda e: e.tensor_copy(out=out, in_=in_), reads=[in_], writes=[out])

    def MM(out, lhsT, rhs, start, stop, sgc=False):
        S.op("pe", lambda e: e.matmul(out, lhsT=lhsT, rhs=rhs, start=start, stop=stop, skip_group_check=sgc),
             reads=[lhsT, rhs] + ([] if start else [out]), writes=[out])

    def TR(out, in_, idn):
        S.op("pe", lambda e: e.transpose(out, in_, idn), reads=[in_, idn], writes=[out])

    def MEMSET(eng, out, val):
        S.op(eng, lambda e: e.memset(out, val), writes=[out])

    def RECIP(out, in_):
        S.op("dve", lambda e: e.reciprocal(out=out, in_=in_), reads=[in_], writes=[out])

    def dbg_dump(name, ap, shape):
        if not dbg:
            return
        n = int(np.prod(shape[1:]))
        o = nc.dram_tensor("dbg_" + name, [128, n], F32, kind="ExternalOutput").ap()
        dbg_out[name] = o
        letters = "abcd"[:len(shape) - 1]
        flat = ap if len(shape) == 2 else ap.rearrange("p %s -> p (%s)" % (" ".join(letters), " ".join(letters)))
        for i in range(0, n, 1024):
            j = min(n, i + 1024)
            DMA("pool", o[:, i:j], flat[:, i:j])

    IDF = PC.f32(128)
    IDB = PC.bf(128)
    PSWB = PC.bf(128)
    BLKB = PC.bf(128)
    ONEB = PC.bf(128)
    MASKB = PC.bf(2, 256)
    BANDB = PC.bf(5, 2, 256)
    MOD = PC.f32(NL, 48, 2)
    GE = PC.f32(NL, 2, 8, 2)
    EPSC = PC.f32(1)
    NPI = PC.f32(1)
    QKN = PC.f32(4)
    SINKE = PC.f32(8)
    POOLSC = PC.f32(2)
    SDG = PC.f32(2, 2)
    TMAT = PC.bf(7, 128)
    ZEROB = PC.bf(256)

    stg = SC.f32(5 * 2 * 256)
    DMA("sp", IDF, ident)
    COPY("dve", IDB, IDF)
    DMA("sp", stg[:, 0:128], pswap)
    COPY("dve", PSWB, stg[:, 0:128])
    DMA("sp", stg[:, 128:256], blk64)
    COPY("dve", BLKB, stg[:, 128:256])
    MEMSET("dve", ONEB, 1.0 / 1024.0)
    MEMSET("dve", EPSC, EPS)
    MEMSET("dve", NPI, -math.pi)
    DMA("pool", MASKB, maskd)
    for v_ in range(5):
        DMA("pool", BANDB[:, v_], band[:, v_])

    if stop == 'consts':
        n_layers = 0
    CS = SC.f32(8, 2)
    SILC = SC.f32(8, 2)
    DMA("sp", CS, cvec)
    ACT(SILC, CS, AF.Silu)
    WMb = [SC.bf(8, 512), SC.bf(8, 512), SC.bf(8, 512)]
    SILB = SC.bf(8, 2)
    COPY("dve", SILB, SILC)
    MROW = SC.f32(6 * D)
    BMOD = SC.f32(48)
    NRM = SC.f32(2, 8)
    wi = 0
    for l in range(n_layers):
        wv = w_mod[l].rearrange("(kt p) c -> p kt c", p=128)
        for cc in range(12):
            wb = WMb[wi % 3]
            wi += 1
            DMA("pool", wb, wv[:, :, cc * 512:(cc + 1) * 512])
            ps = PSB[cc % 2]
            for kt in range(8):
                MM(ps[0:2, :], SILB[:, kt, :], wb[:, kt, :], kt == 0, kt == 7)
            COPY("act", MROW[0:2, cc * 512:(cc + 1) * 512], ps[0:2, :])
        pst = PSB[2]
        for b in range(48):
            MM(pst[:, 2 * b:2 * b + 2], MROW[0:2, b * 128:(b + 1) * 128], IDF[0:2, 0:2], True, True)
        DMA("sp", BMOD, b_modp[l])
        TT("dve", MOD[:, l], pst[:, 0:96].rearrange("p (a b) -> p a b", a=48, b=2),
           BMOD.unsqueeze(2).broadcast_to([128, 48, 2]), ALU.add)
        DMA("sp", NRM, normp[l])
        for which in range(2):
            sc = MOD[:, l, (1 + 3 * which) * 8:(2 + 3 * which) * 8, :]
            STT("dve", GE[:, l, which], sc, 1.0, NRM[:, which].unsqueeze(2).broadcast_to([128, 8, 2]),
                ALU.add, ALU.mult)

    def modv(l, m, j, r):
        return MOD[:, l, m * 8 + j, r:r + 1]

    PSI = [0]

    def nextps():
        PSI[0] = (PSI[0] + 1) % 8
        return PSB[PSI[0]]

    def rmsnorm_chunk(XC, w, ge, shm, l, kind, HOUT, SQ, RS, HN):
        ACT(SQ[:, :, 0:w], XC[:, :, 0:w], AF.Square)
        ps = nextps()
        for j in range(8):
            MM(ps[:, 0:w], ONEB, SQ[:, j, 0:w], j == 0, j == 7)
        ACT(RS[:, 0:w], ps[:, 0:w], AF.Sqrt, bias=EPSC[:, 0:1])
        RECIP(RS[:, 0:w], RS[:, 0:w])
        for j in range(8):
            hn = HN[j % 2]
            STT("dve", hn[:, 0:w], XC[:, j, 0:w], ge[:, j, kind:kind + 1], RS[:, 0:w], ALU.mult, ALU.mult)
            ACT(HOUT[:, j, 0:w], hn[:, 0:w], AF.Identity, bias=modv(l, shm, j, kind))

    def layer_body(l):
        xsrc = xT if l == 0 else Xd
        xsv = xsrc.rearrange("(j p) t -> p j t", p=128)
        xdv = Xd.rearrange("(j p) t -> p j t", p=128)
        last = (l == n_layers - 1)

        PP.reset()
        SC.reset()
        UT = PP.bf(2, T)
        QT = PP.bf(4, T)
        KT = PP.bf(2, T)
        VA = PP.bf(NTT, 2, 66)
        POOLU = PP.bf(NTT, 256)

        WIN = SC.bf(8, NCOL)
        ROPEC = SC.f32(SEQ)
        ROPES = SC.f32(SEQ)
        XCb = [SC.f32(8, 512), SC.f32(8, 512)]
        SQ = SC.bf(8, 512)
        HCb = [SC.bf(8, 512), SC.bf(8, 512)]
        RS = SC.f32(512)
        HN = [SC.f32(512), SC.f32(512)]
        QRAWb = [SC.bf(512), SC.bf(512)]
        QSQb = [SC.bf(512), SC.bf(512)]
        RSQb = [SC.f32(512), SC.f32(512)]
        T1b = [SC.f32(512), SC.f32(512)]
        T2b = [SC.f32(512), SC.f32(512)]

        if "win" not in SK:
            DMA("pool", WIN, w_inp[l].rearrange("(kt p) c -> p kt c", p=128))
        DMA("sp", ROPEC, rope_c)
        DMA("sp", ROPES, rope_s)
        DMA("sp", QKN, qkn[l])
        if "vaones" not in SK:
            MEMSET("pool", VA[:, :, :, 64:66], 1.0)
        DMA("sp", XCb[0][:, :, 0:CH[0][1]], xsv[:, :, CH[0][0]:CH[0][0] + CH[0][1]])
        rmsnorm_chunk(XCb[0], CH[0][1], GE[:, l, 0], 0, l, CH[0][2], HCb[0], SQ, RS, HN)
        if len(CH) > 1:
            DMA("sp", XCb[1][:, :, 0:CH[1][1]], xsv[:, :, CH[1][0]:CH[1][0] + CH[1][1]])
        for ci, (c0, w, kind) in enumerate(CH):
            XC = XCb[ci % 2]
            HC = HCb[ci % 2]
            if ci + 1 < len(CH):
                n0, nw, nk = CH[ci + 1]
                rmsnorm_chunk(XCb[(ci + 1) % 2], nw, GE[:, l, 0], 0, l, nk, HCb[(ci + 1) % 2], SQ, RS, HN)
                if ci + 2 < len(CH):
                    m0, mw, _ = CH[ci + 2]
                    DMA("sp", XCb[ci % 2][:, :, 0:mw], xsv[:, :, m0:m0 + mw])
            for tl in range(0 if "tok" in SK else w // 128):
                tt = c0 // 128 + tl
                ps = nextps()
                for j in range(8):
                    MM(ps[:, 0:384], HC[:, j, tl * 128:(tl + 1) * 128], WIN[:, j, 0:384], j == 0, j == 7)
                if "tokpool" not in SK:
                    COPY("act", POOLU[:, tt, :], ps[:, 0:256])
                if "tokva" not in SK:
                    COPY("dve", VA[:, tt, :, 0:64], ps[:, 256:384].rearrange("p (a b) -> p a b", a=2, b=64))
            def post(m, ps):
                if m < 2:
                    COPY("act", UT[:, m, c0:c0 + w], ps[:, 0:w])
                    return
                isq = m < 6
                QRAW, QSQ, RSQ, T1, T2 = QRAWb[m % 2], QSQb[m % 2], RSQb[m % 2], T1b[m % 2], T2b[m % 2]
                dst = QT[:, m - 2, c0:c0 + w] if isq else KT[:, m - 6, c0:c0 + w]
                g = QKN[:, 0:1] if isq else QKN[:, 2:3]
                gp = QKN[:, 1:2] if isq else QKN[:, 3:4]
                ACT(QSQ[:, 0:w], ps[:, 0:w], AF.Square)
                if kind == 0:
                    COPY("act", QRAW[:, 0:w], ps[:, 0:w])
                ps2 = nextps()
                MM(ps2[:, 0:w], BLKB, QSQ[:, 0:w], True, True)
                if kind == 0:
                    ps3 = nextps()
                    MM(ps3[:, 0:w], PSWB, QRAW[:, 0:w], True, True)
                ACT(RSQ[:, 0:w], ps2[:, 0:w], AF.Sqrt, bias=EPSC[:, 0:1])
                RECIP(RSQ[:, 0:w], RSQ[:, 0:w])
                if kind == 1:
                    STT("dve", dst, ps[:, 0:w], g, RSQ[:, 0:w], ALU.mult, ALU.mult)
                else:
                    tp = c0 - LC
                    STT("dve", T1[:, 0:w], ps[:, 0:w], g, ROPEC[:, tp:tp + w], ALU.mult, ALU.mult)
                    STT("dve", T2[:, 0:w], ps3[:, 0:w], gp, ROPES[:, tp:tp + w], ALU.mult, ALU.mult)
                    TT("dve", T1[:, 0:w], T1[:, 0:w], T2[:, 0:w], ALU.add)
                    TT("dve", dst, T1[:, 0:w], RSQ[:, 0:w], ALU.mult)
            prev = None
            for m in range(8):
                ps = nextps()
                for j in range(8):
                    MM(ps[:, 0:w], WIN[:, j, C_SSM + 128 * m:C_SSM + 128 * (m + 1)], HC[:, j, 0:w], j == 0, j == 7)
                if prev is not None:
                    post(*prev)
                prev = (m, ps)
            post(*prev)
        if dbg and l == 0:
            dbg_dump("ut", UT, [128, 2, T])
            dbg_dump("qt", QT, [128, 4, T])
            dbg_dump("kt", KT, [128, 2, T])
            dbg_dump("va", VA, [128, NTT, 2, 66])
            dbg_dump("poolu", POOLU, [128, NTT, 256])
        if stop == "n1":
            return

        SC.reset()
        MIX = SC.bf(8, T)
        sc_mark = SC.cur
        DTb = SC.bf(2, T)
        PWBD = SC.bf(2, 128)
        DMA("pool", PWBD, poolwbd[l])
        DMA("sp", POOLSC, poolsc[l])
        for gp in range(2):
            for tp in range(NTT // 2):
                ps = nextps()
                psv = ps[:, 0:512].rearrange("p (a b) -> p a b", a=2, b=256)
                for ti in range(2):
                    tt = 2 * tp + ti
                    seg0, seg1 = (0, 1) if tt < 2 else (2, NTT - 1)
                    nbs = []
                    if tt > seg0:
                        nbs.append((tt - 1, 3))
                    nbs.append((tt, 0 if tt == seg0 else (2 if tt == seg1 else 1)))
                    if tt < seg1:
                        nbs.append((tt + 1, 4))
                    for i, (nb, var) in enumerate(nbs):
                        MM(psv[:, ti, :], POOLU[:, nb, gp * 128:(gp + 1) * 128], BANDB[:, var, gp, :],
                           i == 0, i == len(nbs) - 1)
                dv = DTb[:, gp, tp * 256:(tp + 1) * 256].rearrange("p (a b) -> p a b", a=2, b=128)
                COPY("act", dv[0:64], psv[0:64, :, 0:128])
                COPY("dve", dv[64:128], psv[64:128, :, 128:256])
        for gp in range(2):
            for (c0, w, kind) in CH:
                ps = nextps()
                MM(ps[:, 0:w], PWBD[:, gp, :], DTb[:, gp, c0:c0 + w], True, True)
                ACT(MIX[:, gp, c0:c0 + w], ps[:, 0:w], AF.Identity, scale=POOLSC[:, gp:gp + 1])

        if stop == "pool":
            dbg_dump("mix_ps", MIX[:, 0:4], [128, 4, T])
            return
        SC.cur = sc_mark
        SKR = SC.f32(8)
        DMA("sp", SKR, sinkr[l])
        ACT(SINKE, SKR, AF.Exp)
        PT = SC.bf(5, 2, 2, 128)
        ONb = [SC.bf(8, 64), SC.bf(8, 64)]
        DENb = SC.f32(4)
        RECb = SC.f32(4)
        ATT_I = [0]
        att_end = SC.cur

        def attn_slice(qt):
            if qt < 2:
                keys = [(0, None), (1, None)]
            else:
                keys = [(0, None), (1, None)]
                if qt > 2:
                    keys.append((qt - 1, 0))
                keys.append((qt, None))
                if qt < NTT - 1:
                    keys.append((qt + 1, 1))
            ON = ONb[qt % 2]
            for kvh in range(2):
                for ki, (kt, mk) in enumerate(keys):
                    for half in range(2):
                        ps = PSB[5 + half]
                        started = False
                        if mk is not None:
                            MM(ps[:, 0:256], IDB, MASKB[:, mk, :], True, False)
                            started = True
                        for hq in range(2):
                            m = 2 * kvh + hq
                            MM(ps[:, hq * 128:(hq + 1) * 128],
                               KT[64 * half:64 * half + 64, kvh, kt * 128:(kt + 1) * 128],
                               QT[64 * half:64 * half + 64, m, qt * 128:(qt + 1) * 128],
                               not started, hq == 1)
                            started = True
                        ACT(PT[:, ki, half].rearrange("p a b -> p (a b)"), ps[:, 0:256], AF.Exp, scale=0.125)
                po = PSB[7] if kvh == 0 else PSB[2]
                pov = po[:, 0:264].rearrange("p (a b) -> p a b", a=4, b=66)
                for hh in range(4):
                    hq, half = hh // 2, hh % 2
                    for ki, (kt, mk) in enumerate(keys):
                        MM(pov[:, hh, 0:65], PT[:, ki, half, hq, :], VA[:, kt, kvh, 0:65], ki == 0, ki == len(keys) - 1)
                TT("dve", DENb, pov[:, :, 64], SINKE[:, 4 * kvh:4 * kvh + 4], ALU.add)
                RECIP(RECb, DENb)
                TT("dve", ON[:, 4 * kvh:4 * kvh + 4, :], pov[:, :, 0:64], RECb.unsqueeze(2).broadcast_to([128, 4, 64]), ALU.mult)
            pt16 = PSB16[5][:, 512:1024]
            onf = ON.rearrange("p a b -> p (a b)")
            for m in range(4):
                TR(pt16[:, m * 128:(m + 1) * 128], onf[:, m * 128:(m + 1) * 128], IDB)
            COPY("act", MIX[:, 4:8, qt * 128:(qt + 1) * 128], pt16.rearrange("p (a b) -> p a b", a=4, b=128))

        SM = SC.f32(3, 16)
        DMA("sp", SM, ssm_small[l])
        DMA("sp", SDG, ssm_dg[l])
        GLUW = SC.bf(2, 256)
        DMA("pool", GLUW, glu_w[l].rearrange("(ct p) c -> p ct c", p=128))

        def sm16():
            return SC.f32(16)
        (DTt, ARE, ZR, TH, RR, KW, K2, THR, S0, SH, C0, NN, ABR, ABI, NR, DEN, CFR, CFI, TA, TB,
         C1, C2, C3, C4, R2, R4) = [sm16() for _ in range(26)]
        WR = SC.f32(7, 16)
        WI = SC.f32(7, 16)
        VR = SC.f32(5, 16)
        VI = SC.f32(5, 16)
        PR = SC.f32(5, 16)
        PI = SC.f32(5, 16)
        NT = 24
        NK = T // 4
        TAR = SC.f32(16, NT)
        TAI = SC.f32(16, NT)
        TBR = SC.f32(16, NT)
        TBI = SC.f32(16, NT)
        GF = SC.f32(2, T)
        _gff = GF.rearrange("p a b -> p (a b)")
        UU = [_gff[:, i * 512:(i + 1) * 512].rearrange("p (a b) -> p a b", a=16, b=32) for i in range(8)]
        AIM = SM[:, 1]
        ACT(DTt, SM[:, 2], AF.Exp)
        TS("dve", ARE, SM[:, 0], -1e-4, ALU.min)
        TT("dve", ZR, ARE, DTt, ALU.mult)
        TT("dve", TH, AIM, DTt, ALU.mult)
        ACT(RR, ZR, AF.Exp)
        TS("dve", KW, TH, math.pi, ALU.is_gt)
        for mm_ in (3, 5, 7):
            STT("dve", K2, TH, mm_ * math.pi, KW, ALU.is_gt, ALU.add)
            COPY("dve", KW, K2)
        STT("dve", THR, KW, -TWO_PI_HI, TH, ALU.mult, ALU.add)
        STT("dve", TA, KW, -TWO_PI_LO, THR, ALU.mult, ALU.add)
        ACT(S0, TA, AF.Sin)
        ACT(SH, TA, AF.Sin, scale=0.5)
        TT("dve", TB, SH, SH, ALU.mult)
        TS("dve", C0, TB, -2.0, ALU.mult, 1.0, ALU.add)

        def renorm(re, im):
            TT("dve", NN, re, re, ALU.mult)
            TT("dve", TB, im, im, ALU.mult)
            TT("dve", NN, NN, TB, ALU.add)
            TS("dve", TB, NN, -0.5, ALU.mult, 1.5, ALU.add)
            TT("dve", re, re, TB, ALU.mult)
            TT("dve", im, im, TB, ALU.mult)

        def cmul(orr, oi, ar, ai, br, bi):
            TT("dve", C1, ar, br, ALU.mult)
            TT("dve", C2, ai, bi, ALU.mult)
            TT("dve", C3, ar, bi, ALU.mult)
            TT("dve", C4, ai, br, ALU.mult)
            TT("dve", orr, C1, C2, ALU.subtract)
            TT("dve", oi, C3, C4, ALU.add)
        renorm(C0, S0)
        TT("dve", ABR, RR, C0, ALU.mult)
        TT("dve", ABI, RR, S0, ALU.mult)
        TS("dve", NR, ABR, -1.0, ALU.add)
        TT("dve", DEN, ARE, ARE, ALU.mult)
        TT("dve", TB, AIM, AIM, ALU.mult)
        TT("dve", DEN, DEN, TB, ALU.add)
        RECIP(DEN, DEN)
        TT("dve", TA, NR, ARE, ALU.mult)
        TT("dve", TB, ABI, AIM, ALU.mult)
        TT("dve", TA, TA, TB, ALU.add)
        TT("dve", CFR, TA, DEN, ALU.mult)
        TT("dve", TA, ABI, ARE, ALU.mult)
        TT("dve", TB, NR, AIM, ALU.mult)
        TT("dve", TA, TA, TB, ALU.subtract)
        TT("dve", CFI, TA, DEN, ALU.mult)
        MEMSET("dve", PR[:, 0], 1.0)
        MEMSET("dve", PI[:, 0], 0.0)
        COPY("dve", PR[:, 1], ABR)
        COPY("dve", PI[:, 1], ABI)
        cmul(PR[:, 2], PI[:, 2], PR[:, 1], PI[:, 1], PR[:, 1], PI[:, 1])
        cmul(PR[:, 3], PI[:, 3], PR[:, 2], PI[:, 2], PR[:, 1], PI[:, 1])
        cmul(PR[:, 4], PI[:, 4], PR[:, 2], PI[:, 2], PR[:, 2], PI[:, 2])
        TT("dve", R2, RR, RR, ALU.mult)
        TT("dve", R4, R2, R2, ALU.mult)
        COPY("dve", WR[:, 0], C0)
        TS("dve", WI[:, 0], S0, -1.0, ALU.mult)

        def csquare(XR, XI, k):
            TT("dve", TA, XR[:, k], XR[:, k], ALU.mult)
            TT("dve", TB, XI[:, k], XI[:, k], ALU.mult)
            TT("dve", XR[:, k + 1], TA, TB, ALU.subtract)
            STT("dve", XI[:, k + 1], XR[:, k], 2.0, XI[:, k], ALU.mult, ALU.mult)
            renorm(XR[:, k + 1], XI[:, k + 1])
        for k in range(6):
            csquare(WR, WI, k)
        cmul(VR[:, 0], VI[:, 0], WR[:, 6], WI[:, 6], WR[:, 5], WI[:, 5])
        renorm(VR[:, 0], VI[:, 0])
        for k in range(4):
            csquare(VR, VI, k)

        def build_tab(eng, TR_, TI_, XR, XI, U, k0):
            MEMSET(eng, TR_[:, :, 0:1], 1.0)
            MEMSET(eng, TI_[:, :, 0:1], 0.0)
            for k in range(5):
                h = 1 << k
                n = min(h, NT - h)
                wr = XR[:, k0 + k, :].unsqueeze(2).broadcast_to([128, 16, n])
                wi_ = XI[:, k0 + k, :].unsqueeze(2).broadcast_to([128, 16, n])
                TT(eng, U[0][:, :, 0:n], TI_[:, :, 0:n], wi_, ALU.mult)
                TT(eng, U[1][:, :, 0:n], TR_[:, :, 0:n], wr, ALU.mult)
                TT(eng, U[2][:, :, 0:n], TI_[:, :, 0:n], wr, ALU.mult)
                TT(eng, U[3][:, :, 0:n], TR_[:, :, 0:n], wi_, ALU.mult)
                TT(eng, TR_[:, :, h:h + n], U[1][:, :, 0:n], U[0][:, :, 0:n], ALU.subtract)
                TT(eng, TI_[:, :, h:h + n], U[3][:, :, 0:n], U[2][:, :, 0:n], ALU.add)
        build_tab("dve", TBR, TBI, WR, WI, UU[0:4], 2)
        build_tab("pool", TAR, TAI, VR, VI, UU[4:8], 0)

        ER = SC.f32(NK)
        EI = SC.f32(NK)
        MR = SC.f32(NK)
        MI = SC.f32(NK)
        ERv = ER.rearrange("p (a b) -> p a b", a=NT, b=NT)
        EIv = EI.rearrange("p (a b) -> p a b", a=NT, b=NT)
        TM1v = MR.rearrange("p (a b) -> p a b", a=NT, b=NT)
        TM2v = MI.rearrange("p (a b) -> p a b", a=NT, b=NT)
        TQF = SC.f32(1152)
        TQ2 = [TQF[:, 0:NK], TQF[:, NK:2 * NK]]
        TQP = [TQF[:, 0:512].rearrange("p (a b) -> p a b", a=2, b=256), TQF[:, 512:1024].rearrange("p (a b) -> p a b", a=2, b=256)]
        BRAW = SC.f32(2, 128)
        CRAW = SC.f32(2, 128)
        NCIM = SC.f32(128)
        XX = SC.f32(4, 2, 128)
        XTS = [SC.f32(128) for _ in range(4)]
        XT1 = XTS[0]
        XT2 = XTS[1]
        LBb = [SC.bf(4, 2, 128), SC.bf(4, 2, 128)]
        VALL = SC.bf(8, 4, 2, 128)
        SALL = SC.bf(8, 2, NK + 2)
        GB = SC.bf(2, T)
        _xxf = XX.rearrange("p a b c -> p (a b c)")
        YF = _xxf[:, 0:512]
        X2 = _xxf[:, 512:1024]
        MEMSET("pool", ZEROB, 0.0)
        pieces_f = [(0, 64, 0), (64, 256, 256), (320, 256, 1280)]
        pieces_b = [(0, 256, 256), (256, 256, 1280), (512, 64, 0)]
        tok_pieces = [(0, 64, 0, 512), (256, 256, 64, 0), (1280, 256, 320, 256)]
        zi = 0
        oi = 0
        for ct in range(2):
            MEMSET("pool", SALL[:, :, :, 0:1], 0.0)
            MEMSET("pool", SALL[:, :, :, NK + 1:NK + 2], 0.0)
            MM(PSB[3][:, 0:256], IDB, ZEROB, True, False, sgc=True)
            MM(PSB[3][:, 256:512], IDB, ZEROB, False, False, sgc=True)
            MM(PSB[4][:, 0:256], IDB, ZEROB, True, False, sgc=True)
            MM(PSB[4][:, 256:512], IDB, ZEROB, False, False, sgc=True)
            qlist = [(dr_, st_) for dr_ in range(2) for st_ in range(4 * ct, 4 * ct + 4)]
            def ssm_prep(dr, st, q, ql, LB):
                DMA("sp", BRAW, ssm_B[l, q])
                DMA("sp", CRAW, ssm_C[l, q])
                cfr = CFR[:, q:q + 1]
                cfi = CFI[:, q:q + 1]
                ACT(XT1, BRAW[:, 1], AF.Copy, scale=cfi)
                STT("dve", XX[:, 0, 0], BRAW[:, 0], cfr, XT1, ALU.mult, ALU.subtract)
                ACT(XT2, BRAW[:, 0], AF.Copy, scale=cfi)
                STT("dve", XX[:, 0, 1], BRAW[:, 1], cfr, XT2, ALU.mult, ALU.add)
                for tau in range(1, 4):
                    pr = PR[:, tau, q:q + 1]
                    pi_ = PI[:, tau, q:q + 1]
                    xa = XTS[(2 * tau) % 4]
                    xb = XTS[(2 * tau + 1) % 4]
                    ACT(xa, XX[:, 0, 1], AF.Copy, scale=pi_)
                    STT("dve", XX[:, tau, 0], XX[:, 0, 0], pr, xa, ALU.mult, ALU.subtract)
                    ACT(xb, XX[:, 0, 0], AF.Copy, scale=pi_)
                    STT("dve", XX[:, tau, 1], XX[:, 0, 1], pr, xb, ALU.mult, ALU.add)
                lbf = LB.rearrange("p a b c -> p (a b c)")
                for rnd in range(2):
                    for i4 in range(4):
                        i8 = rnd * 4 + i4
                        TR(PSB[2][:, i4 * 128:(i4 + 1) * 128], XX[:, i8 // 2, i8 % 2], IDF)
                    COPY("act", lbf[:, rnd * 512:(rnd + 1) * 512], PSB[2][:, 0:512])
                ACT(NCIM, CRAW[:, 1], AF.Copy, scale=-1.0)
                for r in range(4):
                    e_ = (r + 1) if dr == 0 else (4 - r)
                    pr = PR[:, e_, q:q + 1]
                    pi_ = PI[:, e_, q:q + 1]
                    xa = XTS[(2 * r) % 4]
                    xb = XTS[(2 * r + 1) % 4]
                    ACT(xa, CRAW[:, 1], AF.Copy, scale=pi_)
                    STT("dve", VALL[:, ql, r, 0], CRAW[:, 0], pr, xa, ALU.mult, ALU.subtract)
                    ACT(xb, CRAW[:, 0], AF.Copy, scale=pi_)
                    STT("dve", VALL[:, ql, r, 1], CRAW[:, 1], pr, xb, ALU.mult, ALU.add)
                for tau in range(4):
                    slot = 0 if tau == 0 else (tau if dr == 0 else 3 + tau)
                    bank = PSB[3] if slot < 4 else PSB[4]
                    cs = (slot % 4) * 128
                    lastc = (ql == 7) if (slot == 0 or slot >= 4) else (ql == 3)
                    MM(bank[:, cs:cs + 128], XX[:, tau, 0], CRAW[:, 0], False, False, sgc=True)
                    MM(bank[:, cs:cs + 128], XX[:, tau, 1], NCIM, False, lastc, sgc=True)

            ssm_prep(qlist[0][0], qlist[0][1], qlist[0][0] * 8 + qlist[0][1], qlist[0][0] * 4 + (qlist[0][1] % 4), LBb[0])
            for qi, (dr, st) in enumerate(qlist):
                if True:
                    q = dr * 8 + st
                    ql = dr * 4 + (st % 4)
                    LB = LBb[qi % 2]
                    pcs = (pieces_f if dr == 0 else pieces_b)

                    def zmm(pi3):
                        p0, n, tok0 = pcs[pi3]
                        bank = PSB[pi3 % 2]
                        for ri in range(2):
                            for j in range(4):
                                tau = (3 - j) if dr == 0 else j
                                MM(bank[:, ri * 256:ri * 256 + n], LB[:, tau, ri, :],
                                   UT[:, ct, tok0 + j:tok0 + 4 * n:4], j == 0, j == 3)

                    def zmodul(pi3):
                        p0, n, tok0 = pcs[pi3]
                        bank = PSB[pi3 % 2]
                        if dr == 0:
                            er = ER[:, p0:p0 + n]
                            ei = EI[:, p0:p0 + n]
                        else:
                            er = ER[:, NK - p0 - n:NK - p0][:, ::-1]
                            ei = EI[:, NK - p0 - n:NK - p0][:, ::-1]
                        zv = bank[:, 0:512].rearrange("p (a b) -> p a b", a=2, b=256)[:, :, 0:n]
                        pei = TQP[0][:, :, 0:n]
                        per = TQP[1][:, :, 0:n]
                        TT("dve", pei, zv, ei.unsqueeze(1).broadcast_to([128, 2, n]), ALU.mult)
                        TT("dve", per, zv, er.unsqueeze(1).broadcast_to([128, 2, n]), ALU.mult)
                        TT("dve", MR[:, p0:p0 + n], per[:, 0, :], pei[:, 1, :], ALU.subtract)
                        TT("dve", MI[:, p0:p0 + n], per[:, 1, :], pei[:, 0, :], ALU.add)
                    zmm(0)
                    zmm(1)
                    if qi + 1 < len(qlist):
                        ndr, nst = qlist[qi + 1]
                        ssm_prep(ndr, nst, ndr * 8 + nst, ndr * 4 + (nst % 4), LBb[(qi + 1) % 2])
                    arb = TAR[:, q, :].unsqueeze(2).broadcast_to([128, NT, NT])
                    aib = TAI[:, q, :].unsqueeze(2).broadcast_to([128, NT, NT])
                    brb = TBR[:, q, :].unsqueeze(1).broadcast_to([128, NT, NT])
                    bib = TBI[:, q, :].unsqueeze(1).broadcast_to([128, NT, NT])
                    TT("pool", TM1v, aib, bib, ALU.mult)
                    TT("dve", ERv, arb, brb, ALU.mult)
                    TT("pool", TM2v, aib, brb, ALU.mult)
                    TT("dve", EIv, arb, bib, ALU.mult)
                    TT("dve", ERv, ERv, TM1v, ALU.subtract)
                    TT("dve", EIv, EIv, TM2v, ALU.add)
                    zmodul(0)
                    zmm(2)
                    zmodul(1)
                    zmodul(2)
                    rr = R4[:, q:q + 1].broadcast_to([128, NK])
                    for Mx in (MR, MI):
                        if dr == 0:
                            S.op("dve", (lambda Mx, rr: lambda e: e.tensor_tensor_scan(
                                out=Mx, data0=rr, data1=Mx, initial=0.0, op0=ALU.mult, op1=ALU.add))(Mx, rr),
                                reads=[Mx, R4[:, q:q + 1]], writes=[Mx])
                        else:
                            S.op("dve", (lambda Mx, rr: lambda e: e.tensor_tensor_scan(
                                out=Mx[:, ::-1], data0=rr, data1=Mx[:, ::-1], initial=0.0, op0=ALU.mult, op1=ALU.add))(Mx, rr),
                                reads=[Mx, R4[:, q:q + 1]], writes=[Mx])
                    erf = ER if dr == 0 else ER[:, ::-1]
                    eif = EI if dr == 0 else EI[:, ::-1]
                    t0, t1 = TQ2
                    TT("dve", t0, MR, erf, ALU.mult)
                    TT("dve", t1, MI, eif, ALU.mult)
                    TT("dve", SALL[:, ql, 0, 1:1 + NK], t0, t1, ALU.add)
                    TT("dve", t0, MR, eif, ALU.mult)
                    TT("dve", t1, MI, erf, ALU.mult)
                    TT("dve", SALL[:, ql, 1, 1:1 + NK], t0, t1, ALU.subtract)
                    if ATT_I[0] < NTT:
                        attn_slice(ATT_I[0])
                        ATT_I[0] += 1
            tmf = TMAT.rearrange("p a b -> p (a b)")
            COPY("act", tmf[:, 0:512], PSB[3][:, 0:512])
            COPY("act", tmf[:, 512:896], PSB[4][:, 0:384])
            for r in range(4):
                for (tok0, n, pf0, pb0) in tok_pieces:
                    bank = PSB[oi % 2]
                    oi += 1
                    mms = []
                    for j in range(4):
                        slot = 0 if j == r else ((r - j) if j < r else 3 + (j - r))
                        mms.append((TMAT[:, slot, :], UT[:, ct, tok0 + j:tok0 + 4 * n:4]))
                    for ql in range(8):
                        c0_ = pf0 if ql < 4 else pb0 + 2
                        mms.append((VALL[:, ql, r, 0, :], SALL[:, ql, 0, c0_:c0_ + n]))
                        mms.append((VALL[:, ql, r, 1, :], SALL[:, ql, 1, c0_:c0_ + n]))
                    for i_, (w_, x_) in enumerate(mms):
                        MM(bank[:, 0:n], w_, x_, i_ == 0, i_ == len(mms) - 1)
                    STT("dve", GF[:, ct, tok0 + r:tok0 + 4 * n:4], UT[:, ct, tok0 + r:tok0 + 4 * n:4],
                        SDG[:, ct, 0:1], bank[:, 0:n], ALU.mult, ALU.add)
            for ci, (c0, w, kind) in enumerate(CH):
                yf = GF[:, ct, c0:c0 + w]
                x2 = X2[:, 0:w]
                ACT(x2, yf, AF.Square)
                TS("dve", x2, x2, 0.044715, ALU.mult, 1.0, ALU.add)
                TT("dve", x2, x2, yf, ALU.mult)
                ACT(x2, x2, AF.Sigmoid, scale=1.5957691216057308)
                TT("dve", yf, yf, x2, ALU.mult)
                COPY("pool", GB[:, ct, c0:c0 + w], yf)
        for ot in range(2):
            for (c0, w, kind) in CH:
                ps = PSB[(ot % 2)]
                for ct in range(2):
                    MM(ps[:, 0:w], GLUW[:, ct, ot * 128:(ot + 1) * 128], GB[:, ct, c0:c0 + w], ct == 0, ct == 1)
                sg = YF[:, 0:w]
                ACT(sg, ps[:, 0:w], AF.Sigmoid, bias=SDG[:, ot, 1:2])
                TT("dve", MIX[:, 2 + ot, c0:c0 + w], GF[:, ot, c0:c0 + w], sg, ALU.mult)
        if dbg and l == 0:
            dbg_dump("mix_ps", MIX[:, 0:4], [128, 4, T])
        if stop == "ssm":
            return

        while ATT_I[0] < NTT:
            attn_slice(ATT_I[0])
            ATT_I[0] += 1
        if dbg and l == 0:
            dbg_dump("mix", MIX, [128, 8, T])
        if stop == "attn":
            return

        PP.reset()
        SC.cur = att_end
        H2 = PP.bf(8, T)
        WO = SC.bf(8, D)
        XCb = [SC.f32(8, 512), SC.f32(8, 512)]
        SQ = SC.bf(8, 512)
        RS = SC.f32(512)
        HN = [SC.f32(512), SC.f32(512)]
        DMA("pool", WO, w_out[l].rearrange("(kt p) c -> p kt c", p=128))
        DMA("sp", XCb[0][:, :, 0:CH[0][1]], xsv[:, :, CH[0][0]:CH[0][0] + CH[0][1]])
        for ci, (c0, w, kind) in enumerate(CH):
            XC = XCb[ci % 2]
            if ci + 1 < len(CH):
                n0, nw, _ = CH[ci + 1]
                DMA("sp", XCb[(ci + 1) % 2][:, :, 0:nw], xsv[:, :, n0:n0 + nw])
            for j in range(8):
                ps = nextps()
                for k in range(8):
                    MM(ps[:, 0:w], WO[:, k, j * 128:(j + 1) * 128], MIX[:, k, c0:c0 + w], k == 0, k == 7)
                STT("dve", XC[:, j, 0:w], ps[:, 0:w], modv(l, 2, j, kind), XC[:, j, 0:w], ALU.mult, ALU.add)
            DMA("pool", xdv[:, :, c0:c0 + w], XC[:, :, 0:w])
            rmsnorm_chunk(XC, w, GE[:, l, 1], 3, l, kind, H2[:, :, c0:c0 + w], SQ, RS, HN)
        if dbg and l == 0:
            dbg_dump("h2", H2, [128, 8, T])
        if stop == "wout":
            return

        SC.reset()
        ACTF = SC.bf(NF, T)
        WGb = [SC.bf(8, 128), SC.bf(8, 128)]
        WUb = [SC.bf(8, 128), SC.bf(8, 128)]
        SGb = [SC.f32(512), SC.f32(512)]
        X2b = [SC.f32(T), SC.f32(T)]
        WDb = [PP.bf(NF, 128), PP.bf(NF, 128)]
        wgv = w_gate[l].rearrange("(kt p) c -> p kt c", p=128)
        wuv = w_up[l].rearrange("(kt p) c -> p kt c", p=128)
        wdv = w_down[l].rearrange("(f p) c -> p f c", p=128)
        chs = CH[1:] if last else CH
        DMA("pool", WGb[0], wgv[:, :, 0:128])
        DMA("pool", WUb[0], wuv[:, :, 0:128])
        k2 = 0
        for f in range(NF):
            if f + 1 < NF:
                DMA("pool", WGb[(f + 1) % 2], wgv[:, :, (f + 1) * 128:(f + 2) * 128])
                DMA("pool", WUb[(f + 1) % 2], wuv[:, :, (f + 1) * 128:(f + 2) * 128])
            WG, WU = WGb[f % 2], WUb[f % 2]
            for (c0, w, kind) in chs:
                pg = PSB[(2 * k2) % 8]
                pu = PSB[(2 * k2 + 1) % 8]
                sg = SGb[k2 % 2]
                k2 += 1
                for k in range(8):
                    MM(pg[:, 0:w], WG[:, k, :], H2[:, k, c0:c0 + w], k == 0, k == 7)
                for k in range(8):
                    MM(pu[:, 0:w], WU[:, k, :], H2[:, k, c0:c0 + w], k == 0, k == 7)
                ACT(sg[:, 0:w], pg[:, 0:w], AF.Silu)
                TT("dve", ACTF[:, f, c0:c0 + w], sg[:, 0:w], pu[:, 0:w], ALU.mult)
        DMA("pool", WDb[0], wdv[:, :, 0:128])
        DMA("sp", X2b[0], Xd[0:128, :])
        k2 = 0
        for j in range(8):
            if j + 1 < 8:
                DMA("pool", WDb[(j + 1) % 2], wdv[:, :, (j + 1) * 128:(j + 2) * 128])
                DMA("sp", X2b[(j + 1) % 2], Xd[(j + 1) * 128:(j + 2) * 128, :])
            WD = WDb[j % 2]
            x2 = X2b[j % 2]
            for (c0, w, kind) in chs:
                ps = PSB[k2 % 8]
                k2 += 1
                for f in range(NF):
                    MM(ps[:, 0:w], WD[:, f, :], ACTF[:, f, c0:c0 + w], f == 0, f == NF - 1)
                STT("dve", x2[:, c0:c0 + w], ps[:, 0:w], modv(l, 5, j, kind), x2[:, c0:c0 + w], ALU.mult, ALU.add)
            if last:
                DMA("pool", outT[j * 128:(j + 1) * 128, :], x2[:, LC:T])
            else:
                DMA("pool", Xd[j * 128:(j + 1) * 128, :], x2)

    for l in range(n_layers):
        if stop in ('pro', 'consts'):
            break
        layer_body(l)
    S.emit()
    return nc, dbg_out


def _host_consts():
    f32 = np.float32
    ident = np.eye(128, dtype=f32)
    pswap = np.zeros((128, 128), f32)
    for m in range(128):
        hb, d = divmod(m, 64)
        pswap[hb * 64 + _pi_perm(d), m] = 1.0
    blk64 = np.zeros((128, 128), f32)
    blk64[:64, :64] = 1.0 / 64
    blk64[64:, 64:] = 1.0 / 64
    t = np.arange(SEQ)
    row = (t // 64).astype(f32)
    col = (t % 64).astype(f32)
    inv = np.power(f32(10000.0), -np.arange(16, dtype=f32) / f32(16)).astype(f32)
    rope_c = np.zeros((128, SEQ), f32)
    rope_s = np.zeros((128, SEQ), f32)
    for p in range(128):
        d = p % 64
        fidx = d % 16
        pos = row if d < 32 else col
        ang = (pos * inv[fidx]).astype(f32)
        rope_c[p] = np.cos(ang)
        s = np.sin(ang)
        rope_s[p] = -s if (d % 32) < 16 else s
    ik = np.arange(128)[:, None]
    iq = np.arange(128)[None, :]
    mL = np.where(ik < iq, NEG, 0.0).astype(f32)
    mR = np.where(ik > iq, NEG, 0.0).astype(f32)
    maskd = np.zeros((128, 2, 256), f32)
    maskd[:, 0, :128] = mL
    maskd[:, 0, 128:] = mL
    maskd[:, 1, :128] = mR
    maskd[:, 1, 128:] = mR
    TT_ = 512
    band = np.zeros((128, 5, 2, 256), f32)
    for gi, wdw in enumerate((2, 4, 8, 16)):
        tt = np.arange(TT_)
        lo = np.clip(tt - wdw // 2, 0, TT_)
        hi = np.clip(tt + wdw // 2, 0, TT_)
        M = np.zeros((TT_, TT_), np.float64)
        for i in range(TT_):
            M[i, lo[i]:hi[i]] = 1.0 / (hi[i] - lo[i])
            M[i, i] -= 1.0
        MT = M.T
        gp, gl = divmod(gi, 2)
        sl = slice(gl * 128, (gl + 1) * 128)
        band[:, 0, gp, sl] = MT[0:128, 0:128]
        band[:, 1, gp, sl] = MT[128:256, 128:256]
        band[:, 2, gp, sl] = MT[384:512, 384:512]
        band[:, 3, gp, sl] = MT[128:256, 256:384]
        band[:, 4, gp, sl] = MT[256:384, 128:256]
    return dict(ident=ident, pswap=pswap, blk64=blk64, rope_c=rope_c, rope_s=rope_s, maskd=maskd, band=band)


def _host_layout(inp):
    f32 = np.float32
    g = lambda k: np.asarray(inp[k], dtype=f32)
    sh = {}
    sh["w_mod"] = np.ascontiguousarray(g("w_mod"))
    sh["b_modp"] = np.ascontiguousarray(g("b_mod").reshape(NL, 6, 8, 128).transpose(0, 3, 1, 2).reshape(NL, 128, 48))
    nm = g("norm_mix").reshape(NL, 8, 128).transpose(0, 2, 1)
    nf = g("norm_ffn").reshape(NL, 8, 128).transpose(0, 2, 1)
    sh["normp"] = np.ascontiguousarray(np.stack([nm, nf], axis=2))
    wi = g("w_in")
    k0 = wi[:, :, 1024:1088]
    k1 = wi[:, :, 1088:1152]
    sh["w_inp"] = np.ascontiguousarray(np.concatenate(
        [wi[:, :, 0:256], wi[:, :, 1152:1280], wi[:, :, 256:512], wi[:, :, 512:1024], k0, k0, k1, k1], axis=2))
    sh["w_out"] = np.ascontiguousarray(g("w_out"))
    pw = g("pool_w")
    pbd = np.zeros((NL, 128, 2, 128), f32)
    for gp in range(2):
        for gl in range(2):
            pbd[:, gl * 64:(gl + 1) * 64, gp, gl * 64:(gl + 1) * 64] = pw[:, 2 * gp + gl]
    sh["poolwbd"] = pbd
    sh["poolsc"] = np.ascontiguousarray(g("pool_scale").reshape(NL, 2, 128).transpose(0, 2, 1))
    def pq(a):
        a = a.reshape(NL, 2, 8, 2, 64)
        return a.transpose(0, 3, 4, 1, 2).reshape(NL, 128, 16)
    are = pq(g("ssm_a_re"))
    aim = pq(g("ssm_a_im"))
    ldt = pq(np.repeat(g("ssm_log_dt")[:, :, :, None], 64, axis=3))
    sh["ssm_small"] = np.ascontiguousarray(np.stack([are, aim, ldt], axis=2))
    bre, bim = g("ssm_b_re"), g("ssm_b_im")
    cre, cim = g("ssm_c_re"), g("ssm_c_im")
    sB = np.zeros((NL, 16, 128, 2, 128), f32)
    sC = np.zeros((NL, 16, 128, 2, 128), f32)
    for dr in range(2):
        for st in range(8):
            q = dr * 8 + st
            for gg in range(2):
                gi = 2 * st + gg
                c0 = 32 * (st % 4) + 16 * gg
                sB[:, q, gg * 64:(gg + 1) * 64, 0, c0:c0 + 16] = bre[:, dr, gi]
                sB[:, q, gg * 64:(gg + 1) * 64, 1, c0:c0 + 16] = bim[:, dr, gi]
                sC[:, q, gg * 64:(gg + 1) * 64, 0, c0:c0 + 16] = cre[:, dr, gi].transpose(0, 2, 1)
                sC[:, q, gg * 64:(gg + 1) * 64, 1, c0:c0 + 16] = cim[:, dr, gi].transpose(0, 2, 1)
    sh["ssm_B"] = sB
    sh["ssm_C"] = sC
    dsk = g("ssm_d").reshape(NL, 2, 128).transpose(0, 2, 1)
    glb = g("ssm_glu_b").reshape(NL, 2, 128).transpose(0, 2, 1)
    sh["ssm_dg"] = np.ascontiguousarray(np.stack([dsk, glb], axis=3))
    sh["glu_w"] = np.ascontiguousarray(g("ssm_glu_w"))
    qn, kn = g("q_norm"), g("k_norm")
    idx = np.arange(128) % 64
    pidx = np.array([_pi_perm(d) for d in idx])
    sh["qkn"] = np.ascontiguousarray(np.stack([qn[:, idx], qn[:, pidx], kn[:, idx], kn[:, pidx]], axis=2))
    sh["sinkr"] = np.ascontiguousarray(np.repeat(g("attn_sink")[:, None, :], 128, axis=1))
    sh["w_gate"] = np.ascontiguousarray(g("ffn_w_gate"))
    sh["w_up"] = np.ascontiguousarray(g("ffn_w_up"))
    sh["w_down"] = np.ascontiguousarray(g("ffn_w_down"))
    sh.update(_host_consts())
    x, c, ctx, c_ctx = g("x"), g("c"), g("ctx"), g("c_ctx")
    maps = []
    for b in range(NB):
        m = dict(sh)
        m["xT"] = np.ascontiguousarray(np.concatenate([ctx[b].T, x[b].T], axis=1))
        cv = np.stack([c[b].reshape(8, 128).T, c_ctx.reshape(8, 128).T], axis=2)
        m["cvec"] = np.ascontiguousarray(cv)
        maps.append(m)
    return maps


_PROG = {}


def kernel(**inputs):
    maps = _host_layout(inputs)
    if "nc" not in _PROG:
        _PROG["nc"] = build_program()[0]
    nc = _PROG["nc"]
    res = run_bass_kernel_spmd(nc, maps, core_ids=list(range(NB)))
    out = np.stack([np.ascontiguousarray(r["outT"].T) for r in res.results], axis=0)
    return out.astype(np.float32)
```

```python
import math
import numpy as np
SK = ()
import concourse.bass as bass
import concourse.mybir as mybir
from concourse.bass_utils import run_bass_kernel_spmd

F32 = mybir.dt.float32
BF = mybir.dt.bfloat16
AF = mybir.ActivationFunctionType
ALU = mybir.AluOpType

ENGS = ("pe", "act", "dve", "pool", "sp")


class Foot:
    __slots__ = ("name", "p0", "p1", "iv")

    def __init__(self, ap):
        self.name = ap.tensor.name
        es = mybir.dt.size(ap.dtype)
        pairs = [tuple(x) for x in ap.ap]
        off = ap.offset
        if str(ap.space) == "DRAM":
            self.p0, self.p1 = 0, 1
            dims = pairs
            base = off
        else:
            pstep, pcnt = pairs[0]
            if pstep == 0:
                self.p0 = 0
                base = off
            else:
                self.p0 = off // pstep
                base = off - self.p0 * pstep
            self.p1 = self.p0 + pcnt
            dims = pairs[1:]
        ivs = [(base, base + 1)]
        for step, cnt in reversed(dims):
            if cnt == 1 or step == 0:
                continue
            if len(ivs) == 1 and abs(step) == ivs[0][1] - ivs[0][0]:
                lo, hi = ivs[0]
                if step > 0:
                    ivs = [(lo, lo + step * cnt)]
                else:
                    ivs = [(lo + step * (cnt - 1), hi)]
                continue
            if len(ivs) * cnt > 48:
                lo = min(a for a, _ in ivs)
                hi = max(b for _, b in ivs)
                ext = step * (cnt - 1)
                ivs = [(lo + min(0, ext), hi + max(0, ext))]
                continue
            new = []
            for k in range(cnt):
                for a, b in ivs:
                    new.append((a + k * step, b + k * step))
            new.sort()
            ivs = new
        ivs.sort()
        self.iv = tuple((a * es, b * es) for a, b in ivs)

    def overlaps(self, o):
        if self.p1 <= o.p0 or o.p1 <= self.p0:
            return False
        x, y = self.iv, o.iv
        if len(x) == 1 and len(y) == 1:
            return x[0][0] < y[0][1] and y[0][0] < x[0][1]
        i = j = 0
        while i < len(x) and j < len(y):
            a, b = x[i]
            c, d = y[j]
            if a < d and c < b:
                return True
            if b <= d:
                i += 1
            else:
                j += 1
        return False

    def covers(self, o):
        if not (self.p0 <= o.p0 and o.p1 <= self.p1):
            return False
        if len(self.iv) != 1:
            return False
        a, b = self.iv[0]
        return all(a <= c and d <= b for c, d in o.iv)


class Op:
    __slots__ = ("eng", "fn", "deps", "dma", "sem", "semval", "incd", "count", "seq")
    SEQ = 0

    def __init__(self, eng, fn, dma):
        self.eng, self.fn, self.dma = eng, fn, dma
        Op.SEQ += 1
        self.seq = Op.SEQ
        self.deps = []
        self.sem = None
        self.semval = 0
        self.incd = False
        self.count = 0


class Sched:
    def __init__(self, nc, n_dma_sems=56):
        self.nc = nc
        self.ops = {e: [] for e in ENGS}
        self.recs = {}
        self.n_dma_sems = n_dma_sems

    def op(self, eng, fn, reads=(), writes=(), dma=False):
        o = Op(eng, fn, dma)
        self.ops[eng].append(o)
        rf = [Foot(a) for a in reads if a is not None and not isinstance(a, (int, float))]
        wf = [Foot(a) for a in writes if a is not None]
        deps = {}
        psum_names = set()
        for f in rf:
            if f.name.startswith("psb"):
                psum_names.add(f.name)
                continue
            for rec in self.recs.get(f.name, ()):
                if rec[1] == "w" and rec[0].overlaps(f):
                    deps[rec[2]] = True
        for f in wf:
            if f.name.startswith("psb"):
                psum_names.add(f.name)
                continue
            for rec in self.recs.get(f.name, ()):
                if rec[0].overlaps(f):
                    deps.setdefault(rec[2], False)
        for nm in psum_names:
            for rec in self.recs.get(nm, ()):
                if rec[2].eng != eng:
                    deps[rec[2]] = True
        for d, israw in deps.items():
            if d is o:
                continue
            if d.eng == o.eng and not d.dma and not o.dma:
                if o.eng == "pe":
                    continue
            o.deps.append(d)
        for nm in psum_names:
            self.recs[nm] = [(None, "w", o)]
        for f in wf:
            if f.name in psum_names:
                continue
            lst = self.recs.setdefault(f.name, [])
            lst[:] = [r for r in lst if not f.covers(r[0])]
            lst.append((f, "w", o))
        for f in rf:
            if f.name in psum_names:
                continue
            self.recs.setdefault(f.name, []).append((f, "r", o))
        return o

    def emit(self):
        nc = self.nc
        from contextlib import ExitStack
        for e in ENGS:
            for o in self.ops[e]:
                for d in o.deps:
                    d.incd = True
        for e in ENGS:
            c = 0
            for o in self.ops[e]:
                if not o.dma and o.incd:
                    c += 1
                    o.count = c
        with ExitStack() as st:
            csem = {e: st.enter_context(nc.semaphore("c_" + e)) for e in ENGS}
            dsems = [st.enter_context(nc.semaphore("d%d" % i)) for i in range(self.n_dma_sems)]
            alldma = [o for e in ENGS for o in self.ops[e] if o.dma]
            alldma.sort(key=lambda o: o.seq)
            half = self.n_dma_sems // 2
            use = [0] * self.n_dma_sems
            prev = [None] * self.n_dma_sems
            cnt = {"hw": 0, "sw": 0}
            for o in alldma:
                kind = "sw" if o.eng == "pool" else "hw"
                s = (cnt[kind] % half) + (half if kind == "sw" else 0)
                cnt[kind] += 1
                use[s] += 1
                o.sem = s
                o.semval = 16 * use[s]
                if prev[s] is not None:
                    o.deps.append(prev[s])
                prev[s] = o
            blk = st.enter_context(nc.Block())

            def run(ename, eng):
                waited_c = {e: 0 for e in ENGS}
                waited_d = {}
                for o in self.ops[ename]:
                    for d in o.deps:
                        if d.dma:
                            if waited_d.get(d.sem, 0) < d.semval:
                                eng.wait_ge(dsems[d.sem], d.semval)
                                waited_d[d.sem] = d.semval
                        elif waited_c[d.eng] < d.count:
                            eng.wait_ge(csem[d.eng], d.count)
                            waited_c[d.eng] = d.count
                    ins = o.fn(eng)
                    if o.dma:
                        ins.then_inc(dsems[o.sem], 16)
                    elif o.incd:
                        ins.then_inc(csem[ename], 1)
                if ename == "sp":
                    last = {}
                    for o in alldma:
                        last[o.sem] = max(last.get(o.sem, 0), o.semval)
                    for s, v in last.items():
                        eng.wait_ge(dsems[s], v)

            blk.tensor(lambda e: run("pe", e))
            blk.scalar(lambda e: run("act", e))
            blk.vector(lambda e: run("dve", e))
            blk.gpsimd(lambda e: run("pool", e))
            blk.sync(lambda e: run("sp", e))


D = 1024
NB = 8
SEQ = 2048
LC = 256
T = SEQ + LC
NL = 4
DFF = 2816
NF = DFF // 128
EPS = 1e-6
CH = [(0, 256, 1), (256, 512, 0), (768, 512, 0), (1280, 512, 0), (1792, 512, 0)]
NTT = T // 128
C_POOL, C_V, C_SSM, C_Q, C_K, NCOL = 0, 256, 384, 640, 1152, 1408
TWO_PI_HI = 6.28125
TWO_PI_LO = 2.0 * math.pi - 6.28125
NEG = -30000.0


def _pi_perm(d):
    r = d % 32
    return d + 16 if r < 16 else d - 16


def build_program(n_layers=NL, dbg=False, stop=None):
    nc = bass.Bass("TRN2", target_bir_lowering=False)
    S = Sched(nc)

    def din(name, shape):
        return nc.dram_tensor(name, list(shape), F32, kind="ExternalInput").ap()

    xT = din("xT", [D, T])
    cvec = din("cvec", [128, 8, 2])
    w_mod = din("w_mod", [NL, D, 6 * D])
    b_modp = din("b_modp", [NL, 128, 48])
    normp = din("normp", [NL, 128, 2, 8])
    w_inp = din("w_inp", [NL, D, NCOL])
    w_out = din("w_out", [NL, D, D])
    poolwbd = din("poolwbd", [NL, 128, 2, 128])
    poolsc = din("poolsc", [NL, 128, 2])
    band = din("band", [128, 5, 2, 256])
    ssm_small = din("ssm_small", [NL, 128, 3, 16])
    ssm_B = din("ssm_B", [NL, 16, 128, 2, 128])
    ssm_C = din("ssm_C", [NL, 16, 128, 2, 128])
    ssm_dg = din("ssm_dg", [NL, 128, 2, 2])
    glu_w = din("glu_w", [NL, 256, 256])
    qkn = din("qkn", [NL, 128, 4])
    sinkr = din("sinkr", [NL, 128, 8])
    w_gate = din("w_gate", [NL, D, DFF])
    w_up = din("w_up", [NL, D, DFF])
    w_down = din("w_down", [NL, DFF, D])
    ident = din("ident", [128, 128])
    pswap = din("pswap", [128, 128])
    blk64 = din("blk64", [128, 128])
    rope_c = din("rope_c", [128, SEQ])
    rope_s = din("rope_s", [128, SEQ])
    maskd = din("maskd", [128, 2, 256])

    outT = nc.dram_tensor("outT", [D, SEQ], F32, kind="ExternalOutput").ap()
    Xd = nc.dram_tensor("xscratch", [D, T], F32).ap()
    dbg_out = {}

    AW = 53200
    a32 = nc.alloc_sbuf_tensor("arena", [128, AW], F32)
    a16 = a32.bitcast(BF)
    PSB = [nc.alloc_psum_tensor("psb%d" % i, [128, 512], F32) for i in range(8)]
    PSB16 = [p.bitcast(BF) for p in PSB]

    def _shape(v, shape):
        if len(shape) == 1:
            return v
        if len(shape) == 2:
            return v.rearrange("p (a b) -> p a b", a=shape[0], b=shape[1])
        if len(shape) == 3:
            return v.rearrange("p (a b c) -> p a b c", a=shape[0], b=shape[1], c=shape[2])
        return v.rearrange("p (a b c d) -> p a b c d", a=shape[0], b=shape[1], c=shape[2], d=shape[3])

    class Bump:
        def __init__(self, lo, hi):
            self.lo, self.hi, self.cur = lo, hi, lo

        def reset(self):
            self.cur = self.lo

        def take(self, nbytes):
            nbytes = (nbytes + 31) // 32 * 32
            o = self.cur
            self.cur += nbytes
            assert self.cur <= self.hi, ("arena overflow", self.cur, self.hi)
            return o

        def f32(self, *shape):
            n = int(np.prod(shape))
            o = self.take(4 * n)
            return _shape(a32[:, o // 4:o // 4 + n], shape)

        def bf(self, *shape):
            n = int(np.prod(shape))
            o = self.take(2 * n)
            return _shape(a16[:, o // 2:o // 2 + n], shape)

    CONST_B = 14 * 1024
    PROJ_B = 52 * 1024
    PC = Bump(0, CONST_B)
    PP = Bump(CONST_B, CONST_B + PROJ_B)
    SC = Bump(CONST_B + PROJ_B, AW * 4)

    def isap(x):
        return x is not None and not isinstance(x, (int, float))

    def DMA(q, out, in_):
        S.op(q, lambda e: e.dma_start(out=out, in_=in_), reads=[in_], writes=[out], dma=True)

    def TT(eng, out, a, b, op):
        S.op(eng, lambda e: e.tensor_tensor(out=out, in0=a, in1=b, op=op), reads=[a, b], writes=[out])

    def TS(eng, out, a, s1, op0, s2=None, op1=None):
        if op1 is None:
            S.op(eng, lambda e: e.tensor_scalar(out=out, in0=a, scalar1=s1, scalar2=None, op0=op0),
                 reads=[a, s1], writes=[out])
        else:
            S.op(eng, lambda e: e.tensor_scalar(out=out, in0=a, scalar1=s1, scalar2=s2, op0=op0, op1=op1),
                 reads=[a, s1, s2], writes=[out])

    def STT(eng, out, a, sc, b, op0, op1):
        S.op(eng, lambda e: e.scalar_tensor_tensor(out=out, in0=a, scalar=sc, in1=b, op0=op0, op1=op1),
             reads=[a, sc, b], writes=[out])

    def ACT(out, in_, func, bias=None, scale=None):
        kw = {}
        if bias is not None:
            kw["bias"] = bias
        if scale is not None:
            kw["scale"] = scale
        S.op("act", lambda e: e.activation(out=out, in_=in_, func=func, **kw), reads=[in_, bias, scale], writes=[out])

    def COPY(eng, out, in_):
        if eng == "act":
            S.op("act", lambda e: e.activation(out=out, in_=in_, func=AF.Copy), reads=[in_], writes=[out])
        else:
            S.op(eng, lambda e: e.tensor_copy(out=out, in_=in_), reads=[in_], writes=[out])

    def MM(out, lhsT, rhs, start, stop, sgc=False):
        S.op("pe", lambda e: e.matmul(out, lhsT=lhsT, rhs=rhs, start=start, stop=stop, skip_group_check=sgc),
             reads=[lhsT, rhs] + ([] if start else [out]), writes=[out])

    def TR(out, in_, idn):
        S.op("pe", lambda e: e.transpose(out, in_, idn), reads=[in_, idn], writes=[out])

    def MEMSET(eng, out, val):
        S.op(eng, lambda e: e.memset(out, val), writes=[out])

    def RECIP(out, in_):
        S.op("dve", lambda e: e.reciprocal(out=out, in_=in_), reads=[in_], writes=[out])

    def dbg_dump(name, ap, shape):
        if not dbg:
            return
        n = int(np.prod(shape[1:]))
        o = nc.dram_tensor("dbg_" + name, [128, n], F32, kind="ExternalOutput").ap()
        dbg_out[name] = o
        letters = "abcd"[:len(shape) - 1]
        flat = ap if len(shape) == 2 else ap.rearrange("p %s -> p (%s)" % (" ".join(letters), " ".join(letters)))
        for i in range(0, n, 1024):
            j = min(n, i + 1024)
            DMA("pool", o[:, i:j], flat[:, i:j])

    IDF = PC.f32(128)
    IDB = PC.bf(128)
    PSWB = PC.bf(128)
    BLKB = PC.bf(128)
    ONEB = PC.bf(128)
    MASKB = PC.bf(2, 256)
    BANDB = PC.bf(5, 2, 256)
    MOD = PC.f32(NL, 48, 2)
    GE = PC.f32(NL, 2, 8, 2)
    EPSC = PC.f32(1)
    NPI = PC.f32(1)
    QKN = PC.f32(4)
    SINKE = PC.f32(8)
    POOLSC = PC.f32(2)
    SDG = PC.f32(2, 2)
    TMAT = PC.bf(7, 128)
    ZEROB = PC.bf(256)

    stg = SC.f32(5 * 2 * 256)
    DMA("sp", IDF, ident)
    COPY("dve", IDB, IDF)
    DMA("sp", stg[:, 0:128], pswap)
    COPY("dve", PSWB, stg[:, 0:128])
    DMA("sp", stg[:, 128:256], blk64)
    COPY("dve", BLKB, stg[:, 128:256])
    MEMSET("dve", ONEB, 1.0 / 1024.0)
    MEMSET("dve", EPSC, EPS)
    MEMSET("dve", NPI, -math.pi)
    DMA("pool", MASKB, maskd)
    for v_ in range(5):
        DMA("pool", BANDB[:, v_], band[:, v_])

    if stop == 'consts':
        n_layers = 0
    CS = SC.f32(8, 2)
    SILC = SC.f32(8, 2)
    SILB = PC.bf(8, 2)
    BMOD = PC.f32(48)
    NRM = PC.f32(2, 8)
    DMA("sp", CS, cvec)
    ACT(SILC, CS, AF.Silu)
    COPY("dve", SILB, SILC)

    def mod_alloc():
        return dict(wm=[SC.bf(8, 512), SC.bf(8, 512)], row=SC.f32(512))

    def mod_load(l, cc, mb):
        wv = w_mod[l].rearrange("(kt p) c -> p kt c", p=128)
        DMA("pool", mb["wm"][cc % 2], wv[:, :, cc * 512:(cc + 1) * 512])

    def mod_chunk(l, cc, mb):
        if cc == 0:
            DMA("sp", BMOD, b_modp[l])
            DMA("sp", NRM, normp[l])
        if cc + 1 < 12:
            mod_load(l, cc + 1, mb)
        wb = mb["wm"][cc % 2]
        ps = nextps()
        for kt in range(8):
            MM(ps[0:2, :], SILB[:, kt, :], wb[:, kt, :], kt == 0, kt == 7)
        COPY("act", mb["row"][0:2, :], ps[0:2, :])
        pst = nextps()
        for b in range(4):
            MM(pst[:, 2 * b:2 * b + 2], mb["row"][0:2, b * 128:(b + 1) * 128], IDF[0:2, 0:2], True, True)
        TT("dve", MOD[:, l, 4 * cc:4 * cc + 4, :], pst[:, 0:8].rearrange("p (a b) -> p a b", a=4, b=2),
           BMOD[:, 4 * cc:4 * cc + 4].unsqueeze(2).broadcast_to([128, 4, 2]), ALU.add)
        if cc == 11:
            for which in range(2):
                sc = MOD[:, l, (1 + 3 * which) * 8:(2 + 3 * which) * 8, :]
                STT("dve", GE[:, l, which], sc, 1.0, NRM[:, which].unsqueeze(2).broadcast_to([128, 8, 2]),
                    ALU.add, ALU.mult)

    PSI = [0]

    def nextps():
        PSI[0] = (PSI[0] + 1) % 8
        return PSB[PSI[0]]

    if n_layers > 0:
        mb0 = mod_alloc()
        mod_load(0, 0, mb0)
        for cc in range(12):
            mod_chunk(0, cc, mb0)

    def modv(l, m, j, r):
        return MOD[:, l, m * 8 + j, r:r + 1]


    def rmsnorm_chunk(XC, w, ge, shm, l, kind, HOUT, SQ, RS, HN):
        ACT(SQ[:, :, 0:w], XC[:, :, 0:w], AF.Square)
        ps = nextps()
        for j in range(8):
            MM(ps[:, 0:w], ONEB, SQ[:, j, 0:w], j == 0, j == 7)
        ACT(RS[:, 0:w], ps[:, 0:w], AF.Sqrt, bias=EPSC[:, 0:1])
        RECIP(RS[:, 0:w], RS[:, 0:w])
        for j in range(8):
            hn = HN[j % 2]
            STT("dve", hn[:, 0:w], XC[:, j, 0:w], ge[:, j, kind:kind + 1], RS[:, 0:w], ALU.mult, ALU.mult)
            ACT(HOUT[:, j, 0:w], hn[:, 0:w], AF.Identity, bias=modv(l, shm, j, kind))

    def layer_body(l):
        xsrc = xT if l == 0 else Xd
        xsv = xsrc.rearrange("(j p) t -> p j t", p=128)
        xdv = Xd.rearrange("(j p) t -> p j t", p=128)
        last = (l == n_layers - 1)

        PP.reset()
        SC.reset()
        UT = PP.bf(2, T)
        QT = PP.bf(4, T)
        KT = PP.bf(2, T)
        VA = PP.bf(NTT, 2, 66)
        POOLU = PP.bf(NTT, 256)

        WIN = SC.bf(8, NCOL)
        ROPEC = SC.f32(SEQ)
        ROPES = SC.f32(SEQ)
        XCb = [SC.f32(8, 512), SC.f32(8, 512)]
        SQ = SC.bf(8, 512)
        HCb = [SC.bf(8, 512), SC.bf(8, 512)]
        RS = SC.f32(512)
        HN = [SC.f32(512), SC.f32(512)]
        QRAWb = [SC.bf(512), SC.bf(512)]
        QSQb = [SC.bf(512), SC.bf(512)]
        RSQb = [SC.f32(512), SC.f32(512)]
        T1b = [SC.f32(512), SC.f32(512)]
        T2b = [SC.f32(512), SC.f32(512)]

        if "win" not in SK:
            DMA("pool", WIN, w_inp[l].rearrange("(kt p) c -> p kt c", p=128))
        DMA("sp", ROPEC, rope_c)
        DMA("sp", ROPES, rope_s)
        DMA("sp", QKN, qkn[l])
        if "vaones" not in SK:
            MEMSET("pool", VA[:, :, :, 64:66], 1.0)
        DMA("sp", XCb[0][:, :, 0:CH[0][1]], xsv[:, :, CH[0][0]:CH[0][0] + CH[0][1]])
        rmsnorm_chunk(XCb[0], CH[0][1], GE[:, l, 0], 0, l, CH[0][2], HCb[0], SQ, RS, HN)
        if len(CH) > 1:
            DMA("sp", XCb[1][:, :, 0:CH[1][1]], xsv[:, :, CH[1][0]:CH[1][0] + CH[1][1]])
        for ci, (c0, w, kind) in enumerate(CH):
            XC = XCb[ci % 2]
            HC = HCb[ci % 2]
            if ci + 1 < len(CH):
                n0, nw, nk = CH[ci + 1]
                rmsnorm_chunk(XCb[(ci + 1) % 2], nw, GE[:, l, 0], 0, l, nk, HCb[(ci + 1) % 2], SQ, RS, HN)
                if ci + 2 < len(CH):
                    m0, mw, _ = CH[ci + 2]
                    DMA("sp", XCb[ci % 2][:, :, 0:mw], xsv[:, :, m0:m0 + mw])
            for tl in range(0 if "tok" in SK else w // 128):
                tt = c0 // 128 + tl
                ps = nextps()
                for j in range(8):
                    MM(ps[:, 0:384], HC[:, j, tl * 128:(tl + 1) * 128], WIN[:, j, 0:384], j == 0, j == 7)
                if "tokpool" not in SK:
                    COPY("act", POOLU[:, tt, :], ps[:, 0:256])
                if "tokva" not in SK:
                    COPY("dve", VA[:, tt, :, 0:64], ps[:, 256:384].rearrange("p (a b) -> p a b", a=2, b=64))
            def post(m, ps):
                if m < 2:
                    COPY("act", UT[:, m, c0:c0 + w], ps[:, 0:w])
                    return
                isq = m < 6
                QRAW, QSQ, RSQ, T1, T2 = QRAWb[m % 2], QSQb[m % 2], RSQb[m % 2], T1b[m % 2], T2b[m % 2]
                dst = QT[:, m - 2, c0:c0 + w] if isq else KT[:, m - 6, c0:c0 + w]
                g = QKN[:, 0:1] if isq else QKN[:, 2:3]
                gp = QKN[:, 1:2] if isq else QKN[:, 3:4]
                ACT(QSQ[:, 0:w], ps[:, 0:w], AF.Square)
                if kind == 0:
                    COPY("act", QRAW[:, 0:w], ps[:, 0:w])
                ps2 = nextps()
                MM(ps2[:, 0:w], BLKB, QSQ[:, 0:w], True, True)
                if kind == 0:
                    ps3 = nextps()
                    MM(ps3[:, 0:w], PSWB, QRAW[:, 0:w], True, True)
                ACT(RSQ[:, 0:w], ps2[:, 0:w], AF.Sqrt, bias=EPSC[:, 0:1])
                RECIP(RSQ[:, 0:w], RSQ[:, 0:w])
                if kind == 1:
                    STT("dve", dst, ps[:, 0:w], g, RSQ[:, 0:w], ALU.mult, ALU.mult)
                else:
                    tp = c0 - LC
                    STT("dve", T1[:, 0:w], ps[:, 0:w], g, ROPEC[:, tp:tp + w], ALU.mult, ALU.mult)
                    STT("dve", T2[:, 0:w], ps3[:, 0:w], gp, ROPES[:, tp:tp + w], ALU.mult, ALU.mult)
                    TT("dve", T1[:, 0:w], T1[:, 0:w], T2[:, 0:w], ALU.add)
                    TT("dve", dst, T1[:, 0:w], RSQ[:, 0:w], ALU.mult)
            prev = None
            for m in range(8):
                ps = nextps()
                for j in range(8):
                    MM(ps[:, 0:w], WIN[:, j, C_SSM + 128 * m:C_SSM + 128 * (m + 1)], HC[:, j, 0:w], j == 0, j == 7)
                if prev is not None:
                    post(*prev)
                prev = (m, ps)
            post(*prev)
        if dbg and l == 0:
            dbg_dump("ut", UT, [128, 2, T])
            dbg_dump("qt", QT, [128, 4, T])
            dbg_dump("kt", KT, [128, 2, T])
            dbg_dump("va", VA, [128, NTT, 2, 66])
            dbg_dump("poolu", POOLU, [128, NTT, 256])
        if stop == "n1":
            return

        SC.reset()
        MIX = SC.bf(8, T)
        sc_mark = SC.cur
        DTb = SC.bf(2, T)
        PWBD = SC.bf(2, 128)
        DMA("pool", PWBD, poolwbd[l])
        DMA("sp", POOLSC, poolsc[l])
        for gp in range(2):
            for tp in range(NTT // 2):
                ps = nextps()
                psv = ps[:, 0:512].rearrange("p (a b) -> p a b", a=2, b=256)
                for ti in range(2):
                    tt = 2 * tp + ti
                    seg0, seg1 = (0, 1) if tt < 2 else (2, NTT - 1)
                    nbs = []
                    if tt > seg0:
                        nbs.append((tt - 1, 3))
                    nbs.append((tt, 0 if tt == seg0 else (2 if tt == seg1 else 1)))
                    if tt < seg1:
                        nbs.append((tt + 1, 4))
                    for i, (nb, var) in enumerate(nbs):
                        MM(psv[:, ti, :], POOLU[:, nb, gp * 128:(gp + 1) * 128], BANDB[:, var, gp, :],
                           i == 0, i == len(nbs) - 1)
                dv = DTb[:, gp, tp * 256:(tp + 1) * 256].rearrange("p (a b) -> p a b", a=2, b=128)
                COPY("act", dv[0:64], psv[0:64, :, 0:128])
                COPY("dve", dv[64:128], psv[64:128, :, 128:256])
        for gp in range(2):
            for (c0, w, kind) in CH:
                ps = nextps()
                MM(ps[:, 0:w], PWBD[:, gp, :], DTb[:, gp, c0:c0 + w], True, True)
                ACT(MIX[:, gp, c0:c0 + w], ps[:, 0:w], AF.Identity, scale=POOLSC[:, gp:gp + 1])

        if stop == "pool":
            dbg_dump("mix_ps", MIX[:, 0:4], [128, 4, T])
            return
        SC.cur = sc_mark
        SKR = SC.f32(8)
        DMA("sp", SKR, sinkr[l])
        ACT(SINKE, SKR, AF.Exp)
        PT = SC.bf(5, 2, 2, 128)
        ONb = [SC.bf(8, 64), SC.bf(8, 64)]
        DENb = SC.f32(4)
        RECb = SC.f32(4)
        ATT_I = [0]
        att_end = SC.cur

        def attn_slice(qt):
            if qt < 2:
                keys = [(0, None), (1, None)]
            else:
                keys = [(0, None), (1, None)]
                if qt > 2:
                    keys.append((qt - 1, 0))
                keys.append((qt, None))
                if qt < NTT - 1:
                    keys.append((qt + 1, 1))
            ON = ONb[qt % 2]
            for kvh in range(2):
                for ki, (kt, mk) in enumerate(keys):
                    for half in range(2):
                        ps = PSB[5 + half]
                        started = False
                        if mk is not None:
                            MM(ps[:, 0:256], IDB, MASKB[:, mk, :], True, False)
                            started = True
                        for hq in range(2):
                            m = 2 * kvh + hq
                            MM(ps[:, hq * 128:(hq + 1) * 128],
                               KT[64 * half:64 * half + 64, kvh, kt * 128:(kt + 1) * 128],
                               QT[64 * half:64 * half + 64, m, qt * 128:(qt + 1) * 128],
                               not started, hq == 1)
                            started = True
                        ACT(PT[:, ki, half].rearrange("p a b -> p (a b)"), ps[:, 0:256], AF.Exp, scale=0.125)
                po = PSB[7] if kvh == 0 else PSB[2]
                pov = po[:, 0:264].rearrange("p (a b) -> p a b", a=4, b=66)
                for hh in range(4):
                    hq, half = hh // 2, hh % 2
                    for ki, (kt, mk) in enumerate(keys):
                        MM(pov[:, hh, 0:65], PT[:, ki, half, hq, :], VA[:, kt, kvh, 0:65], ki == 0, ki == len(keys) - 1)
                TT("dve", DENb, pov[:, :, 64], SINKE[:, 4 * kvh:4 * kvh + 4], ALU.add)
                RECIP(RECb, DENb)
                TT("dve", ON[:, 4 * kvh:4 * kvh + 4, :], pov[:, :, 0:64], RECb.unsqueeze(2).broadcast_to([128, 4, 64]), ALU.mult)
            pt16 = PSB16[5][:, 512:1024]
            onf = ON.rearrange("p a b -> p (a b)")
            for m in range(4):
                TR(pt16[:, m * 128:(m + 1) * 128], onf[:, m * 128:(m + 1) * 128], IDB)
            COPY("act", MIX[:, 4:8, qt * 128:(qt + 1) * 128], pt16.rearrange("p (a b) -> p a b", a=4, b=128))

        SM = SC.f32(3, 16)
        DMA("sp", SM, ssm_small[l])
        DMA("sp", SDG, ssm_dg[l])
        GLUW = SC.bf(2, 256)
        DMA("pool", GLUW, glu_w[l].rearrange("(ct p) c -> p ct c", p=128))

        def sm16():
            return SC.f32(16)
        (DTt, ARE, ZR, TH, RR, KW, K2, THR, S0, SH, C0, NN, ABR, ABI, NR, DEN, CFR, CFI, TA, TB,
         C1, C2, C3, C4, R2, R4) = [sm16() for _ in range(26)]
        WR = SC.f32(7, 16)
        WI = SC.f32(7, 16)
        VR = SC.f32(5, 16)
        VI = SC.f32(5, 16)
        PR = SC.f32(5, 16)
        PI = SC.f32(5, 16)
        NT = 24
        NK = T // 4
        TAR = SC.f32(16, NT)
        TAI = SC.f32(16, NT)
        TBR = SC.f32(16, NT)
        TBI = SC.f32(16, NT)
        GF = SC.f32(2, T)
        _gff = GF.rearrange("p a b -> p (a b)")
        UU = [_gff[:, i * 512:(i + 1) * 512].rearrange("p (a b) -> p a b", a=16, b=32) for i in range(8)]
        AIM = SM[:, 1]
        ACT(DTt, SM[:, 2], AF.Exp)
        TS("dve", ARE, SM[:, 0], -1e-4, ALU.min)
        TT("dve", ZR, ARE, DTt, ALU.mult)
        TT("dve", TH, AIM, DTt, ALU.mult)
        ACT(RR, ZR, AF.Exp)
        TS("dve", KW, TH, math.pi, ALU.is_gt)
        for mm_ in (3, 5, 7):
            STT("dve", K2, TH, mm_ * math.pi, KW, ALU.is_gt, ALU.add)
            COPY("dve", KW, K2)
        STT("dve", THR, KW, -TWO_PI_HI, TH, ALU.mult, ALU.add)
        STT("dve", TA, KW, -TWO_PI_LO, THR, ALU.mult, ALU.add)
        ACT(S0, TA, AF.Sin)
        ACT(SH, TA, AF.Sin, scale=0.5)
        TT("dve", TB, SH, SH, ALU.mult)
        TS("dve", C0, TB, -2.0, ALU.mult, 1.0, ALU.add)

        def renorm(re, im):
            TT("dve", NN, re, re, ALU.mult)
            TT("dve", TB, im, im, ALU.mult)
            TT("dve", NN, NN, TB, ALU.add)
            TS("dve", TB, NN, -0.5, ALU.mult, 1.5, ALU.add)
            TT("dve", re, re, TB, ALU.mult)
            TT("dve", im, im, TB, ALU.mult)

        def cmul(orr, oi, ar, ai, br, bi):
            TT("dve", C1, ar, br, ALU.mult)
            TT("dve", C2, ai, bi, ALU.mult)
            TT("dve", C3, ar, bi, ALU.mult)
            TT("dve", C4, ai, br, ALU.mult)
            TT("dve", orr, C1, C2, ALU.subtract)
            TT("dve", oi, C3, C4, ALU.add)
        renorm(C0, S0)
        TT("dve", ABR, RR, C0, ALU.mult)
        TT("dve", ABI, RR, S0, ALU.mult)
        TS("dve", NR, ABR, -1.0, ALU.add)
        TT("dve", DEN, ARE, ARE, ALU.mult)
        TT("dve", TB, AIM, AIM, ALU.mult)
        TT("dve", DEN, DEN, TB, ALU.add)
        RECIP(DEN, DEN)
        TT("dve", TA, NR, ARE, ALU.mult)
        TT("dve", TB, ABI, AIM, ALU.mult)
        TT("dve", TA, TA, TB, ALU.add)
        TT("dve", CFR, TA, DEN, ALU.mult)
        TT("dve", TA, ABI, ARE, ALU.mult)
        TT("dve", TB, NR, AIM, ALU.mult)
        TT("dve", TA, TA, TB, ALU.subtract)
        TT("dve", CFI, TA, DEN, ALU.mult)
        MEMSET("dve", PR[:, 0], 1.0)
        MEMSET("dve", PI[:, 0], 0.0)
        COPY("dve", PR[:, 1], ABR)
        COPY("dve", PI[:, 1], ABI)
        cmul(PR[:, 2], PI[:, 2], PR[:, 1], PI[:, 1], PR[:, 1], PI[:, 1])
        cmul(PR[:, 3], PI[:, 3], PR[:, 2], PI[:, 2], PR[:, 1], PI[:, 1])
        cmul(PR[:, 4], PI[:, 4], PR[:, 2], PI[:, 2], PR[:, 2], PI[:, 2])
        TT("dve", R2, RR, RR, ALU.mult)
        TT("dve", R4, R2, R2, ALU.mult)
        COPY("dve", WR[:, 0], C0)
        TS("dve", WI[:, 0], S0, -1.0, ALU.mult)

        def csquare(XR, XI, k):
            TT("dve", TA, XR[:, k], XR[:, k], ALU.mult)
            TT("dve", TB, XI[:, k], XI[:, k], ALU.mult)
            TT("dve", XR[:, k + 1], TA, TB, ALU.subtract)
            STT("dve", XI[:, k + 1], XR[:, k], 2.0, XI[:, k], ALU.mult, ALU.mult)
            renorm(XR[:, k + 1], XI[:, k + 1])
        for k in range(6):
            csquare(WR, WI, k)
        cmul(VR[:, 0], VI[:, 0], WR[:, 6], WI[:, 6], WR[:, 5], WI[:, 5])
        renorm(VR[:, 0], VI[:, 0])
        for k in range(4):
            csquare(VR, VI, k)

        def build_tab(eng, TR_, TI_, XR, XI, U, k0):
            MEMSET(eng, TR_[:, :, 0:1], 1.0)
            MEMSET(eng, TI_[:, :, 0:1], 0.0)
            for k in range(5):
                h = 1 << k
                n = min(h, NT - h)
                wr = XR[:, k0 + k, :].unsqueeze(2).broadcast_to([128, 16, n])
                wi_ = XI[:, k0 + k, :].unsqueeze(2).broadcast_to([128, 16, n])
                TT(eng, U[0][:, :, 0:n], TI_[:, :, 0:n], wi_, ALU.mult)
                TT(eng, U[1][:, :, 0:n], TR_[:, :, 0:n], wr, ALU.mult)
                TT(eng, U[2][:, :, 0:n], TI_[:, :, 0:n], wr, ALU.mult)
                TT(eng, U[3][:, :, 0:n], TR_[:, :, 0:n], wi_, ALU.mult)
                TT(eng, TR_[:, :, h:h + n], U[1][:, :, 0:n], U[0][:, :, 0:n], ALU.subtract)
                TT(eng, TI_[:, :, h:h + n], U[3][:, :, 0:n], U[2][:, :, 0:n], ALU.add)
        build_tab("dve", TBR, TBI, WR, WI, UU[0:4], 2)
        build_tab("pool", TAR, TAI, VR, VI, UU[4:8], 0)

        ER = SC.f32(NK)
        EI = SC.f32(NK)
        MR = SC.f32(NK)
        MI = SC.f32(NK)
        ERv = ER.rearrange("p (a b) -> p a b", a=NT, b=NT)
        EIv = EI.rearrange("p (a b) -> p a b", a=NT, b=NT)
        TM1v = MR.rearrange("p (a b) -> p a b", a=NT, b=NT)
        TM2v = MI.rearrange("p (a b) -> p a b", a=NT, b=NT)
        TQF = SC.f32(1152)
        TQ2 = [TQF[:, 0:NK], TQF[:, NK:2 * NK]]
        TQP = [TQF[:, 0:512].rearrange("p (a b) -> p a b", a=2, b=256), TQF[:, 512:1024].rearrange("p (a b) -> p a b", a=2, b=256)]
        BRAW = SC.f32(2, 128)
        CRAW = SC.f32(2, 128)
        NCIM = SC.f32(128)
        XX = SC.f32(4, 2, 128)
        XTS = [SC.f32(128) for _ in range(4)]
        XT1 = XTS[0]
        XT2 = XTS[1]
        LBb = [SC.bf(4, 2, 128), SC.bf(4, 2, 128)]
        VALL = SC.bf(8, 4, 2, 128)
        SALL = SC.bf(8, 2, NK + 2)
        GB = SC.bf(2, T)
        _xxf = XX.rearrange("p a b c -> p (a b c)")
        YF = _xxf[:, 0:512]
        X2 = _xxf[:, 512:1024]
        MEMSET("pool", ZEROB, 0.0)
        pieces_f = [(0, 64, 0), (64, 256, 256), (320, 256, 1280)]
        pieces_b = [(0, 256, 256), (256, 256, 1280), (512, 64, 0)]
        tok_pieces = [(0, 64, 0, 512), (256, 256, 64, 0), (1280, 256, 320, 256)]
        zi = 0
        oi = 0
        for ct in range(2):
            MEMSET("pool", SALL[:, :, :, 0:1], 0.0)
            MEMSET("pool", SALL[:, :, :, NK + 1:NK + 2], 0.0)
            MM(PSB[3][:, 0:256], IDB, ZEROB, True, False, sgc=True)
            MM(PSB[3][:, 256:512], IDB, ZEROB, False, False, sgc=True)
            MM(PSB[4][:, 0:256], IDB, ZEROB, True, False, sgc=True)
            MM(PSB[4][:, 256:512], IDB, ZEROB, False, False, sgc=True)
            qlist = [(dr_, st_) for dr_ in range(2) for st_ in range(4 * ct, 4 * ct + 4)]
            def ssm_prep(dr, st, q, ql, LB):
                DMA("sp", BRAW, ssm_B[l, q])
                DMA("sp", CRAW, ssm_C[l, q])
                cfr = CFR[:, q:q + 1]
                cfi = CFI[:, q:q + 1]
                ACT(XT1, BRAW[:, 1], AF.Copy, scale=cfi)
                STT("dve", XX[:, 0, 0], BRAW[:, 0], cfr, XT1, ALU.mult, ALU.subtract)
                ACT(XT2, BRAW[:, 0], AF.Copy, scale=cfi)
                STT("dve", XX[:, 0, 1], BRAW[:, 1], cfr, XT2, ALU.mult, ALU.add)
                for tau in range(1, 4):
                    pr = PR[:, tau, q:q + 1]
                    pi_ = PI[:, tau, q:q + 1]
                    xa = XTS[(2 * tau) % 4]
                    xb = XTS[(2 * tau + 1) % 4]
                    ACT(xa, XX[:, 0, 1], AF.Copy, scale=pi_)
                    STT("dve", XX[:, tau, 0], XX[:, 0, 0], pr, xa, ALU.mult, ALU.subtract)
                    ACT(xb, XX[:, 0, 0], AF.Copy, scale=pi_)
                    STT("dve", XX[:, tau, 1], XX[:, 0, 1], pr, xb, ALU.mult, ALU.add)
                lbf = LB.rearrange("p a b c -> p (a b c)")
                for rnd in range(2):
                    for i4 in range(4):
                        i8 = rnd * 4 + i4
                        TR(PSB[2][:, i4 * 128:(i4 + 1) * 128], XX[:, i8 // 2, i8 % 2], IDF)
                    COPY("act", lbf[:, rnd * 512:(rnd + 1) * 512], PSB[2][:, 0:512])
                ACT(NCIM, CRAW[:, 1], AF.Copy, scale=-1.0)
                for r in range(4):
                    e_ = (r + 1) if dr == 0 else (4 - r)
                    pr = PR[:, e_, q:q + 1]
                    pi_ = PI[:, e_, q:q + 1]
                    xa = XTS[(2 * r) % 4]
                    xb = XTS[(2 * r + 1) % 4]
                    ACT(xa, CRAW[:, 1], AF.Copy, scale=pi_)
                    STT("dve", VALL[:, ql, r, 0], CRAW[:, 0], pr, xa, ALU.mult, ALU.subtract)
                    ACT(xb, CRAW[:, 0], AF.Copy, scale=pi_)
                    STT("dve", VALL[:, ql, r, 1], CRAW[:, 1], pr, xb, ALU.mult, ALU.add)
                for tau in range(4):
                    slot = 0 if tau == 0 else (tau if dr == 0 else 3 + tau)
                    bank = PSB[3] if slot < 4 else PSB[4]
                    cs = (slot % 4) * 128
                    lastc = (ql == 7) if (slot == 0 or slot >= 4) else (ql == 3)
                    MM(bank[:, cs:cs + 128], XX[:, tau, 0], CRAW[:, 0], False, False, sgc=True)
                    MM(bank[:, cs:cs + 128], XX[:, tau, 1], NCIM, False, lastc, sgc=True)

            ssm_prep(qlist[0][0], qlist[0][1], qlist[0][0] * 8 + qlist[0][1], qlist[0][0] * 4 + (qlist[0][1] % 4), LBb[0])
            for qi, (dr, st) in enumerate(qlist):
                if True:
                    q = dr * 8 + st
                    ql = dr * 4 + (st % 4)
                    LB = LBb[qi % 2]
                    pcs = (pieces_f if dr == 0 else pieces_b)

                    def zmm(pi3):
                        p0, n, tok0 = pcs[pi3]
                        bank = PSB[pi3 % 2]
                        for ri in range(2):
                            for j in range(4):
                                tau = (3 - j) if dr == 0 else j
                                MM(bank[:, ri * 256:ri * 256 + n], LB[:, tau, ri, :],
                                   UT[:, ct, tok0 + j:tok0 + 4 * n:4], j == 0, j == 3)

                    def zmodul(pi3):
                        p0, n, tok0 = pcs[pi3]
                        bank = PSB[pi3 % 2]
                        if dr == 0:
                            er = ER[:, p0:p0 + n]
                            ei = EI[:, p0:p0 + n]
                        else:
                            er = ER[:, NK - p0 - n:NK - p0][:, ::-1]
                            ei = EI[:, NK - p0 - n:NK - p0][:, ::-1]
                        zv = bank[:, 0:512].rearrange("p (a b) -> p a b", a=2, b=256)[:, :, 0:n]
                        pei = TQP[0][:, :, 0:n]
                        per = TQP[1][:, :, 0:n]
                        TT("dve", pei, zv, ei.unsqueeze(1).broadcast_to([128, 2, n]), ALU.mult)
                        TT("dve", per, zv, er.unsqueeze(1).broadcast_to([128, 2, n]), ALU.mult)
                        TT("dve", MR[:, p0:p0 + n], per[:, 0, :], pei[:, 1, :], ALU.subtract)
                        TT("dve", MI[:, p0:p0 + n], per[:, 1, :], pei[:, 0, :], ALU.add)
                    zmm(0)
                    zmm(1)
                    if qi + 1 < len(qlist):
                        ndr, nst = qlist[qi + 1]
                        ssm_prep(ndr, nst, ndr * 8 + nst, ndr * 4 + (nst % 4), LBb[(qi + 1) % 2])
                    arb = TAR[:, q, :].unsqueeze(2).broadcast_to([128, NT, NT])
                    aib = TAI[:, q, :].unsqueeze(2).broadcast_to([128, NT, NT])
                    brb = TBR[:, q, :].unsqueeze(1).broadcast_to([128, NT, NT])
                    bib = TBI[:, q, :].unsqueeze(1).broadcast_to([128, NT, NT])
                    TT("pool", TM1v, aib, bib, ALU.mult)
                    TT("dve", ERv, arb, brb, ALU.mult)
                    TT("pool", TM2v, aib, brb, ALU.mult)
                    TT("dve", EIv, arb, bib, ALU.mult)
                    TT("dve", ERv, ERv, TM1v, ALU.subtract)
                    TT("dve", EIv, EIv, TM2v, ALU.add)
                    zmodul(0)
                    zmm(2)
                    zmodul(1)
                    zmodul(2)
                    rr = R4[:, q:q + 1].broadcast_to([128, NK])
                    for Mx in (MR, MI):
                        if dr == 0:
                            S.op("dve", (lambda Mx, rr: lambda e: e.tensor_tensor_scan(
                                out=Mx, data0=rr, data1=Mx, initial=0.0, op0=ALU.mult, op1=ALU.add))(Mx, rr),
                                reads=[Mx, R4[:, q:q + 1]], writes=[Mx])
                        else:
                            S.op("dve", (lambda Mx, rr: lambda e: e.tensor_tensor_scan(
                                out=Mx[:, ::-1], data0=rr, data1=Mx[:, ::-1], initial=0.0, op0=ALU.mult, op1=ALU.add))(Mx, rr),
                                reads=[Mx, R4[:, q:q + 1]], writes=[Mx])
                    erf = ER if dr == 0 else ER[:, ::-1]
                    eif = EI if dr == 0 else EI[:, ::-1]
                    t0, t1 = TQ2
                    TT("dve", t0, MR, erf, ALU.mult)
                    TT("dve", t1, MI, eif, ALU.mult)
                    TT("dve", SALL[:, ql, 0, 1:1 + NK], t0, t1, ALU.add)
                    TT("dve", t0, MR, eif, ALU.mult)
                    TT("dve", t1, MI, erf, ALU.mult)
                    TT("dve", SALL[:, ql, 1, 1:1 + NK], t0, t1, ALU.subtract)
                    if ATT_I[0] < NTT:
                        attn_slice(ATT_I[0])
                        ATT_I[0] += 1
            tmf = TMAT.rearrange("p a b -> p (a b)")
            COPY("act", tmf[:, 0:512], PSB[3][:, 0:512])
            COPY("act", tmf[:, 512:896], PSB[4][:, 0:384])
            for r in range(4):
                for (tok0, n, pf0, pb0) in tok_pieces:
                    bank = PSB[oi % 2]
                    oi += 1
                    mms = []
                    for j in range(4):
                        slot = 0 if j == r else ((r - j) if j < r else 3 + (j - r))
                        mms.append((TMAT[:, slot, :], UT[:, ct, tok0 + j:tok0 + 4 * n:4]))
                    for ql in range(8):
                        c0_ = pf0 if ql < 4 else pb0 + 2
                        mms.append((VALL[:, ql, r, 0, :], SALL[:, ql, 0, c0_:c0_ + n]))
                        mms.append((VALL[:, ql, r, 1, :], SALL[:, ql, 1, c0_:c0_ + n]))
                    for i_, (w_, x_) in enumerate(mms):
                        MM(bank[:, 0:n], w_, x_, i_ == 0, i_ == len(mms) - 1)
                    STT("dve", GF[:, ct, tok0 + r:tok0 + 4 * n:4], UT[:, ct, tok0 + r:tok0 + 4 * n:4],
                        SDG[:, ct, 0:1], bank[:, 0:n], ALU.mult, ALU.add)
            for ci, (c0, w, kind) in enumerate(CH):
                yf = GF[:, ct, c0:c0 + w]
                x2 = X2[:, 0:w]
                ACT(x2, yf, AF.Square)
                TS("dve", x2, x2, 0.044715, ALU.mult, 1.0, ALU.add)
                TT("dve", x2, x2, yf, ALU.mult)
                ACT(x2, x2, AF.Sigmoid, scale=1.5957691216057308)
                TT("dve", yf, yf, x2, ALU.mult)
                COPY("pool", GB[:, ct, c0:c0 + w], yf)
        for ot in range(2):
            for (c0, w, kind) in CH:
                ps = PSB[(ot % 2)]
                for ct in range(2):
                    MM(ps[:, 0:w], GLUW[:, ct, ot * 128:(ot + 1) * 128], GB[:, ct, c0:c0 + w], ct == 0, ct == 1)
                sg = YF[:, 0:w]
                ACT(sg, ps[:, 0:w], AF.Sigmoid, bias=SDG[:, ot, 1:2])
                TT("dve", MIX[:, 2 + ot, c0:c0 + w], GF[:, ot, c0:c0 + w], sg, ALU.mult)
        if dbg and l == 0:
            dbg_dump("mix_ps", MIX[:, 0:4], [128, 4, T])
        if stop == "ssm":
            return

        while ATT_I[0] < NTT:
            attn_slice(ATT_I[0])
            ATT_I[0] += 1
        if dbg and l == 0:
            dbg_dump("mix", MIX, [128, 8, T])
        if stop == "attn":
            return

        PP.reset()
        SC.cur = att_end
        H2 = PP.bf(8, T)
        WO = SC.bf(8, D)
        XCb = [SC.f32(8, 512), SC.f32(8, 512)]
        SQ = SC.bf(8, 512)
        RS = SC.f32(512)
        HN = [SC.f32(512), SC.f32(512)]
        DMA("pool", WO, w_out[l].rearrange("(kt p) c -> p kt c", p=128))
        DMA("sp", XCb[0][:, :, 0:CH[0][1]], xsv[:, :, CH[0][0]:CH[0][0] + CH[0][1]])
        nxt_mod = (l + 1 < n_layers)
        if nxt_mod:
            mbn = mod_alloc()
            mod_load(l + 1, 0, mbn)
        mod_sched = [3, 3, 2, 2, 2]
        mod_cc = 0
        for ci, (c0, w, kind) in enumerate(CH):
            XC = XCb[ci % 2]
            if ci + 1 < len(CH):
                n0, nw, _ = CH[ci + 1]
                DMA("sp", XCb[(ci + 1) % 2][:, :, 0:nw], xsv[:, :, n0:n0 + nw])
            for j in range(8):
                ps = nextps()
                for k in range(8):
                    MM(ps[:, 0:w], WO[:, k, j * 128:(j + 1) * 128], MIX[:, k, c0:c0 + w], k == 0, k == 7)
                STT("dve", XC[:, j, 0:w], ps[:, 0:w], modv(l, 2, j, kind), XC[:, j, 0:w], ALU.mult, ALU.add)
            DMA("pool", xdv[:, :, c0:c0 + w], XC[:, :, 0:w])
            if nxt_mod:
                for _ in range(mod_sched[ci]):
                    mod_chunk(l + 1, mod_cc, mbn)
                    mod_cc += 1
            rmsnorm_chunk(XC, w, GE[:, l, 1], 3, l, kind, H2[:, :, c0:c0 + w], SQ, RS, HN)
        if dbg and l == 0:
            dbg_dump("h2", H2, [128, 8, T])
        if stop == "wout":
            return

        SC.reset()
        ACTF = SC.bf(NF, T)
        WGb = [SC.bf(8, 128), SC.bf(8, 128)]
        WUb = [SC.bf(8, 128), SC.bf(8, 128)]
        SGb = [SC.f32(512), SC.f32(512)]
        X2b = [SC.f32(T), SC.f32(T)]
        WDb = [PP.bf(NF, 128), PP.bf(NF, 128)]
        wgv = w_gate[l].rearrange("(kt p) c -> p kt c", p=128)
        wuv = w_up[l].rearrange("(kt p) c -> p kt c", p=128)
        wdv = w_down[l].rearrange("(f p) c -> p f c", p=128)
        chs = CH[1:] if last else CH
        DMA("pool", WGb[0], wgv[:, :, 0:128])
        DMA("pool", WUb[0], wuv[:, :, 0:128])
        k2 = 0
        for f in range(NF):
            if f + 1 < NF:
                DMA("pool", WGb[(f + 1) % 2], wgv[:, :, (f + 1) * 128:(f + 2) * 128])
                DMA("pool", WUb[(f + 1) % 2], wuv[:, :, (f + 1) * 128:(f + 2) * 128])
            WG, WU = WGb[f % 2], WUb[f % 2]
            for (c0, w, kind) in chs:
                pg = PSB[(2 * k2) % 8]
                pu = PSB[(2 * k2 + 1) % 8]
                sg = SGb[k2 % 2]
                k2 += 1
                for k in range(8):
                    MM(pg[:, 0:w], WG[:, k, :], H2[:, k, c0:c0 + w], k == 0, k == 7)
                for k in range(8):
                    MM(pu[:, 0:w], WU[:, k, :], H2[:, k, c0:c0 + w], k == 0, k == 7)
                ACT(sg[:, 0:w], pg[:, 0:w], AF.Silu)
                TT("dve", ACTF[:, f, c0:c0 + w], sg[:, 0:w], pu[:, 0:w], ALU.mult)
        DMA("pool", WDb[0], wdv[:, :, 0:128])
        DMA("sp", X2b[0], Xd[0:128, :])
        k2 = 0
        for j in range(8):
            if j + 1 < 8:
                DMA("pool", WDb[(j + 1) % 2], wdv[:, :, (j + 1) * 128:(j + 2) * 128])
                DMA("sp", X2b[(j + 1) % 2], Xd[(j + 1) * 128:(j + 2) * 128, :])
            WD = WDb[j % 2]
            x2 = X2b[j % 2]
            for (c0, w, kind) in chs:
                ps = PSB[k2 % 8]
                k2 += 1
                for f in range(NF):
                    MM(ps[:, 0:w], WD[:, f, :], ACTF[:, f, c0:c0 + w], f == 0, f == NF - 1)
                STT("dve", x2[:, c0:c0 + w], ps[:, 0:w], modv(l, 5, j, kind), x2[:, c0:c0 + w], ALU.mult, ALU.add)
            if last:
                DMA("pool", outT[j * 128:(j + 1) * 128, :], x2[:, LC:T])
            else:
                DMA("pool", Xd[j * 128:(j + 1) * 128, :], x2)

    for l in range(n_layers):
        if stop in ('pro', 'consts'):
            break
        layer_body(l)
    S.emit()
    return nc, dbg_out


def _host_consts():
    f32 = np.float32
    ident = np.eye(128, dtype=f32)
    pswap = np.zeros((128, 128), f32)
    for m in range(128):
        hb, d = divmod(m, 64)
        pswap[hb * 64 + _pi_perm(d), m] = 1.0
    blk64 = np.zeros((128, 128), f32)
    blk64[:64, :64] = 1.0 / 64
    blk64[64:, 64:] = 1.0 / 64
    t = np.arange(SEQ)
    row = (t // 64).astype(f32)
    col = (t % 64).astype(f32)
    inv = np.power(f32(10000.0), -np.arange(16, dtype=f32) / f32(16)).astype(f32)
    rope_c = np.zeros((128, SEQ), f32)
    rope_s = np.zeros((128, SEQ), f32)
    for p in range(128):
        d = p % 64
        fidx = d % 16
        pos = row if d < 32 else col
        ang = (pos * inv[fidx]).astype(f32)
        rope_c[p] = np.cos(ang)
        s = np.sin(ang)
        rope_s[p] = -s if (d % 32) < 16 else s
    ik = np.arange(128)[:, None]
    iq = np.arange(128)[None, :]
    mL = np.where(ik < iq, NEG, 0.0).astype(f32)
    mR = np.where(ik > iq, NEG, 0.0).astype(f32)
    maskd = np.zeros((128, 2, 256), f32)
    maskd[:, 0, :128] = mL
    maskd[:, 0, 128:] = mL
    maskd[:, 1, :128] = mR
    maskd[:, 1, 128:] = mR
    TT_ = 512
    band = np.zeros((128, 5, 2, 256), f32)
    for gi, wdw in enumerate((2, 4, 8, 16)):
        tt = np.arange(TT_)
        lo = np.clip(tt - wdw // 2, 0, TT_)
        hi = np.clip(tt + wdw // 2, 0, TT_)
        M = np.zeros((TT_, TT_), np.float64)
        for i in range(TT_):
            M[i, lo[i]:hi[i]] = 1.0 / (hi[i] - lo[i])
            M[i, i] -= 1.0
        MT = M.T
        gp, gl = divmod(gi, 2)
        sl = slice(gl * 128, (gl + 1) * 128)
        band[:, 0, gp, sl] = MT[0:128, 0:128]
        band[:, 1, gp, sl] = MT[128:256, 128:256]
        band[:, 2, gp, sl] = MT[384:512, 384:512]
        band[:, 3, gp, sl] = MT[128:256, 256:384]
        band[:, 4, gp, sl] = MT[256:384, 128:256]
    return dict(ident=ident, pswap=pswap, blk64=blk64, rope_c=rope_c, rope_s=rope_s, maskd=maskd, band=band)


def _host_layout(inp):
    f32 = np.float32
    g = lambda k: np.asarray(inp[k], dtype=f32)
    sh = {}
    sh["w_mod"] = np.ascontiguousarray(g("w_mod"))
    sh["b_modp"] = np.ascontiguousarray(g("b_mod").reshape(NL, 6, 8, 128).transpose(0, 3, 1, 2).reshape(NL, 128, 48))
    nm = g("norm_mix").reshape(NL, 8, 128).transpose(0, 2, 1)
    nf = g("norm_ffn").reshape(NL, 8, 128).transpose(0, 2, 1)
    sh["normp"] = np.ascontiguousarray(np.stack([nm, nf], axis=2))
    wi = g("w_in")
    k0 = wi[:, :, 1024:1088]
    k1 = wi[:, :, 1088:1152]
    sh["w_inp"] = np.ascontiguousarray(np.concatenate(
        [wi[:, :, 0:256], wi[:, :, 1152:1280], wi[:, :, 256:512], wi[:, :, 512:1024], k0, k0, k1, k1], axis=2))
    sh["w_out"] = np.ascontiguousarray(g("w_out"))
    pw = g("pool_w")
    pbd = np.zeros((NL, 128, 2, 128), f32)
    for gp in range(2):
        for gl in range(2):
            pbd[:, gl * 64:(gl + 1) * 64, gp, gl * 64:(gl + 1) * 64] = pw[:, 2 * gp + gl]
    sh["poolwbd"] = pbd
    sh["poolsc"] = np.ascontiguousarray(g("pool_scale").reshape(NL, 2, 128).transpose(0, 2, 1))
    def pq(a):
        a = a.reshape(NL, 2, 8, 2, 64)
        return a.transpose(0, 3, 4, 1, 2).reshape(NL, 128, 16)
    are = pq(g("ssm_a_re"))
    aim = pq(g("ssm_a_im"))
    ldt = pq(np.repeat(g("ssm_log_dt")[:, :, :, None], 64, axis=3))
    sh["ssm_small"] = np.ascontiguousarray(np.stack([are, aim, ldt], axis=2))
    bre, bim = g("ssm_b_re"), g("ssm_b_im")
    cre, cim = g("ssm_c_re"), g("ssm_c_im")
    sB = np.zeros((NL, 16, 128, 2, 128), f32)
    sC = np.zeros((NL, 16, 128, 2, 128), f32)
    for dr in range(2):
        for st in range(8):
            q = dr * 8 + st
            for gg in range(2):
                gi = 2 * st + gg
                c0 = 32 * (st % 4) + 16 * gg
                sB[:, q, gg * 64:(gg + 1) * 64, 0, c0:c0 + 16] = bre[:, dr, gi]
                sB[:, q, gg * 64:(gg + 1) * 64, 1, c0:c0 + 16] = bim[:, dr, gi]
                sC[:, q, gg * 64:(gg + 1) * 64, 0, c0:c0 + 16] = cre[:, dr, gi].transpose(0, 2, 1)
                sC[:, q, gg * 64:(gg + 1) * 64, 1, c0:c0 + 16] = cim[:, dr, gi].transpose(0, 2, 1)
    sh["ssm_B"] = sB
    sh["ssm_C"] = sC
    dsk = g("ssm_d").reshape(NL, 2, 128).transpose(0, 2, 1)
    glb = g("ssm_glu_b").reshape(NL, 2, 128).transpose(0, 2, 1)
    sh["ssm_dg"] = np.ascontiguousarray(np.stack([dsk, glb], axis=3))
    sh["glu_w"] = np.ascontiguousarray(g("ssm_glu_w"))
    qn, kn = g("q_norm"), g("k_norm")
    idx = np.arange(128) % 64
    pidx = np.array([_pi_perm(d) for d in idx])
    sh["qkn"] = np.ascontiguousarray(np.stack([qn[:, idx], qn[:, pidx], kn[:, idx], kn[:, pidx]], axis=2))
    sh["sinkr"] = np.ascontiguousarray(np.repeat(g("attn_sink")[:, None, :], 128, axis=1))
    sh["w_gate"] = np.ascontiguousarray(g("ffn_w_gate"))
    sh["w_up"] = np.ascontiguousarray(g("ffn_w_up"))
    sh["w_down"] = np.ascontiguousarray(g("ffn_w_down"))
    sh.update(_host_consts())
    x, c, ctx, c_ctx = g("x"), g("c"), g("ctx"), g("c_ctx")
    maps = []
    for b in range(NB):
        m = dict(sh)
        m["xT"] = np.ascontiguousarray(np.concatenate([ctx[b].T, x[b].T], axis=1))
        cv = np.stack([c[b].reshape(8, 128).T, c_ctx.reshape(8, 128).T], axis=2)
        m["cvec"] = np.ascontiguousarray(cv)
        maps.append(m)
    return maps


_PROG = {}


def kernel(**inputs):
    maps = _host_layout(inputs)
    if "nc" not in _PROG:
        _PROG["nc"] = build_program()[0]
    nc = _PROG["nc"]
    res = run_bass_kernel_spmd(nc, maps, core_ids=list(range(NB)))
    out = np.stack([np.ascontiguousarray(r["outT"].T) for r in res.results], axis=0)
    return out.astype(np.float32)
```

```python
import math
import numpy as np
SK = ()
import concourse.bass as bass
import concourse.mybir as mybir
from concourse.bass_utils import run_bass_kernel_spmd

F32 = mybir.dt.float32
BF = mybir.dt.bfloat16
AF = mybir.ActivationFunctionType
ALU = mybir.AluOpType

ENGS = ("pe", "act", "dve", "pool", "sp")


class Foot:
    __slots__ = ("name", "p0", "p1", "iv")

    def __init__(self, ap):
        self.name = ap.tensor.name
        es = mybir.dt.size(ap.dtype)
        pairs = [tuple(x) for x in ap.ap]
        off = ap.offset
        if str(ap.space) == "DRAM":
            self.p0, self.p1 = 0, 1
            dims = pairs
            base = off
        else:
            pstep, pcnt = pairs[0]
            if pstep == 0:
                self.p0 = 0
                base = off
            else:
                self.p0 = off // pstep
                base = off - self.p0 * pstep
            self.p1 = self.p0 + pcnt
            dims = pairs[1:]
        ivs = [(base, base + 1)]
        for step, cnt in reversed(dims):
            if cnt == 1 or step == 0:
                continue
            if len(ivs) == 1 and abs(step) == ivs[0][1] - ivs[0][0]:
                lo, hi = ivs[0]
                if step > 0:
                    ivs = [(lo, lo + step * cnt)]
                else:
                    ivs = [(lo + step * (cnt - 1), hi)]
                continue
            if len(ivs) * cnt > 48:
                lo = min(a for a, _ in ivs)
                hi = max(b for _, b in ivs)
                ext = step * (cnt - 1)
                ivs = [(lo + min(0, ext), hi + max(0, ext))]
                continue
            new = []
            for k in range(cnt):
                for a, b in ivs:
                    new.append((a + k * step, b + k * step))
            new.sort()
            ivs = new
        ivs.sort()
        self.iv = tuple((a * es, b * es) for a, b in ivs)

    def overlaps(self, o):
        if self.p1 <= o.p0 or o.p1 <= self.p0:
            return False
        x, y = self.iv, o.iv
        if len(x) == 1 and len(y) == 1:
            return x[0][0] < y[0][1] and y[0][0] < x[0][1]
        i = j = 0
        while i < len(x) and j < len(y):
            a, b = x[i]
            c, d = y[j]
            if a < d and c < b:
                return True
            if b <= d:
                i += 1
            else:
                j += 1
        return False

    def covers(self, o):
        if not (self.p0 <= o.p0 and o.p1 <= self.p1):
            return False
        if len(self.iv) != 1:
            return False
        a, b = self.iv[0]
        return all(a <= c and d <= b for c, d in o.iv)


class Op:
    __slots__ = ("eng", "fn", "deps", "dma", "sem", "semval", "incd", "count", "seq")
    SEQ = 0

    def __init__(self, eng, fn, dma):
        self.eng, self.fn, self.dma = eng, fn, dma
        Op.SEQ += 1
        self.seq = Op.SEQ
        self.deps = []
        self.sem = None
        self.semval = 0
        self.incd = False
        self.count = 0


class Sched:
    def __init__(self, nc, n_dma_sems=56):
        self.nc = nc
        self.ops = {e: [] for e in ENGS}
        self.recs = {}
        self.n_dma_sems = n_dma_sems

    def op(self, eng, fn, reads=(), writes=(), dma=False):
        o = Op(eng, fn, dma)
        self.ops[eng].append(o)
        rf = [Foot(a) for a in reads if a is not None and not isinstance(a, (int, float))]
        wf = [Foot(a) for a in writes if a is not None]
        deps = {}
        psum_names = set()
        for f in rf:
            if f.name.startswith("psb"):
                psum_names.add(f.name)
                continue
            for rec in self.recs.get(f.name, ()):
                if rec[1] == "w" and rec[0].overlaps(f):
                    deps[rec[2]] = True
        for f in wf:
            if f.name.startswith("psb"):
                psum_names.add(f.name)
                continue
            for rec in self.recs.get(f.name, ()):
                if rec[0].overlaps(f):
                    deps.setdefault(rec[2], False)
        for nm in psum_names:
            for rec in self.recs.get(nm, ()):
                if rec[2].eng != eng:
                    deps[rec[2]] = True
        for d, israw in deps.items():
            if d is o:
                continue
            if d.eng == o.eng and not d.dma and not o.dma:
                if o.eng == "pe":
                    continue
            o.deps.append(d)
        for nm in psum_names:
            self.recs[nm] = [(None, "w", o)]
        for f in wf:
            if f.name in psum_names:
                continue
            lst = self.recs.setdefault(f.name, [])
            lst[:] = [r for r in lst if not f.covers(r[0])]
            lst.append((f, "w", o))
        for f in rf:
            if f.name in psum_names:
                continue
            self.recs.setdefault(f.name, []).append((f, "r", o))
        return o

    def emit(self):
        nc = self.nc
        from contextlib import ExitStack
        for e in ENGS:
            for o in self.ops[e]:
                for d in o.deps:
                    d.incd = True
        for e in ENGS:
            c = 0
            for o in self.ops[e]:
                if not o.dma and o.incd:
                    c += 1
                    o.count = c
        with ExitStack() as st:
            csem = {e: st.enter_context(nc.semaphore("c_" + e)) for e in ENGS}
            dsems = [st.enter_context(nc.semaphore("d%d" % i)) for i in range(self.n_dma_sems)]
            alldma = [o for e in ENGS for o in self.ops[e] if o.dma]
            alldma.sort(key=lambda o: o.seq)
            half = self.n_dma_sems // 2
            use = [0] * self.n_dma_sems
            prev = [None] * self.n_dma_sems
            cnt = {"hw": 0, "sw": 0}
            for o in alldma:
                kind = "sw" if o.eng == "pool" else "hw"
                s = (cnt[kind] % half) + (half if kind == "sw" else 0)
                cnt[kind] += 1
                use[s] += 1
                o.sem = s
                o.semval = 16 * use[s]
                if prev[s] is not None:
                    o.deps.append(prev[s])
                prev[s] = o
            blk = st.enter_context(nc.Block())

            def run(ename, eng):
                waited_c = {e: 0 for e in ENGS}
                waited_d = {}
                for o in self.ops[ename]:
                    for d in o.deps:
                        if d.dma:
                            if waited_d.get(d.sem, 0) < d.semval:
                                eng.wait_ge(dsems[d.sem], d.semval)
                                waited_d[d.sem] = d.semval
                        elif waited_c[d.eng] < d.count:
                            eng.wait_ge(csem[d.eng], d.count)
                            waited_c[d.eng] = d.count
                    ins = o.fn(eng)
                    if o.dma:
                        ins.then_inc(dsems[o.sem], 16)
                    elif o.incd:
                        ins.then_inc(csem[ename], 1)
                if ename == "sp":
                    last = {}
                    for o in alldma:
                        last[o.sem] = max(last.get(o.sem, 0), o.semval)
                    for s, v in last.items():
                        eng.wait_ge(dsems[s], v)

            blk.tensor(lambda e: run("pe", e))
            blk.scalar(lambda e: run("act", e))
            blk.vector(lambda e: run("dve", e))
            blk.gpsimd(lambda e: run("pool", e))
            blk.sync(lambda e: run("sp", e))


D = 1024
NB = 8
SEQ = 2048
LC = 256
T = SEQ + LC
NL = 4
DFF = 2816
NF = DFF // 128
EPS = 1e-6
CH = [(0, 256, 1), (256, 512, 0), (768, 512, 0), (1280, 512, 0), (1792, 512, 0)]
NTT = T // 128
C_POOL, C_V, C_SSM, C_Q, C_K, NCOL = 0, 256, 384, 640, 1152, 1408
TWO_PI_HI = 6.28125
TWO_PI_LO = 2.0 * math.pi - 6.28125
NEG = -30000.0


def _pi_perm(d):
    r = d % 32
    return d + 16 if r < 16 else d - 16


def build_program(n_layers=NL, dbg=False, stop=None):
    nc = bass.Bass("TRN2", target_bir_lowering=False)
    S = Sched(nc)

    def din(name, shape):
        return nc.dram_tensor(name, list(shape), F32, kind="ExternalInput").ap()

    xT = din("xT", [D, T])
    cvec = din("cvec", [128, 8, 2])
    w_mod = din("w_mod", [NL, D, 6 * D])
    b_modp = din("b_modp", [NL, 128, 48])
    normp = din("normp", [NL, 128, 2, 8])
    w_inp = din("w_inp", [NL, D, NCOL])
    w_out = din("w_out", [NL, D, D])
    poolwbd = din("poolwbd", [NL, 128, 2, 128])
    poolsc = din("poolsc", [NL, 128, 2])
    band = din("band", [128, 5, 2, 256])
    ssm_small = din("ssm_small", [NL, 128, 3, 16])
    ssm_B = din("ssm_B", [NL, 16, 128, 2, 128])
    ssm_C = din("ssm_C", [NL, 16, 128, 2, 128])
    ssm_dg = din("ssm_dg", [NL, 128, 2, 2])
    glu_w = din("glu_w", [NL, 256, 256])
    qkn = din("qkn", [NL, 128, 4])
    sinkr = din("sinkr", [NL, 128, 8])
    w_gate = din("w_gate", [NL, D, DFF])
    w_up = din("w_up", [NL, D, DFF])
    w_down = din("w_down", [NL, DFF, D])
    ident = din("ident", [128, 128])
    pswap = din("pswap", [128, 128])
    blk64 = din("blk64", [128, 128])
    rope_c = din("rope_c", [128, SEQ])
    rope_s = din("rope_s", [128, SEQ])
    maskd = din("maskd", [128, 2, 256])

    outT = nc.dram_tensor("outT", [D, SEQ], F32, kind="ExternalOutput").ap()
    Xd = nc.dram_tensor("xscratch", [D, T], F32).ap()
    dbg_out = {}

    AW = 53200
    a32 = nc.alloc_sbuf_tensor("arena", [128, AW], F32)
    a16 = a32.bitcast(BF)
    PSB = [nc.alloc_psum_tensor("psb%d" % i, [128, 512], F32) for i in range(8)]
    PSB16 = [p.bitcast(BF) for p in PSB]

    def _shape(v, shape):
        if len(shape) == 1:
            return v
        if len(shape) == 2:
            return v.rearrange("p (a b) -> p a b", a=shape[0], b=shape[1])
        if len(shape) == 3:
            return v.rearrange("p (a b c) -> p a b c", a=shape[0], b=shape[1], c=shape[2])
        return v.rearrange("p (a b c d) -> p a b c d", a=shape[0], b=shape[1], c=shape[2], d=shape[3])

    class Bump:
        def __init__(self, lo, hi):
            self.lo, self.hi, self.cur = lo, hi, lo

        def reset(self):
            self.cur = self.lo

        def take(self, nbytes):
            nbytes = (nbytes + 31) // 32 * 32
            o = self.cur
            self.cur += nbytes
            assert self.cur <= self.hi, ("arena overflow", self.cur, self.hi)
            return o

        def f32(self, *shape):
            n = int(np.prod(shape))
            o = self.take(4 * n)
            return _shape(a32[:, o // 4:o // 4 + n], shape)

        def bf(self, *shape):
            n = int(np.prod(shape))
            o = self.take(2 * n)
            return _shape(a16[:, o // 2:o // 2 + n], shape)

    CONST_B = 14 * 1024
    PROJ_B = 52 * 1024
    PC = Bump(0, CONST_B)
    PP = Bump(CONST_B, CONST_B + PROJ_B)
    SC = Bump(CONST_B + PROJ_B, AW * 4)

    def isap(x):
        return x is not None and not isinstance(x, (int, float))

    def DMA(q, out, in_):
        S.op(q, lambda e: e.dma_start(out=out, in_=in_), reads=[in_], writes=[out], dma=True)

    def TT(eng, out, a, b, op):
        S.op(eng, lambda e: e.tensor_tensor(out=out, in0=a, in1=b, op=op), reads=[a, b], writes=[out])

    def TS(eng, out, a, s1, op0, s2=None, op1=None):
        if op1 is None:
            S.op(eng, lambda e: e.tensor_scalar(out=out, in0=a, scalar1=s1, scalar2=None, op0=op0),
                 reads=[a, s1], writes=[out])
        else:
            S.op(eng, lambda e: e.tensor_scalar(out=out, in0=a, scalar1=s1, scalar2=s2, op0=op0, op1=op1),
                 reads=[a, s1, s2], writes=[out])

    def STT(eng, out, a, sc, b, op0, op1):
        S.op(eng, lambda e: e.scalar_tensor_tensor(out=out, in0=a, scalar=sc, in1=b, op0=op0, op1=op1),
             reads=[a, sc, b], writes=[out])

    def ACT(out, in_, func, bias=None, scale=None):
        kw = {}
        if bias is not None:
            kw["bias"] = bias
        if scale is not None:
            kw["scale"] = scale
        S.op("act", lambda e: e.activation(out=out, in_=in_, func=func, **kw), reads=[in_, bias, scale], writes=[out])

    def COPY(eng, out, in_):
        if eng == "act":
            S.op("act", lambda e: e.activation(out=out, in_=in_, func=AF.Copy), reads=[in_], writes=[out])
        else:
            S.op(eng, lambda e: e.tensor_copy(out=out, in_=in_), reads=[in_], writes=[out])

    def MM(out, lhsT, rhs, start, stop, sgc=False):
        S.op("pe", lambda e: e.matmul(out, lhsT=lhsT, rhs=rhs, start=start, stop=stop, skip_group_check=sgc),
             reads=[lhsT, rhs] + ([] if start else [out]), writes=[out])

    def TR(out, in_, idn):
        S.op("pe", lambda e: e.transpose(out, in_, idn), reads=[in_, idn], writes=[out])

    def MEMSET(eng, out, val):
        S.op(eng, lambda e: e.memset(out, val), writes=[out])

    def RECIP(out, in_):
        S.op("dve", lambda e: e.reciprocal(out=out, in_=in_), reads=[in_], writes=[out])

    def dbg_dump(name, ap, shape):
        if not dbg:
            return
        n = int(np.prod(shape[1:]))
        o = nc.dram_tensor("dbg_" + name, [128, n], F32, kind="ExternalOutput").ap()
        dbg_out[name] = o
        letters = "abcd"[:len(shape) - 1]
        flat = ap if len(shape) == 2 else ap.rearrange("p %s -> p (%s)" % (" ".join(letters), " ".join(letters)))
        for i in range(0, n, 1024):
            j = min(n, i + 1024)
            DMA("pool", o[:, i:j], flat[:, i:j])

    IDF = PC.f32(128)
    IDB = PC.bf(128)
    PSWB = PC.bf(128)
    BLKB = PC.bf(128)
    ONEB = PC.bf(128)
    MASKB = PC.bf(2, 256)
    BANDB = PC.bf(5, 2, 256)
    MOD = PC.f32(NL, 48, 2)
    GE = PC.f32(NL, 2, 8, 2)
    EPSC = PC.f32(1)
    NPI = PC.f32(1)
    QKN = PC.f32(4)
    SINKE = PC.f32(8)
    POOLSC = PC.f32(2)
    SDG = PC.f32(2, 2)
    TMAT = PC.bf(7, 128)
    ZEROB = PC.bf(256)

    stg = SC.f32(5 * 2 * 256)
    DMA("sp", IDF, ident)
    COPY("dve", IDB, IDF)
    DMA("sp", stg[:, 0:128], pswap)
    COPY("dve", PSWB, stg[:, 0:128])
    DMA("sp", stg[:, 128:256], blk64)
    COPY("dve", BLKB, stg[:, 128:256])
    MEMSET("dve", ONEB, 1.0 / 1024.0)
    MEMSET("dve", EPSC, EPS)
    MEMSET("dve", NPI, -math.pi)
    DMA("pool", MASKB, maskd)
    for v_ in range(5):
        DMA("pool", BANDB[:, v_], band[:, v_])

    if stop == 'consts':
        n_layers = 0
    CS = SC.f32(8, 2)
    SILC = SC.f32(8, 2)
    SILB = PC.bf(8, 2)
    BMOD = PC.f32(48)
    NRM = PC.f32(2, 8)
    DMA("sp", CS, cvec)
    ACT(SILC, CS, AF.Silu)
    COPY("dve", SILB, SILC)

    def mod_alloc():
        return dict(wm=[SC.bf(8, 512), SC.bf(8, 512)], row=SC.f32(512))

    def mod_load(l, cc, mb):
        wv = w_mod[l].rearrange("(kt p) c -> p kt c", p=128)
        DMA("pool", mb["wm"][cc % 2], wv[:, :, cc * 512:(cc + 1) * 512])

    def mod_chunk(l, cc, mb):
        if cc == 0:
            DMA("sp", BMOD, b_modp[l])
            DMA("sp", NRM, normp[l])
        if cc + 1 < 12:
            mod_load(l, cc + 1, mb)
        wb = mb["wm"][cc % 2]
        ps = nextps()
        for kt in range(8):
            MM(ps[0:2, :], SILB[:, kt, :], wb[:, kt, :], kt == 0, kt == 7)
        COPY("act", mb["row"][0:2, :], ps[0:2, :])
        pst = nextps()
        for b in range(4):
            MM(pst[:, 2 * b:2 * b + 2], mb["row"][0:2, b * 128:(b + 1) * 128], IDF[0:2, 0:2], True, True)
        TT("dve", MOD[:, l, 4 * cc:4 * cc + 4, :], pst[:, 0:8].rearrange("p (a b) -> p a b", a=4, b=2),
           BMOD[:, 4 * cc:4 * cc + 4].unsqueeze(2).broadcast_to([128, 4, 2]), ALU.add)
        if cc == 11:
            for which in range(2):
                sc = MOD[:, l, (1 + 3 * which) * 8:(2 + 3 * which) * 8, :]
                STT("dve", GE[:, l, which], sc, 1.0, NRM[:, which].unsqueeze(2).broadcast_to([128, 8, 2]),
                    ALU.add, ALU.mult)

    PSI = [0]

    def nextps():
        PSI[0] = (PSI[0] + 1) % 8
        return PSB[PSI[0]]

    if n_layers > 0:
        mb0 = mod_alloc()
        mod_load(0, 0, mb0)
        for cc in range(12):
            mod_chunk(0, cc, mb0)

    def modv(l, m, j, r):
        return MOD[:, l, m * 8 + j, r:r + 1]


    def rmsnorm_chunk(XC, w, ge, shm, l, kind, HOUT, SQ, RS, HN):
        ACT(SQ[:, :, 0:w], XC[:, :, 0:w], AF.Square)
        ps = nextps()
        for j in range(8):
            MM(ps[:, 0:w], ONEB, SQ[:, j, 0:w], j == 0, j == 7)
        ACT(RS[:, 0:w], ps[:, 0:w], AF.Sqrt, bias=EPSC[:, 0:1])
        RECIP(RS[:, 0:w], RS[:, 0:w])
        for j in range(8):
            hn = HN[j % 2]
            STT("dve", hn[:, 0:w], XC[:, j, 0:w], ge[:, j, kind:kind + 1], RS[:, 0:w], ALU.mult, ALU.mult)
            ACT(HOUT[:, j, 0:w], hn[:, 0:w], AF.Identity, bias=modv(l, shm, j, kind))

    def layer_body(l):
        xsrc = xT if l == 0 else Xd
        xsv = xsrc.rearrange("(j p) t -> p j t", p=128)
        xdv = Xd.rearrange("(j p) t -> p j t", p=128)
        last = (l == n_layers - 1)

        PP.reset()
        SC.reset()
        UT = PP.bf(2, T)
        QT = PP.bf(4, T)
        KT = PP.bf(2, T)
        VA = PP.bf(NTT, 2, 66)
        POOLU = PP.bf(NTT, 256)

        WIN = SC.bf(8, NCOL)
        ROPEC = SC.f32(SEQ)
        ROPES = SC.f32(SEQ)
        XCb = [SC.f32(8, 512), SC.f32(8, 512)]
        SQ = SC.bf(8, 512)
        HCb = [SC.bf(8, 512), SC.bf(8, 512)]
        RS = SC.f32(512)
        HN = [SC.f32(512), SC.f32(512)]
        QRAWb = [SC.bf(512), SC.bf(512)]
        QSQb = [SC.bf(512), SC.bf(512)]
        RSQb = [SC.f32(512), SC.f32(512)]
        T1b = [SC.f32(512), SC.f32(512)]
        T2b = [SC.f32(512), SC.f32(512)]

        if "win" not in SK:
            DMA("pool", WIN, w_inp[l].rearrange("(kt p) c -> p kt c", p=128))
        DMA("sp", ROPEC, rope_c)
        DMA("sp", ROPES, rope_s)
        DMA("sp", QKN, qkn[l])
        if "vaones" not in SK:
            MEMSET("pool", VA[:, :, :, 64:66], 1.0)
        DMA("sp", XCb[0][:, :, 0:CH[0][1]], xsv[:, :, CH[0][0]:CH[0][0] + CH[0][1]])
        rmsnorm_chunk(XCb[0], CH[0][1], GE[:, l, 0], 0, l, CH[0][2], HCb[0], SQ, RS, HN)
        if len(CH) > 1:
            DMA("sp", XCb[1][:, :, 0:CH[1][1]], xsv[:, :, CH[1][0]:CH[1][0] + CH[1][1]])
        for ci, (c0, w, kind) in enumerate(CH):
            XC = XCb[ci % 2]
            HC = HCb[ci % 2]
            if ci + 1 < len(CH):
                n0, nw, nk = CH[ci + 1]
                rmsnorm_chunk(XCb[(ci + 1) % 2], nw, GE[:, l, 0], 0, l, nk, HCb[(ci + 1) % 2], SQ, RS, HN)
                if ci + 2 < len(CH):
                    m0, mw, _ = CH[ci + 2]
                    DMA("sp", XCb[ci % 2][:, :, 0:mw], xsv[:, :, m0:m0 + mw])
            for tl in range(0 if "tok" in SK else w // 128):
                tt = c0 // 128 + tl
                ps = nextps()
                for j in range(8):
                    MM(ps[:, 0:384], HC[:, j, tl * 128:(tl + 1) * 128], WIN[:, j, 0:384], j == 0, j == 7)
                if "tokpool" not in SK:
                    COPY("act", POOLU[:, tt, :], ps[:, 0:256])
                if "tokva" not in SK:
                    COPY("dve", VA[:, tt, :, 0:64], ps[:, 256:384].rearrange("p (a b) -> p a b", a=2, b=64))
            def post(m, ps):
                if m < 2:
                    COPY("act", UT[:, m, c0:c0 + w], ps[:, 0:w])
                    return
                isq = m < 6
                QRAW, QSQ, RSQ, T1, T2 = QRAWb[m % 2], QSQb[m % 2], RSQb[m % 2], T1b[m % 2], T2b[m % 2]
                dst = QT[:, m - 2, c0:c0 + w] if isq else KT[:, m - 6, c0:c0 + w]
                g = QKN[:, 0:1] if isq else QKN[:, 2:3]
                gp = QKN[:, 1:2] if isq else QKN[:, 3:4]
                ACT(QSQ[:, 0:w], ps[:, 0:w], AF.Square)
                if kind == 0:
                    COPY("act", QRAW[:, 0:w], ps[:, 0:w])
                ps2 = nextps()
                MM(ps2[:, 0:w], BLKB, QSQ[:, 0:w], True, True)
                if kind == 0:
                    ps3 = nextps()
                    MM(ps3[:, 0:w], PSWB, QRAW[:, 0:w], True, True)
                ACT(RSQ[:, 0:w], ps2[:, 0:w], AF.Sqrt, bias=EPSC[:, 0:1])
                RECIP(RSQ[:, 0:w], RSQ[:, 0:w])
                if kind == 1:
                    STT("dve", dst, ps[:, 0:w], g, RSQ[:, 0:w], ALU.mult, ALU.mult)
                else:
                    tp = c0 - LC
                    STT("dve", T1[:, 0:w], ps[:, 0:w], g, ROPEC[:, tp:tp + w], ALU.mult, ALU.mult)
                    STT("dve", T2[:, 0:w], ps3[:, 0:w], gp, ROPES[:, tp:tp + w], ALU.mult, ALU.mult)
                    TT("dve", T1[:, 0:w], T1[:, 0:w], T2[:, 0:w], ALU.add)
                    TT("dve", dst, T1[:, 0:w], RSQ[:, 0:w], ALU.mult)
            prev = None
            for m in range(8):
                ps = nextps()
                for j in range(8):
                    MM(ps[:, 0:w], WIN[:, j, C_SSM + 128 * m:C_SSM + 128 * (m + 1)], HC[:, j, 0:w], j == 0, j == 7)
                if prev is not None:
                    post(*prev)
                prev = (m, ps)
            post(*prev)
        if dbg and l == 0:
            dbg_dump("ut", UT, [128, 2, T])
            dbg_dump("qt", QT, [128, 4, T])
            dbg_dump("kt", KT, [128, 2, T])
            dbg_dump("va", VA, [128, NTT, 2, 66])
            dbg_dump("poolu", POOLU, [128, NTT, 256])
        if stop == "n1":
            return

        SC.reset()
        MIX = SC.bf(8, T)
        sc_mark = SC.cur
        DTb = SC.bf(2, T)
        PWBD = SC.bf(2, 128)
        DMA("pool", PWBD, poolwbd[l])
        DMA("sp", POOLSC, poolsc[l])
        for gp in range(2):
            for tp in range(NTT // 2):
                ps = nextps()
                psv = ps[:, 0:512].rearrange("p (a b) -> p a b", a=2, b=256)
                for ti in range(2):
                    tt = 2 * tp + ti
                    seg0, seg1 = (0, 1) if tt < 2 else (2, NTT - 1)
                    nbs = []
                    if tt > seg0:
                        nbs.append((tt - 1, 3))
                    nbs.append((tt, 0 if tt == seg0 else (2 if tt == seg1 else 1)))
                    if tt < seg1:
                        nbs.append((tt + 1, 4))
                    for i, (nb, var) in enumerate(nbs):
                        MM(psv[:, ti, :], POOLU[:, nb, gp * 128:(gp + 1) * 128], BANDB[:, var, gp, :],
                           i == 0, i == len(nbs) - 1)
                dv = DTb[:, gp, tp * 256:(tp + 1) * 256].rearrange("p (a b) -> p a b", a=2, b=128)
                COPY("act", dv[0:64], psv[0:64, :, 0:128])
                COPY("dve", dv[64:128], psv[64:128, :, 128:256])
        for gp in range(2):
            for (c0, w, kind) in CH:
                ps = nextps()
                MM(ps[:, 0:w], PWBD[:, gp, :], DTb[:, gp, c0:c0 + w], True, True)
                ACT(MIX[:, gp, c0:c0 + w], ps[:, 0:w], AF.Identity, scale=POOLSC[:, gp:gp + 1])

        if stop == "pool":
            dbg_dump("mix_ps", MIX[:, 0:4], [128, 4, T])
            return
        SC.cur = sc_mark
        SKR = SC.f32(8)
        DMA("sp", SKR, sinkr[l])
        ACT(SINKE, SKR, AF.Exp)
        PT = SC.bf(5, 2, 2, 128)
        ONb = [SC.bf(8, 64), SC.bf(8, 64)]
        DENb = SC.f32(4)
        RECb = SC.f32(4)
        ATT_I = [0]
        att_end = SC.cur

        def attn_slice(qt):
            if qt < 2:
                keys = [(0, None), (1, None)]
            else:
                keys = [(0, None), (1, None)]
                if qt > 2:
                    keys.append((qt - 1, 0))
                keys.append((qt, None))
                if qt < NTT - 1:
                    keys.append((qt + 1, 1))
            ON = ONb[qt % 2]
            for kvh in range(2):
                for ki, (kt, mk) in enumerate(keys):
                    for half in range(2):
                        ps = PSB[5 + half]
                        started = False
                        if mk is not None:
                            MM(ps[:, 0:256], IDB, MASKB[:, mk, :], True, False)
                            started = True
                        for hq in range(2):
                            m = 2 * kvh + hq
                            MM(ps[:, hq * 128:(hq + 1) * 128],
                               KT[64 * half:64 * half + 64, kvh, kt * 128:(kt + 1) * 128],
                               QT[64 * half:64 * half + 64, m, qt * 128:(qt + 1) * 128],
                               not started, hq == 1)
                            started = True
                        ACT(PT[:, ki, half].rearrange("p a b -> p (a b)"), ps[:, 0:256], AF.Exp, scale=0.125)
                po = PSB[7] if kvh == 0 else PSB[2]
                pov = po[:, 0:264].rearrange("p (a b) -> p a b", a=4, b=66)
                for hh in range(4):
                    hq, half = hh // 2, hh % 2
                    for ki, (kt, mk) in enumerate(keys):
                        MM(pov[:, hh, 0:65], PT[:, ki, half, hq, :], VA[:, kt, kvh, 0:65], ki == 0, ki == len(keys) - 1)
                TT("dve", DENb, pov[:, :, 64], SINKE[:, 4 * kvh:4 * kvh + 4], ALU.add)
                RECIP(RECb, DENb)
                TT("dve", ON[:, 4 * kvh:4 * kvh + 4, :], pov[:, :, 0:64], RECb.unsqueeze(2).broadcast_to([128, 4, 64]), ALU.mult)
            pt16 = PSB16[5][:, 512:1024]
            onf = ON.rearrange("p a b -> p (a b)")
            for m in range(4):
                TR(pt16[:, m * 128:(m + 1) * 128], onf[:, m * 128:(m + 1) * 128], IDB)
            COPY("act", MIX[:, 4:8, qt * 128:(qt + 1) * 128], pt16.rearrange("p (a b) -> p a b", a=4, b=128))

        SM = SC.f32(3, 16)
        DMA("sp", SM, ssm_small[l])
        DMA("sp", SDG, ssm_dg[l])
        GLUW = SC.bf(2, 256)
        DMA("pool", GLUW, glu_w[l].rearrange("(ct p) c -> p ct c", p=128))

        def sm16():
            return SC.f32(16)
        (DTt, ARE, ZR, TH, RR, KW, K2, THR, S0, SH, C0, NN, ABR, ABI, NR, DEN, CFR, CFI, TA, TB,
         C1, C2, C3, C4, R2, R4) = [sm16() for _ in range(26)]
        WR = SC.f32(7, 16)
        WI = SC.f32(7, 16)
        VR = SC.f32(5, 16)
        VI = SC.f32(5, 16)
        PR = SC.f32(5, 16)
        PI = SC.f32(5, 16)
        NT = 24
        NK = T // 4
        TAR = SC.f32(16, NT)
        TAI = SC.f32(16, NT)
        TBR = SC.f32(16, NT)
        TBI = SC.f32(16, NT)
        GF = SC.f32(2, T)
        _gff = GF.rearrange("p a b -> p (a b)")
        UU = [_gff[:, i * 512:(i + 1) * 512].rearrange("p (a b) -> p a b", a=16, b=32) for i in range(8)]
        AIM = SM[:, 1]
        ACT(DTt, SM[:, 2], AF.Exp)
        TS("dve", ARE, SM[:, 0], -1e-4, ALU.min)
        TT("dve", ZR, ARE, DTt, ALU.mult)
        TT("dve", TH, AIM, DTt, ALU.mult)
        ACT(RR, ZR, AF.Exp)
        TS("dve", KW, TH, math.pi, ALU.is_gt)
        for mm_ in (3, 5, 7):
            STT("dve", K2, TH, mm_ * math.pi, KW, ALU.is_gt, ALU.add)
            COPY("dve", KW, K2)
        STT("dve", THR, KW, -TWO_PI_HI, TH, ALU.mult, ALU.add)
        STT("dve", TA, KW, -TWO_PI_LO, THR, ALU.mult, ALU.add)
        ACT(S0, TA, AF.Sin)
        ACT(SH, TA, AF.Sin, scale=0.5)
        TT("dve", TB, SH, SH, ALU.mult)
        TS("dve", C0, TB, -2.0, ALU.mult, 1.0, ALU.add)

        def renorm(re, im):
            TT("dve", NN, re, re, ALU.mult)
            TT("dve", TB, im, im, ALU.mult)
            TT("dve", NN, NN, TB, ALU.add)
            TS("dve", TB, NN, -0.5, ALU.mult, 1.5, ALU.add)
            TT("dve", re, re, TB, ALU.mult)
            TT("dve", im, im, TB, ALU.mult)

        def cmul(orr, oi, ar, ai, br, bi):
            TT("dve", C1, ar, br, ALU.mult)
            TT("dve", C2, ai, bi, ALU.mult)
            TT("dve", C3, ar, bi, ALU.mult)
            TT("dve", C4, ai, br, ALU.mult)
            TT("dve", orr, C1, C2, ALU.subtract)
            TT("dve", oi, C3, C4, ALU.add)
        renorm(C0, S0)
        TT("dve", ABR, RR, C0, ALU.mult)
        TT("dve", ABI, RR, S0, ALU.mult)
        TS("dve", NR, ABR, -1.0, ALU.add)
        TT("dve", DEN, ARE, ARE, ALU.mult)
        TT("dve", TB, AIM, AIM, ALU.mult)
        TT("dve", DEN, DEN, TB, ALU.add)
        RECIP(DEN, DEN)
        TT("dve", TA, NR, ARE, ALU.mult)
        TT("dve", TB, ABI, AIM, ALU.mult)
        TT("dve", TA, TA, TB, ALU.add)
        TT("dve", CFR, TA, DEN, ALU.mult)
        TT("dve", TA, ABI, ARE, ALU.mult)
        TT("dve", TB, NR, AIM, ALU.mult)
        TT("dve", TA, TA, TB, ALU.subtract)
        TT("dve", CFI, TA, DEN, ALU.mult)
        MEMSET("dve", PR[:, 0], 1.0)
        MEMSET("dve", PI[:, 0], 0.0)
        COPY("dve", PR[:, 1], ABR)
        COPY("dve", PI[:, 1], ABI)
        cmul(PR[:, 2], PI[:, 2], PR[:, 1], PI[:, 1], PR[:, 1], PI[:, 1])
        cmul(PR[:, 3], PI[:, 3], PR[:, 2], PI[:, 2], PR[:, 1], PI[:, 1])
        cmul(PR[:, 4], PI[:, 4], PR[:, 2], PI[:, 2], PR[:, 2], PI[:, 2])
        TT("dve", R2, RR, RR, ALU.mult)
        TT("dve", R4, R2, R2, ALU.mult)
        COPY("dve", WR[:, 0], C0)
        TS("dve", WI[:, 0], S0, -1.0, ALU.mult)

        def csquare(XR, XI, k):
            TT("dve", TA, XR[:, k], XR[:, k], ALU.mult)
            TT("dve", TB, XI[:, k], XI[:, k], ALU.mult)
            TT("dve", XR[:, k + 1], TA, TB, ALU.subtract)
            STT("dve", XI[:, k + 1], XR[:, k], 2.0, XI[:, k], ALU.mult, ALU.mult)
            renorm(XR[:, k + 1], XI[:, k + 1])
        for k in range(6):
            csquare(WR, WI, k)
        cmul(VR[:, 0], VI[:, 0], WR[:, 6], WI[:, 6], WR[:, 5], WI[:, 5])
        renorm(VR[:, 0], VI[:, 0])
        for k in range(4):
            csquare(VR, VI, k)

        def build_tab(eng, TR_, TI_, XR, XI, U, k0):
            MEMSET(eng, TR_[:, :, 0:1], 1.0)
            MEMSET(eng, TI_[:, :, 0:1], 0.0)
            for k in range(5):
                h = 1 << k
                n = min(h, NT - h)
                wr = XR[:, k0 + k, :].unsqueeze(2).broadcast_to([128, 16, n])
                wi_ = XI[:, k0 + k, :].unsqueeze(2).broadcast_to([128, 16, n])
                TT(eng, U[0][:, :, 0:n], TI_[:, :, 0:n], wi_, ALU.mult)
                TT(eng, U[1][:, :, 0:n], TR_[:, :, 0:n], wr, ALU.mult)
                TT(eng, U[2][:, :, 0:n], TI_[:, :, 0:n], wr, ALU.mult)
                TT(eng, U[3][:, :, 0:n], TR_[:, :, 0:n], wi_, ALU.mult)
                TT(eng, TR_[:, :, h:h + n], U[1][:, :, 0:n], U[0][:, :, 0:n], ALU.subtract)
                TT(eng, TI_[:, :, h:h + n], U[3][:, :, 0:n], U[2][:, :, 0:n], ALU.add)
        build_tab("dve", TBR, TBI, WR, WI, UU[0:4], 2)
        build_tab("pool", TAR, TAI, VR, VI, UU[4:8], 0)

        ER = SC.f32(NK)
        EI = SC.f32(NK)
        MR = SC.f32(NK)
        MI = SC.f32(NK)
        ERv = ER.rearrange("p (a b) -> p a b", a=NT, b=NT)
        EIv = EI.rearrange("p (a b) -> p a b", a=NT, b=NT)
        TM1v = MR.rearrange("p (a b) -> p a b", a=NT, b=NT)
        TM2v = MI.rearrange("p (a b) -> p a b", a=NT, b=NT)
        TQF = SC.f32(1152)
        TQ2 = [TQF[:, 0:NK], TQF[:, NK:2 * NK]]
        TQP = [TQF[:, 0:512].rearrange("p (a b) -> p a b", a=2, b=256), TQF[:, 512:1024].rearrange("p (a b) -> p a b", a=2, b=256)]
        BRAW = SC.f32(2, 128)
        CRAW = SC.f32(2, 128)
        NCIM = SC.f32(128)
        XX = SC.f32(4, 2, 128)
        XTS = [SC.f32(128) for _ in range(4)]
        XT1 = XTS[0]
        XT2 = XTS[1]
        LBb = [SC.bf(4, 2, 128), SC.bf(4, 2, 128)]
        VALL = SC.bf(8, 4, 2, 128)
        SALL = SC.bf(8, 2, NK + 2)
        GB = SC.bf(2, T)
        _xxf = XX.rearrange("p a b c -> p (a b c)")
        YF = _xxf[:, 0:512]
        X2 = _xxf[:, 512:1024]
        MEMSET("pool", ZEROB, 0.0)
        pieces_f = [(0, 64, 0), (64, 256, 256), (320, 256, 1280)]
        pieces_b = [(0, 256, 256), (256, 256, 1280), (512, 64, 0)]
        tok_pieces = [(0, 64, 0, 512), (256, 256, 64, 0), (1280, 256, 320, 256)]
        zi = 0
        oi = 0
        for ct in range(2):
            MEMSET("pool", SALL[:, :, :, 0:1], 0.0)
            MEMSET("pool", SALL[:, :, :, NK + 1:NK + 2], 0.0)
            MM(PSB[3][:, 0:256], IDB, ZEROB, True, False, sgc=True)
            MM(PSB[3][:, 256:512], IDB, ZEROB, False, False, sgc=True)
            MM(PSB[4][:, 0:256], IDB, ZEROB, True, False, sgc=True)
            MM(PSB[4][:, 256:512], IDB, ZEROB, False, False, sgc=True)
            qlist = [(dr_, st_) for dr_ in range(2) for st_ in range(4 * ct, 4 * ct + 4)]
            def ssm_prep(dr, st, q, ql, LB):
                DMA("sp", BRAW, ssm_B[l, q])
                DMA("sp", CRAW, ssm_C[l, q])
                cfr = CFR[:, q:q + 1]
                cfi = CFI[:, q:q + 1]
                ACT(XT1, BRAW[:, 1], AF.Copy, scale=cfi)
                STT("dve", XX[:, 0, 0], BRAW[:, 0], cfr, XT1, ALU.mult, ALU.subtract)
                ACT(XT2, BRAW[:, 0], AF.Copy, scale=cfi)
                STT("dve", XX[:, 0, 1], BRAW[:, 1], cfr, XT2, ALU.mult, ALU.add)
                for tau in range(1, 4):
                    pr = PR[:, tau, q:q + 1]
                    pi_ = PI[:, tau, q:q + 1]
                    xa = XTS[(2 * tau) % 4]
                    xb = XTS[(2 * tau + 1) % 4]
                    ACT(xa, XX[:, 0, 1], AF.Copy, scale=pi_)
                    STT("dve", XX[:, tau, 0], XX[:, 0, 0], pr, xa, ALU.mult, ALU.subtract)
                    ACT(xb, XX[:, 0, 0], AF.Copy, scale=pi_)
                    STT("dve", XX[:, tau, 1], XX[:, 0, 1], pr, xb, ALU.mult, ALU.add)
                lbf = LB.rearrange("p a b c -> p (a b c)")
                for rnd in range(2):
                    for i4 in range(4):
                        i8 = rnd * 4 + i4
                        TR(PSB[2][:, i4 * 128:(i4 + 1) * 128], XX[:, i8 // 2, i8 % 2], IDF)
                    COPY("act", lbf[:, rnd * 512:(rnd + 1) * 512], PSB[2][:, 0:512])
                ACT(NCIM, CRAW[:, 1], AF.Copy, scale=-1.0)
                for r in range(4):
                    e_ = (r + 1) if dr == 0 else (4 - r)
                    pr = PR[:, e_, q:q + 1]
                    pi_ = PI[:, e_, q:q + 1]
                    xa = XTS[(2 * r) % 4]
                    xb = XTS[(2 * r + 1) % 4]
                    ACT(xa, CRAW[:, 1], AF.Copy, scale=pi_)
                    STT("dve", VALL[:, ql, r, 0], CRAW[:, 0], pr, xa, ALU.mult, ALU.subtract)
                    ACT(xb, CRAW[:, 0], AF.Copy, scale=pi_)
                    STT("dve", VALL[:, ql, r, 1], CRAW[:, 1], pr, xb, ALU.mult, ALU.add)
                for tau in range(4):
                    slot = 0 if tau == 0 else (tau if dr == 0 else 3 + tau)
                    bank = PSB[3] if slot < 4 else PSB[4]
                    cs = (slot % 4) * 128
                    lastc = (ql == 7) if (slot == 0 or slot >= 4) else (ql == 3)
                    MM(bank[:, cs:cs + 128], XX[:, tau, 0], CRAW[:, 0], False, False, sgc=True)
                    MM(bank[:, cs:cs + 128], XX[:, tau, 1], NCIM, False, lastc, sgc=True)

            ssm_prep(qlist[0][0], qlist[0][1], qlist[0][0] * 8 + qlist[0][1], qlist[0][0] * 4 + (qlist[0][1] % 4), LBb[0])
            for qi, (dr, st) in enumerate(qlist):
                if True:
                    q = dr * 8 + st
                    ql = dr * 4 + (st % 4)
                    LB = LBb[qi % 2]
                    pcs = (pieces_f if dr == 0 else pieces_b)

                    def zmm(pi3):
                        p0, n, tok0 = pcs[pi3]
                        bank = PSB[pi3 % 2]
                        for ri in range(2):
                            for j in range(4):
                                tau = (3 - j) if dr == 0 else j
                                MM(bank[:, ri * 256:ri * 256 + n], LB[:, tau, ri, :],
                                   UT[:, ct, tok0 + j:tok0 + 4 * n:4], j == 0, j == 3)

                    def zmodul(pi3):
                        p0, n, tok0 = pcs[pi3]
                        bank = PSB[pi3 % 2]
                        if dr == 0:
                            er = ER[:, p0:p0 + n]
                            ei = EI[:, p0:p0 + n]
                        else:
                            er = ER[:, NK - p0 - n:NK - p0][:, ::-1]
                            ei = EI[:, NK - p0 - n:NK - p0][:, ::-1]
                        zv = bank[:, 0:512].rearrange("p (a b) -> p a b", a=2, b=256)[:, :, 0:n]
                        pei = TQP[0][:, :, 0:n]
                        per = TQP[1][:, :, 0:n]
                        TT("dve", pei, zv, ei.unsqueeze(1).broadcast_to([128, 2, n]), ALU.mult)
                        TT("dve", per, zv, er.unsqueeze(1).broadcast_to([128, 2, n]), ALU.mult)
                        TT("dve", MR[:, p0:p0 + n], per[:, 0, :], pei[:, 1, :], ALU.subtract)
                        TT("dve", MI[:, p0:p0 + n], per[:, 1, :], pei[:, 0, :], ALU.add)
                    zmm(0)
                    zmm(1)
                    if qi + 1 < len(qlist):
                        ndr, nst = qlist[qi + 1]
                        ssm_prep(ndr, nst, ndr * 8 + nst, ndr * 4 + (nst % 4), LBb[(qi + 1) % 2])
                    arb = TAR[:, q, :].unsqueeze(2).broadcast_to([128, NT, NT])
                    aib = TAI[:, q, :].unsqueeze(2).broadcast_to([128, NT, NT])
                    brb = TBR[:, q, :].unsqueeze(1).broadcast_to([128, NT, NT])
                    bib = TBI[:, q, :].unsqueeze(1).broadcast_to([128, NT, NT])
                    TT("pool", TM1v, aib, bib, ALU.mult)
                    TT("dve", ERv, arb, brb, ALU.mult)
                    TT("pool", TM2v, aib, brb, ALU.mult)
                    TT("dve", EIv, arb, bib, ALU.mult)
                    TT("dve", ERv, ERv, TM1v, ALU.subtract)
                    TT("dve", EIv, EIv, TM2v, ALU.add)
                    zmodul(0)
                    zmm(2)
                    zmodul(1)
                    zmodul(2)
                    rr = R4[:, q:q + 1].broadcast_to([128, NK])
                    for Mx in (MR, MI):
                        if dr == 0:
                            S.op("dve", (lambda Mx, rr: lambda e: e.tensor_tensor_scan(
                                out=Mx, data0=rr, data1=Mx, initial=0.0, op0=ALU.mult, op1=ALU.add))(Mx, rr),
                                reads=[Mx, R4[:, q:q + 1]], writes=[Mx])
                        else:
                            S.op("dve", (lambda Mx, rr: lambda e: e.tensor_tensor_scan(
                                out=Mx[:, ::-1], data0=rr, data1=Mx[:, ::-1], initial=0.0, op0=ALU.mult, op1=ALU.add))(Mx, rr),
                                reads=[Mx, R4[:, q:q + 1]], writes=[Mx])
                    erf = ER if dr == 0 else ER[:, ::-1]
                    eif = EI if dr == 0 else EI[:, ::-1]
                    t0, t1 = TQ2
                    TT("dve", t0, MR, erf, ALU.mult)
                    TT("dve", t1, MI, eif, ALU.mult)
                    TT("dve", SALL[:, ql, 0, 1:1 + NK], t0, t1, ALU.add)
                    TT("dve", t0, MR, eif, ALU.mult)
                    TT("dve", t1, MI, erf, ALU.mult)
                    TT("dve", SALL[:, ql, 1, 1:1 + NK], t0, t1, ALU.subtract)
                    if ATT_I[0] < NTT:
                        attn_slice(ATT_I[0])
                        ATT_I[0] += 1
            tmf = TMAT.rearrange("p a b -> p (a b)")
            COPY("act", tmf[:, 0:512], PSB[3][:, 0:512])
            COPY("act", tmf[:, 512:896], PSB[4][:, 0:384])
            for r in range(4):
                for (tok0, n, pf0, pb0) in tok_pieces:
                    bank = PSB[oi % 2]
                    oi += 1
                    mms = []
                    for j in range(4):
                        slot = 0 if j == r else ((r - j) if j < r else 3 + (j - r))
                        mms.append((TMAT[:, slot, :], UT[:, ct, tok0 + j:tok0 + 4 * n:4]))
                    for ql in range(8):
                        c0_ = pf0 if ql < 4 else pb0 + 2
                        mms.append((VALL[:, ql, r, 0, :], SALL[:, ql, 0, c0_:c0_ + n]))
                        mms.append((VALL[:, ql, r, 1, :], SALL[:, ql, 1, c0_:c0_ + n]))
                    for i_, (w_, x_) in enumerate(mms):
                        MM(bank[:, 0:n], w_, x_, i_ == 0, i_ == len(mms) - 1)
                    STT("dve", GF[:, ct, tok0 + r:tok0 + 4 * n:4], UT[:, ct, tok0 + r:tok0 + 4 * n:4],
                        SDG[:, ct, 0:1], bank[:, 0:n], ALU.mult, ALU.add)
            for ci, (c0, w, kind) in enumerate(CH):
                yf = GF[:, ct, c0:c0 + w]
                x2 = X2[:, 0:w]
                ACT(x2, yf, AF.Square)
                TS("dve", x2, x2, 0.044715, ALU.mult, 1.0, ALU.add)
                TT("dve", x2, x2, yf, ALU.mult)
                ACT(x2, x2, AF.Sigmoid, scale=1.5957691216057308)
                TT("dve", yf, yf, x2, ALU.mult)
                COPY("pool", GB[:, ct, c0:c0 + w], yf)
        for ot in range(2):
            for (c0, w, kind) in CH:
                ps = PSB[(ot % 2)]
                for ct in range(2):
                    MM(ps[:, 0:w], GLUW[:, ct, ot * 128:(ot + 1) * 128], GB[:, ct, c0:c0 + w], ct == 0, ct == 1)
                sg = YF[:, 0:w]
                ACT(sg, ps[:, 0:w], AF.Sigmoid, bias=SDG[:, ot, 1:2])
                TT("dve", MIX[:, 2 + ot, c0:c0 + w], GF[:, ot, c0:c0 + w], sg, ALU.mult)
        if dbg and l == 0:
            dbg_dump("mix_ps", MIX[:, 0:4], [128, 4, T])
        if stop == "ssm":
            return

        while ATT_I[0] < NTT:
            attn_slice(ATT_I[0])
            ATT_I[0] += 1
        if dbg and l == 0:
            dbg_dump("mix", MIX, [128, 8, T])
        if stop == "attn":
            return

        PP.reset()
        SC.cur = att_end
        H2 = PP.bf(8, T)
        WO = SC.bf(8, D)
        XCb = [SC.f32(8, 512), SC.f32(8, 512)]
        SQ = SC.bf(8, 512)
        RS = SC.f32(512)
        HN = [SC.f32(512), SC.f32(512)]
        DMA("pool", WO, w_out[l].rearrange("(kt p) c -> p kt c", p=128))
        DMA("sp", XCb[0][:, :, 0:CH[0][1]], xsv[:, :, CH[0][0]:CH[0][0] + CH[0][1]])
        nxt_mod = (l + 1 < n_layers)
        if nxt_mod:
            mbn = mod_alloc()
            mod_load(l + 1, 0, mbn)
        mod_sched = [3, 3, 2, 2, 2]
        mod_cc = 0
        for ci, (c0, w, kind) in enumerate(CH):
            XC = XCb[ci % 2]
            if ci + 1 < len(CH):
                n0, nw, _ = CH[ci + 1]
                DMA("sp", XCb[(ci + 1) % 2][:, :, 0:nw], xsv[:, :, n0:n0 + nw])
            for j in range(8):
                ps = nextps()
                for k in range(8):
                    MM(ps[:, 0:w], WO[:, k, j * 128:(j + 1) * 128], MIX[:, k, c0:c0 + w], k == 0, k == 7)
                STT("dve", XC[:, j, 0:w], ps[:, 0:w], modv(l, 2, j, kind), XC[:, j, 0:w], ALU.mult, ALU.add)
            DMA("sp", xdv[:, :, c0:c0 + w], XC[:, :, 0:w])
            if nxt_mod:
                for _ in range(mod_sched[ci]):
                    mod_chunk(l + 1, mod_cc, mbn)
                    mod_cc += 1
            rmsnorm_chunk(XC, w, GE[:, l, 1], 3, l, kind, H2[:, :, c0:c0 + w], SQ, RS, HN)
        if dbg and l == 0:
            dbg_dump("h2", H2, [128, 8, T])
        if stop == "wout":
            return

        SC.reset()
        ACTF = SC.bf(NF, T)
        WGb = [SC.bf(8, 128), SC.bf(8, 128)]
        WUb = [SC.bf(8, 128), SC.bf(8, 128)]
        SGb = [SC.f32(512), SC.f32(512)]
        X2b = [SC.f32(T), SC.f32(T)]
        WDb = [PP.bf(NF, 128), PP.bf(NF, 128)]
        wgv = w_gate[l].rearrange("(kt p) c -> p kt c", p=128)
        wuv = w_up[l].rearrange("(kt p) c -> p kt c", p=128)
        wdv = w_down[l].rearrange("(f p) c -> p f c", p=128)
        chs = CH[1:] if last else CH
        DMA("pool", WGb[0], wgv[:, :, 0:128])
        DMA("pool", WUb[0], wuv[:, :, 0:128])
        k2 = 0
        for f in range(NF):
            if f + 1 < NF:
                DMA("pool", WGb[(f + 1) % 2], wgv[:, :, (f + 1) * 128:(f + 2) * 128])
                DMA("pool", WUb[(f + 1) % 2], wuv[:, :, (f + 1) * 128:(f + 2) * 128])
            WG, WU = WGb[f % 2], WUb[f % 2]
            for (c0, w, kind) in chs:
                pg = PSB[(2 * k2) % 8]
                pu = PSB[(2 * k2 + 1) % 8]
                sg = SGb[k2 % 2]
                k2 += 1
                for k in range(8):
                    MM(pg[:, 0:w], WG[:, k, :], H2[:, k, c0:c0 + w], k == 0, k == 7)
                for k in range(8):
                    MM(pu[:, 0:w], WU[:, k, :], H2[:, k, c0:c0 + w], k == 0, k == 7)
                ACT(sg[:, 0:w], pg[:, 0:w], AF.Silu)
                TT("dve", ACTF[:, f, c0:c0 + w], sg[:, 0:w], pu[:, 0:w], ALU.mult)
        DMA("pool", WDb[0], wdv[:, :, 0:128])
        DMA("sp", X2b[0], Xd[0:128, :])
        k2 = 0
        for j in range(8):
            if j + 1 < 8:
                DMA("pool", WDb[(j + 1) % 2], wdv[:, :, (j + 1) * 128:(j + 2) * 128])
                DMA("sp", X2b[(j + 1) % 2], Xd[(j + 1) * 128:(j + 2) * 128, :])
            WD = WDb[j % 2]
            x2 = X2b[j % 2]
            for (c0, w, kind) in chs:
                ps = PSB[k2 % 8]
                k2 += 1
                for f in range(NF):
                    MM(ps[:, 0:w], WD[:, f, :], ACTF[:, f, c0:c0 + w], f == 0, f == NF - 1)
                STT("dve", x2[:, c0:c0 + w], ps[:, 0:w], modv(l, 5, j, kind), x2[:, c0:c0 + w], ALU.mult, ALU.add)
            if last:
                DMA("sp", outT[j * 128:(j + 1) * 128, :], x2[:, LC:T])
            else:
                DMA("sp", Xd[j * 128:(j + 1) * 128, :], x2)

    for l in range(n_layers):
        if stop in ('pro', 'consts'):
            break
        layer_body(l)
    S.emit()
    return nc, dbg_out


def _host_consts():
    f32 = np.float32
    ident = np.eye(128, dtype=f32)
    pswap = np.zeros((128, 128), f32)
    for m in range(128):
        hb, d = divmod(m, 64)
        pswap[hb * 64 + _pi_perm(d), m] = 1.0
    blk64 = np.zeros((128, 128), f32)
    blk64[:64, :64] = 1.0 / 64
    blk64[64:, 64:] = 1.0 / 64
    t = np.arange(SEQ)
    row = (t // 64).astype(f32)
    col = (t % 64).astype(f32)
    inv = np.power(f32(10000.0), -np.arange(16, dtype=f32) / f32(16)).astype(f32)
    rope_c = np.zeros((128, SEQ), f32)
    rope_s = np.zeros((128, SEQ), f32)
    for p in range(128):
        d = p % 64
        fidx = d % 16
        pos = row if d < 32 else col
        ang = (pos * inv[fidx]).astype(f32)
        rope_c[p] = np.cos(ang)
        s = np.sin(ang)
        rope_s[p] = -s if (d % 32) < 16 else s
    ik = np.arange(128)[:, None]
    iq = np.arange(128)[None, :]
    mL = np.where(ik < iq, NEG, 0.0).astype(f32)
    mR = np.where(ik > iq, NEG, 0.0).astype(f32)
    maskd = np.zeros((128, 2, 256), f32)
    maskd[:, 0, :128] = mL
    maskd[:, 0, 128:] = mL
    maskd[:, 1, :128] = mR
    maskd[:, 1, 128:] = mR
    TT_ = 512
    band = np.zeros((128, 5, 2, 256), f32)
    for gi, wdw in enumerate((2, 4, 8, 16)):
        tt = np.arange(TT_)
        lo = np.clip(tt - wdw // 2, 0, TT_)
        hi = np.clip(tt + wdw // 2, 0, TT_)
        M = np.zeros((TT_, TT_), np.float64)
        for i in range(TT_):
            M[i, lo[i]:hi[i]] = 1.0 / (hi[i] - lo[i])
            M[i, i] -= 1.0
        MT = M.T
        gp, gl = divmod(gi, 2)
        sl = slice(gl * 128, (gl + 1) * 128)
        band[:, 0, gp, sl] = MT[0:128, 0:128]
        band[:, 1, gp, sl] = MT[128:256, 128:256]
        band[:, 2, gp, sl] = MT[384:512, 384:512]
        band[:, 3, gp, sl] = MT[128:256, 256:384]
        band[:, 4, gp, sl] = MT[256:384, 128:256]
    return dict(ident=ident, pswap=pswap, blk64=blk64, rope_c=rope_c, rope_s=rope_s, maskd=maskd, band=band)


def _host_layout(inp):
    f32 = np.float32
    g = lambda k: np.asarray(inp[k], dtype=f32)
    sh = {}
    sh["w_mod"] = np.ascontiguousarray(g("w_mod"))
    sh["b_modp"] = np.ascontiguousarray(g("b_mod").reshape(NL, 6, 8, 128).transpose(0, 3, 1, 2).reshape(NL, 128, 48))
    nm = g("norm_mix").reshape(NL, 8, 128).transpose(0, 2, 1)
    nf = g("norm_ffn").reshape(NL, 8, 128).transpose(0, 2, 1)
    sh["normp"] = np.ascontiguousarray(np.stack([nm, nf], axis=2))
    wi = g("w_in")
    k0 = wi[:, :, 1024:1088]
    k1 = wi[:, :, 1088:1152]
    sh["w_inp"] = np.ascontiguousarray(np.concatenate(
        [wi[:, :, 0:256], wi[:, :, 1152:1280], wi[:, :, 256:512], wi[:, :, 512:1024], k0, k0, k1, k1], axis=2))
    sh["w_out"] = np.ascontiguousarray(g("w_out"))
    pw = g("pool_w")
    pbd = np.zeros((NL, 128, 2, 128), f32)
    for gp in range(2):
        for gl in range(2):
            pbd[:, gl * 64:(gl + 1) * 64, gp, gl * 64:(gl + 1) * 64] = pw[:, 2 * gp + gl]
    sh["poolwbd"] = pbd
    sh["poolsc"] = np.ascontiguousarray(g("pool_scale").reshape(NL, 2, 128).transpose(0, 2, 1))
    def pq(a):
        a = a.reshape(NL, 2, 8, 2, 64)
        return a.transpose(0, 3, 4, 1, 2).reshape(NL, 128, 16)
    are = pq(g("ssm_a_re"))
    aim = pq(g("ssm_a_im"))
    ldt = pq(np.repeat(g("ssm_log_dt")[:, :, :, None], 64, axis=3))
    sh["ssm_small"] = np.ascontiguousarray(np.stack([are, aim, ldt], axis=2))
    bre, bim = g("ssm_b_re"), g("ssm_b_im")
    cre, cim = g("ssm_c_re"), g("ssm_c_im")
    sB = np.zeros((NL, 16, 128, 2, 128), f32)
    sC = np.zeros((NL, 16, 128, 2, 128), f32)
    for dr in range(2):
        for st in range(8):
            q = dr * 8 + st
            for gg in range(2):
                gi = 2 * st + gg
                c0 = 32 * (st % 4) + 16 * gg
                sB[:, q, gg * 64:(gg + 1) * 64, 0, c0:c0 + 16] = bre[:, dr, gi]
                sB[:, q, gg * 64:(gg + 1) * 64, 1, c0:c0 + 16] = bim[:, dr, gi]
                sC[:, q, gg * 64:(gg + 1) * 64, 0, c0:c0 + 16] = cre[:, dr, gi].transpose(0, 2, 1)
                sC[:, q, gg * 64:(gg + 1) * 64, 1, c0:c0 + 16] = cim[:, dr, gi].transpose(0, 2, 1)
    sh["ssm_B"] = sB
    sh["ssm_C"] = sC
    dsk = g("ssm_d").reshape(NL, 2, 128).transpose(0, 2, 1)
    glb = g("ssm_glu_b").reshape(NL, 2, 128).transpose(0, 2, 1)
    sh["ssm_dg"] = np.ascontiguousarray(np.stack([dsk, glb], axis=3))
    sh["glu_w"] = np.ascontiguousarray(g("ssm_glu_w"))
    qn, kn = g("q_norm"), g("k_norm")
    idx = np.arange(128) % 64
    pidx = np.array([_pi_perm(d) for d in idx])
    sh["qkn"] = np.ascontiguousarray(np.stack([qn[:, idx], qn[:, pidx], kn[:, idx], kn[:, pidx]], axis=2))
    sh["sinkr"] = np.ascontiguousarray(np.repeat(g("attn_sink")[:, None, :], 128, axis=1))
    sh["w_gate"] = np.ascontiguousarray(g("ffn_w_gate"))
    sh["w_up"] = np.ascontiguousarray(g("ffn_w_up"))
    sh["w_down"] = np.ascontiguousarray(g("ffn_w_down"))
    sh.update(_host_consts())
    x, c, ctx, c_ctx = g("x"), g("c"), g("ctx"), g("c_ctx")
    maps = []
    for b in range(NB):
        m = dict(sh)
        m["xT"] = np.ascontiguousarray(np.concatenate([ctx[b].T, x[b].T], axis=1))
        cv = np.stack([c[b].reshape(8, 128).T, c_ctx.reshape(8, 128).T], axis=2)
        m["cvec"] = np.ascontiguousarray(cv)
        maps.append(m)
    return maps


_PROG = {}


def kernel(**inputs):
    maps = _host_layout(inputs)
    if "nc" not in _PROG:
        _PROG["nc"] = build_program()[0]
    nc = _PROG["nc"]
    res = run_bass_kernel_spmd(nc, maps, core_ids=list(range(NB)))
    out = np.stack([np.ascontiguousarray(r["outT"].T) for r in res.results], axis=0)
    return out.astype(np.float32)
```
